# Optimizing a Trainium2 kernel written in Bass

```python
import jax, jax.numpy as jnp
from jax import lax
import numpy as np

D_MODEL = 1024
BATCH = 32
SEQ = 256
DEPTH = 4
DEC_BATCH = 4
DEC_SEQ = 1024
PAST_LEN = 256

GRID_W = 64
N_MIXERS = 3
N_NA_LAYERS = (DEPTH + 2) // 3
N_RET_LAYERS = (DEPTH + 1) // 3
N_MLP_LAYERS = DEPTH // 3
EPS = 1e-6
NEG_INF = -1e30

NA_HEADS = 16
NA_HEAD_DIM = D_MODEL // NA_HEADS
NA_WIDTH = NA_HEADS * NA_HEAD_DIM
NA_KH = 8
NA_KW = 16
ATTN_Q_BLOCK = 128

RET_HEADS = 4
RET_QK_DIM = D_MODEL // RET_HEADS
RET_V_DIM = 2 * D_MODEL // RET_HEADS
RET_QK_WIDTH = RET_HEADS * RET_QK_DIM
RET_WIDTH = RET_HEADS * RET_V_DIM
RET_CHUNK = 128
ROPE_BASE = 10000.0

MLP_WIDTH = 2 * D_MODEL
MLP_GROUPS = 8
MLP_GROUP_DIM = MLP_WIDTH // MLP_GROUPS
MLP_CHUNK = 128

kernel_name = "hybrid_na_retention_gmlp_diffusion_step"


def rms_norm(x, w):
    xf = x.astype(jnp.float32)
    y = xf * lax.rsqrt(jnp.mean(xf * xf, axis=-1, keepdims=True) + EPS)
    return (y * w.astype(jnp.float32)).astype(x.dtype)


def layer_norm(x, w, b):
    xf = x.astype(jnp.float32)
    mu = jnp.mean(xf, axis=-1, keepdims=True)
    var = jnp.mean(jnp.square(xf - mu), axis=-1, keepdims=True)
    y = (xf - mu) * lax.rsqrt(var + EPS)
    return (y * w.astype(jnp.float32) + b.astype(jnp.float32)).astype(x.dtype)


def adaln(cvec, w, b):
    m = jax.nn.silu(cvec) @ w + b
    return jnp.split(m, 3, axis=-1)


def axial_rope(x):
    N, Dh = x.shape[1], x.shape[-1]
    half = Dh // 2
    t = jnp.arange(N)
    row = (t // GRID_W).astype(jnp.float32)
    col = (t % GRID_W).astype(jnp.float32)

    def rot(xa, pos):
        d = xa.shape[-1]
        inv = ROPE_BASE ** (-jnp.arange(0, d, 2, dtype=jnp.float32) / d)
        ang = pos[:, None] * inv[None, :]
        cos = jnp.cos(ang)[None, :, None, :]
        sin = jnp.sin(ang)[None, :, None, :]
        x1, x2 = xa[..., : d // 2], xa[..., d // 2:]
        return jnp.concatenate([x1 * cos - x2 * sin, x1 * sin + x2 * cos], axis=-1).astype(xa.dtype)

    return jnp.concatenate([rot(x[..., :half], row), rot(x[..., half:], col)], axis=-1)


def na_project(h, w_in, q_gain, k_gain):
    B, N, _ = h.shape
    q, k, v, g = jnp.split(h @ w_in, 4, axis=-1)
    q = rms_norm(q.reshape(B, N, NA_HEADS, NA_HEAD_DIM), q_gain)
    k = rms_norm(k.reshape(B, N, NA_HEADS, NA_HEAD_DIM), k_gain)
    v = v.reshape(B, N, NA_HEADS, NA_HEAD_DIM)
    return q, k, v, g


def dense_attention(q, k, v):
    B, S, H, Dh = q.shape
    scale = Dh ** -0.5
    qb = q.reshape(B, S // ATTN_Q_BLOCK, ATTN_Q_BLOCK, H, Dh).transpose(1, 0, 2, 3, 4)

    def blk(qi):
        s = jnp.einsum('bqhd,bkhd->bhqk', qi, k).astype(jnp.float32) * scale
        p = jax.nn.softmax(s, axis=-1).astype(v.dtype)
        return jnp.einsum('bhqk,bkhd->bqhd', p, v)

    o = lax.map(blk, qb)
    return o.transpose(1, 0, 2, 3, 4).reshape(B, S, H, Dh)


def na_context(h, w_in, w_out, q_gain, k_gain):
    B, S, _ = h.shape
    q, k, v, g = na_project(h, w_in, q_gain, k_gain)
    o = dense_attention(q, k, v).reshape(B, S, NA_WIDTH) * jax.nn.silu(g)
    return o @ w_out, k, v


def na_latent(h, k_ctx, v_ctx, w_in, w_out, q_gain, k_gain, rpb):
    B, N, _ = h.shape
    R = N // GRID_W
    kh = min(NA_KH, R)
    scale = NA_HEAD_DIM ** -0.5
    q, k, v, g = na_project(h, w_in, q_gain, k_gain)
    qg = q.reshape(B, R, GRID_W, NA_HEADS, NA_HEAD_DIM)
    kg = k.reshape(B, R, GRID_W, NA_HEADS, NA_HEAD_DIM)
    vg = v.reshape(B, R, GRID_W, NA_HEADS, NA_HEAD_DIM)

    r = np.arange(R)
    rstart = np.clip(r - kh // 2, 0, R - kh)
    row_idx = rstart[:, None] + np.arange(kh)[None, :]
    cq = np.arange(GRID_W)
    cstart = np.clip(cq - NA_KW // 2, 0, GRID_W - NA_KW)
    ck = np.arange(GRID_W)
    col_in = (ck[None, :] >= cstart[:, None]) & (ck[None, :] < cstart[:, None] + NA_KW)

    k_rows = kg[:, row_idx]
    v_rows = vg[:, row_idx]
    s_loc = jnp.einsum('brqhd,brjkhd->brhqjk', qg, k_rows).astype(jnp.float32) * scale

    roff = row_idx - r[:, None] + (NA_KH - 1)
    coff = np.clip(ck[None, :] - cq[:, None], -(NA_KW - 1), NA_KW - 1) + (NA_KW - 1)
    bias = rpb[:, roff[:, None, :, None], coff[None, :, None, :]]
    bias = bias.transpose(1, 0, 2, 3, 4).astype(jnp.float32)
    s_loc = jnp.where(col_in[:, None, :], s_loc + bias[None], NEG_INF)
    s_loc = s_loc.reshape(B, R, NA_HEADS, GRID_W, kh * GRID_W)

    s_ctx = jnp.einsum('brqhd,bkhd->brhqk', qg, k_ctx).astype(jnp.float32) * scale
    p = jax.nn.softmax(jnp.concatenate([s_loc, s_ctx], axis=-1), axis=-1).astype(v.dtype)
    p_loc = p[..., : kh * GRID_W].reshape(B, R, NA_HEADS, GRID_W, kh, GRID_W)
    p_ctx = p[..., kh * GRID_W:]
    o = (jnp.einsum('brhqjk,brjkhd->brqhd', p_loc, v_rows)
         + jnp.einsum('brhqk,bkhd->brqhd', p_ctx, v_ctx))
    o = o.reshape(B, N, NA_WIDTH) * jax.nn.silu(g)
    return o @ w_out


def retention_scan(q, k, v, log_gamma, s0):
    B, N, H, Dk = q.shape
    Dv = v.shape[-1]
    C = RET_CHUNK
    nc = N // C
    lg = log_gamma.astype(jnp.float32)
    pos = jnp.arange(C, dtype=jnp.float32)
    decay_q = jnp.exp(lg[None, :] * (pos[:, None] + 1.0))[None, :, :, None]
    decay_k = jnp.exp(lg[None, :] * (C - 1.0 - pos[:, None]))[None, :, :, None]
    diff = pos[:, None] - pos[None, :]
    dmat = jnp.where(diff[None] >= 0, jnp.exp(lg[:, None, None] * jnp.maximum(diff, 0.0)[None]), 0.0)
    chunk_decay = jnp.exp(lg * C)[None, :, None, None]
    kscale = Dk ** -0.5

    def to_chunks(a):
        return a.reshape(B, nc, C, H, a.shape[-1]).transpose(1, 0, 2, 3, 4)

    def step(s, xs):
        qc, kc, vc = xs
        qf = qc.astype(jnp.float32)
        kf = kc.astype(jnp.float32) * kscale
        vf = vc.astype(jnp.float32)
        inner = jnp.einsum('bihd,bjhd->bhij', qf, kf) * dmat[None]
        o = (jnp.einsum('bhij,bjhv->bihv', inner, vf)
             + jnp.einsum('bihd,bhdv->bihv', qf, s) * decay_q)
        s = s * chunk_decay + jnp.einsum('bjhd,bjhv->bhdv', kf * decay_k, vf)
        return s, o

    s_final, o = lax.scan(step, s0.astype(jnp.float32), (to_chunks(q), to_chunks(k), to_chunks(v)))
    o = o.transpose(1, 0, 2, 3, 4).reshape(B, N, H, Dv)
    return o, s_final


def ret_project(h, w_in):
    B, N, _ = h.shape
    q, k, v, g = jnp.split(h @ w_in, [RET_QK_WIDTH, 2 * RET_QK_WIDTH, 2 * RET_QK_WIDTH + RET_WIDTH], axis=-1)
    q = q.reshape(B, N, RET_HEADS, RET_QK_DIM)
    k = k.reshape(B, N, RET_HEADS, RET_QK_DIM)
    v = v.reshape(B, N, RET_HEADS, RET_V_DIM)
    return q, k, v, g


def ret_bidirectional(q, k, v, decay_logit, s0_f, s0_b):
    lg = jax.nn.log_sigmoid(decay_logit.astype(jnp.float32))
    o_f, s_f = retention_scan(q, k, v, lg[0], s0_f)
    o_b, s_b = retention_scan(jnp.flip(q, 1), jnp.flip(k, 1), jnp.flip(v, 1), lg[1], s0_b)
    return o_f + jnp.flip(o_b, 1), s_f, s_b


def ret_output(o, g, gn_w, w_out, dtype):
    B, N = o.shape[:2]
    mu = jnp.mean(o, axis=-1, keepdims=True)
    var = jnp.mean(jnp.square(o - mu), axis=-1, keepdims=True)
    y = ((o - mu) * lax.rsqrt(var + EPS)).reshape(B, N, RET_WIDTH) * gn_w.astype(jnp.float32)
    y = y.astype(dtype) * jax.nn.silu(g)
    return y @ w_out


def ret_context(h, w_in, w_out, decay_logit, gn_w):
    B = h.shape[0]
    q, k, v, g = ret_project(h, w_in)
    s0 = jnp.zeros((B, RET_HEADS, RET_QK_DIM, RET_V_DIM), jnp.float32)
    o, s_f, s_b = ret_bidirectional(q, k, v, decay_logit, s0, s0)
    state = jnp.stack([s_f, s_b], axis=1).astype(h.dtype)
    return ret_output(o, g, gn_w, w_out, h.dtype), state


def ret_latent(h, state, w_in, w_out, decay_logit, gn_w):
    q, k, v, g = ret_project(h, w_in)
    q = axial_rope(q)
    k = axial_rope(k)
    o, _, _ = ret_bidirectional(q, k, v, decay_logit, state[:, 0], state[:, 1])
    return ret_output(o, g, gn_w, w_out, h.dtype)


def chunk_mlp(h, w_in, ln_w, ln_b, w_s, b_s, w_out):
    B, N, _ = h.shape
    nc = N // MLP_CHUNK
    u, v, g = jnp.split(h @ w_in, 3, axis=-1)
    u = jax.nn.gelu(u)
    v = layer_norm(jax.nn.gelu(v), ln_w, ln_b)
    vg = v.reshape(B, nc, MLP_CHUNK, MLP_GROUPS, MLP_GROUP_DIM)
    sv = jnp.einsum('gij,bnjgd->bnigd', w_s, vg) + b_s.T[None, None, :, :, None]
    o = u * sv.reshape(B, N, MLP_WIDTH) * jax.nn.silu(g)
    return o @ w_out


def modulate(h, shift, scale):
    return h * (1.0 + scale) + shift


def setup_inputs(seed: int = 0) -> dict:
    key = jax.random.key(seed)
    ks = jax.random.split(key, 26)
    f32 = jnp.float32

    def nrm(k, shape, s):
        return jax.random.normal(k, shape, f32) * s

    dec_base = jnp.log(2.0 ** (5.0 + jnp.arange(RET_HEADS, dtype=f32)) - 1.0)
    return {
        "x_prompt": nrm(ks[0], (BATCH, SEQ, D_MODEL), 1.0),
        "x_sample": nrm(ks[1], (DEC_BATCH, DEC_SEQ, D_MODEL), 1.0),
        "cache_na_k": nrm(ks[2], (DEC_BATCH, N_NA_LAYERS, PAST_LEN, NA_HEADS, NA_HEAD_DIM), 1.0),
        "cache_na_v": nrm(ks[3], (DEC_BATCH, N_NA_LAYERS, PAST_LEN, NA_HEADS, NA_HEAD_DIM), 1.0),
        "state_ret": nrm(ks[4], (DEC_BATCH, N_RET_LAYERS, 2, RET_HEADS, RET_QK_DIM, RET_V_DIM), 0.5),
        "c": nrm(ks[5], (DEC_BATCH, D_MODEL), 1.0),
        "c_ctx": nrm(ks[6], (D_MODEL,), 1.0),
        "norm_w": 1.0 + nrm(ks[7], (DEPTH, D_MODEL), 0.02),
        "w_ada": nrm(ks[8], (DEPTH, D_MODEL, 3 * D_MODEL), 0.5 * D_MODEL ** -0.5),
        "b_ada": nrm(ks[9], (DEPTH, 3 * D_MODEL), 0.01),
        "na_w_in": nrm(ks[10], (N_NA_LAYERS, D_MODEL, 4 * NA_WIDTH), D_MODEL ** -0.5),
        "na_w_out": nrm(ks[11], (N_NA_LAYERS, NA_WIDTH, D_MODEL), NA_WIDTH ** -0.5),
        "na_q_gain": 1.0 + nrm(ks[12], (N_NA_LAYERS, NA_HEAD_DIM), 0.02),
        "na_k_gain": 1.0 + nrm(ks[13], (N_NA_LAYERS, NA_HEAD_DIM), 0.02),
        "na_rpb": nrm(ks[14], (N_NA_LAYERS, NA_HEADS, 2 * NA_KH - 1, 2 * NA_KW - 1), 0.1),
        "ret_w_in": nrm(ks[15], (N_RET_LAYERS, D_MODEL, 2 * RET_QK_WIDTH + 2 * RET_WIDTH), D_MODEL ** -0.5),
        "ret_w_out": nrm(ks[16], (N_RET_LAYERS, RET_WIDTH, D_MODEL), RET_WIDTH ** -0.5),
        "ret_decay_logit": dec_base[None, None, :] + nrm(ks[17], (N_RET_LAYERS, 2, RET_HEADS), 0.1),
        "ret_gn_w": 1.0 + nrm(ks[18], (N_RET_LAYERS, RET_WIDTH), 0.02),
        "mlp_w_in": nrm(ks[19], (N_MLP_LAYERS, D_MODEL, 3 * MLP_WIDTH), D_MODEL ** -0.5),
        "mlp_ln_w": 1.0 + nrm(ks[20], (N_MLP_LAYERS, MLP_WIDTH), 0.02),
        "mlp_ln_b": nrm(ks[21], (N_MLP_LAYERS, MLP_WIDTH), 0.01),
        "mlp_w_s": nrm(ks[22], (N_MLP_LAYERS, MLP_GROUPS, MLP_CHUNK, MLP_CHUNK), MLP_CHUNK ** -0.5),
        "mlp_b_s": 1.0 + nrm(ks[23], (N_MLP_LAYERS, MLP_GROUPS, MLP_CHUNK), 0.02),
        "mlp_w_out": nrm(ks[24], (N_MLP_LAYERS, MLP_WIDTH, D_MODEL), MLP_WIDTH ** -0.5),
    }


def reference(x_prompt, x_sample, cache_na_k, cache_na_v, state_ret, c, c_ctx, norm_w, w_ada, b_ada,
              na_w_in, na_w_out, na_q_gain, na_k_gain, na_rpb,
              ret_w_in, ret_w_out, ret_decay_logit, ret_gn_w,
              mlp_w_in, mlp_ln_w, mlp_ln_b, mlp_w_s, mlp_b_s, mlp_w_out):
    yp = x_prompt
    ys = x_sample
    new_k, new_v, new_s = [], [], []
    for i in range(DEPTH):
        kind = i % N_MIXERS
        j = i // N_MIXERS
        sh_p, sc_p, g_p = adaln(c_ctx, w_ada[i], b_ada[i])
        sh_s, sc_s, g_s = adaln(c, w_ada[i], b_ada[i])
        sh_s, sc_s, g_s = sh_s[:, None, :], sc_s[:, None, :], g_s[:, None, :]
        hp = modulate(rms_norm(yp, norm_w[i]), sh_p, sc_p)
        hs = modulate(rms_norm(ys, norm_w[i]), sh_s, sc_s)
        if kind == 0:
            op, kc, vc = na_context(hp, na_w_in[j], na_w_out[j], na_q_gain[j], na_k_gain[j])
            os_ = na_latent(hs, cache_na_k[:, j], cache_na_v[:, j], na_w_in[j], na_w_out[j],
                            na_q_gain[j], na_k_gain[j], na_rpb[j])
            new_k.append(kc)
            new_v.append(vc)
        elif kind == 1:
            op, st = ret_context(hp, ret_w_in[j], ret_w_out[j], ret_decay_logit[j], ret_gn_w[j])
            os_ = ret_latent(hs, state_ret[:, j], ret_w_in[j], ret_w_out[j], ret_decay_logit[j], ret_gn_w[j])
            new_s.append(st)
        else:
            op = chunk_mlp(hp, mlp_w_in[j], mlp_ln_w[j], mlp_ln_b[j], mlp_w_s[j], mlp_b_s[j], mlp_w_out[j])
            os_ = chunk_mlp(hs, mlp_w_in[j], mlp_ln_w[j], mlp_ln_b[j], mlp_w_s[j], mlp_b_s[j], mlp_w_out[j])
        yp = yp + g_p * op
        ys = ys + g_s * os_
    new_na_k = jnp.stack(new_k, axis=1)
    new_na_v = jnp.stack(new_v, axis=1)
    new_ret = jnp.stack(new_s, axis=1)
    return (yp, ys, new_na_k, new_na_v, new_ret)
```

```python
import contextlib
import numpy as np
import ml_dtypes
import concourse.bass as bass
import concourse.mybir as mybir
from concourse.bass_utils import run_bass_kernel_spmd

F32 = mybir.dt.float32
BF16 = mybir.dt.bfloat16
AF = mybir.ActivationFunctionType
ALU = mybir.AluOpType
AX = mybir.AxisListType

D = 1024
NT = 12
EPS = 1e-6
COMPUTE = ("pe", "dve", "act", "pool")
NSEM_POOL = 90


import types


def freeze(fn):
    if fn is None or fn.__closure__ is None:
        return fn
    cells = []
    for c in fn.__closure__:
        try:
            cells.append(types.CellType(c.cell_contents))
        except ValueError:
            cells.append(c)
    return types.FunctionType(fn.__code__, fn.__globals__, fn.__name__, fn.__defaults__, tuple(cells))


class Buf:
    __slots__ = ("name", "w", "rs", "dsem", "dcnt", "excl")

    def __init__(self, name):
        self.name = name
        self.excl = False
        self.w = None
        self.rs = []
        self.dsem = None
        self.dcnt = 0


class Prog:
    def __init__(self, nc, st):
        self.nc = nc
        self.ops = {e: [] for e in ("pe", "dve", "act", "pool", "sp")}
        self.seq = {e: 0 for e in self.ops}
        self.waited = {e: {} for e in self.ops}
        self.sems = {}
        for e in COMPUTE:
            self.sems["E_" + e] = st.enter_context(nc.semaphore("E_" + e))
        self.free_keys = {"pool": [], "sp": []}
        self.sem_cnt = {}
        for i in range(NSEM_POOL):
            self.sems["D%d" % i] = st.enter_context(nc.semaphore("D%d" % i))
            self.free_keys["pool" if i < 28 else "sp"].append("D%d" % i)
            self.sem_cnt["D%d" % i] = 0
        self.dma_bufs = []

    def buf(self, name):
        return Buf(name)

    def _need(self, eng, tok, waits):
        if tok is None:
            return
        sk, val = tok
        if self.waited[eng].get(sk, 0) >= val:
            return
        self.waited[eng][sk] = val
        waits.append((sk, val))

    def _deps(self, eng, reads, writes):
        waits = []
        for b in reads:
            self._need(eng, b.w, waits)
            if b.excl:
                for t in b.rs:
                    if t[0] != "E_" + eng:
                        self._need(eng, t, waits)
        for b in writes:
            self._need(eng, b.w, waits)
            for t in b.rs:
                self._need(eng, t, waits)
        return waits

    def _record(self, tok, reads, writes):
        for b in reads:
            b.rs.append(tok)
            if len(b.rs) > 64:
                best = {}
                for sk, v in b.rs:
                    if best.get(sk, 0) < v:
                        best[sk] = v
                b.rs = list(best.items())
        for b in writes:
            b.w = tok
            b.rs = []

    def op(self, eng, fn, reads=(), writes=()):
        reads = [b for b in reads if b is not None]
        writes = [b for b in writes if b is not None]
        waits = self._deps(eng, reads, writes)
        sk = "E_" + eng
        if eng == "pe":
            waits = [w for w in waits if w[0] != sk]
        self.seq[eng] += 1
        tok = (sk, self.seq[eng])
        self.ops[eng].append((waits, freeze(fn), (sk, 1)))
        self._record(tok, reads, writes)
        return tok

    def dma(self, queue, fn, reads=(), writes=(), sem_buf=None):
        reads = [b for b in reads if b is not None]
        writes = [b for b in writes if b is not None]
        waits = self._deps(queue, reads, writes)
        sb = sem_buf or (writes[0] if writes else reads[0])
        if sb.dsem is None:
            sb.dsem = {}
        if queue not in sb.dsem:
            sb.dsem[queue] = self.free_keys[queue].pop()
            self.dma_bufs.append((sb, queue))
        key = sb.dsem[queue]
        self.sem_cnt[key] += 16
        tok = (key, self.sem_cnt[key])
        self.ops[queue].append((waits, freeze(fn), (key, 16)))
        self._record(tok, reads, writes)
        return tok

    def barrier(self):
        toks = [("E_" + e, self.seq[e]) for e in COMPUTE if self.seq[e] > 0]
        toks += [(k, v) for k, v in self.sem_cnt.items() if v > 0]
        for eng in self.ops:
            waits = []
            for t in toks:
                if t[0] == "E_" + eng:
                    continue
                self._need(eng, t, waits)
            if waits:
                self.ops[eng].append((waits, None, None))
        for b, q in self.dma_bufs:
            self.free_keys[q].append(b.dsem.pop(q))
        self.dma_bufs = []

    def emit_block(self):
        nc = self.nc
        sems = self.sems
        with nc.Block() as block:
            def run(engname):
                lst = self.ops[engname]

                def body(e):
                    for waits, fn, inc in lst:
                        for sk, val in waits:
                            e.wait_ge(sems[sk], val)
                        if fn is not None:
                            fn(e).then_inc(sems[inc[0]], inc[1])
                return body

            block.tensor(run("pe"))
            block.vector(run("dve"))
            block.scalar(run("act"))
            block.gpsimd(run("pool"))
            block.sync(run("sp"))
        for e in self.ops:
            self.ops[e] = []


class K:
    def __init__(self, layers=(0, 1, 2, 3)):
        self.layers = layers
        self.nc = bass.Bass("TRN2", target_bir_lowering=False)
        self.gst = contextlib.ExitStack()
        self.P = None
        self.dram = {}

    def din(self, name, shape, dt=F32):
        t = self.nc.dram_tensor(name, list(shape), dt, kind="ExternalInput").ap()
        self.dram[name] = t
        return t

    def dout(self, name, shape, dt=F32):
        t = self.nc.dram_tensor(name, list(shape), dt, kind="ExternalOutput").ap()
        self.dram[name] = t
        return t

    def sb(self, st, name, shape, dt=F32):
        self._uid = getattr(self, "_uid", 0) + 1
        t = st.enter_context(self.nc.sbuf_tensor("s%d_%s" % (self._uid, name), list(shape), dt))
        return t, self.P.buf(name)

    def sbs(self, st, name, n, shape, dt=F32):
        ts, bs = [], []
        for i in range(n):
            t, b = self.sb(st, "%s%d" % (name, i), shape, dt)
            ts.append(t)
            bs.append(b)
        return ts, bs

    def next_bank(self, which="g"):
        lst = self.bank_sets[which]
        i = self.bank_ctr[which] % len(lst)
        self.bank_ctr[which] += 1
        return self.banks[lst[i]], self.Bbanks[lst[i]]

    def alt(self):
        self._alt = 1 - self._alt
        return "dve" if self._alt else "act"

    def copy(self, eng, out, in_, reads, writes):
        if eng == "act":
            self.P.op("act", lambda e: e.activation(out, in_, AF.Identity), reads=reads, writes=writes)
        else:
            self.P.op(eng, lambda e: e.tensor_copy(out, in_), reads=reads, writes=writes)

    def slab_load(self, src, view):
        i = self.slab_ctr % len(self.slabs)
        self.slab_ctr += 1
        t = self.slabs[i]
        if view == "k8":
            dst = t[:, :, :]
        else:
            dst = self.slabs16[i]
        shp = src.shape
        d = dst[:, 0:shp[1], 0:shp[2]]
        self.P.dma("pool", lambda e: e.dma_start(out=d, in_=src), writes=[self.Bslabs[i]])
        return dst, self.Bslabs[i]

    def stream(self, srcs, view, compute, pre=None):
        items = [x if isinstance(x, tuple) else (x, view) for x in srcs]
        loaded = list(pre) if pre else []
        depth = len(self.slabs) - 1
        n = len(items)
        while len(loaded) < min(depth, n):
            loaded.append(self.slab_load(*items[len(loaded)]))
        for i in range(n):
            if len(loaded) < n and len(loaded) <= i + depth:
                loaded.append(self.slab_load(*items[len(loaded)]))
            sl, B = loaded[i]
            compute(i, sl, B)

    def preload(self, srcs, view):
        depth = len(self.slabs) - 1
        return [self.slab_load(x, view) for x in srcs[:depth]]

    class Pending(list):
        shared = True

    @staticmethod
    def flush_pending(pending):
        while pending:
            for ent in reversed(list(pending)):
                f = ent.pop(0)
                if f is not None:
                    f()
                if not ent:
                    pending.remove(ent)

    def run_jobs(self, jobs, pre=None, pending=None):
        items, owner = [], []
        for ji, (srcs, view, comp, fin) in enumerate(jobs):
            for li, x in enumerate(srcs):
                items.append((x, view))
                owner.append((ji, li))

        def comp_all(idx, sl, B):
            ji, li = owner[idx]
            jobs[ji][2](li, sl, B)
            if li == len(jobs[ji][0]) - 1 and jobs[ji][3] is not None:
                jobs[ji][3]()
        self.stream(items, None, comp_all, pre=pre)
        if pending is not None:
            self.flush_pending(pending)

    def build(self):
        nc = self.nc
        g = self.gst
        din, dout = self.din, self.dout
        x_d = din("x", [NT * 128, D])
        cond_d = din("cond", [2, D])
        ctxk_d = din("ctxk", [2, 256, D])
        ctxv_d = din("ctxv", [2, 256, D])
        stin_d = din("stin", [2, 4, 256, 512])
        flag_d = din("flag", [128, 1])
        normw_d = din("norm_w", [4, D])
        wada_d = din("w_ada", [4, D, 3 * D])
        bada_d = din("b_ada", [4, 3 * D])
        nawin_d = din("na_w_in", [2, D, 4 * D])
        nawout_d = din("na_w_out", [2, D, D])
        naqg_d = din("na_q_gain", [2, 64])
        nakg_d = din("na_k_gain", [2, 64])
        nabias_d = din("nabias", [2, 16, 128, 23 * 64])
        namask_d = din("namask", [128, 17, 512], BF16)
        retwin_d = din("ret_w_in", [1, D, 6 * D])
        retwout_d = din("ret_w_out", [1, 2 * D, D])
        retdl_d = din("ret_decay_logit", [1, 8])
        retgn_d = din("ret_gn_w", [1, 2 * D])
        ropec_d = din("rope_c", [8 * 128, 256])
        ropes_d = din("rope_s", [8 * 128, 256])
        rtab_d = din("rtab", [128, 4 * 896 + 16])
        mlpwin_d = din("mlp_w_in", [1, D, 6 * D])
        mlplnw_d = din("mlp_ln_w", [1, 2 * D])
        mlplnb_d = din("mlp_ln_b", [1, 2 * D])
        mlpws_d = din("mlp_w_s", [1, 8, 128, 128])
        mlpbs_d = din("mlp_b_s", [1, 8, 128])
        mlpwout_d = din("mlp_w_out", [1, 2 * D, D])
        y_d = dout("y", [NT * 128, D])
        nk_d = dout("nk", [2, NT * 128, D])
        nv_d = dout("nv", [2, NT * 128, D])
        st_d = dout("st", [6, 2, 4, 256, 512])
        self.P = P = Prog(nc, g)
        self._alt = 0

        self.X, _ = self.sb(g, "X", [128, NT, D])
        self.BX = [P.buf("X%d" % t) for t in range(NT)]
        X = self.X
        self.slabs, self.Bslabs = self.sbs(g, "slab", 3, [128, 8, 512], BF16)
        self.slabs16 = [t[:].rearrange("p a (h c) -> p (a h) c", h=2) for t in self.slabs]
        self.slab_ctr = 0
        self.hT, _ = self.sb(g, "hT", [128, 8, 768], BF16)
        self.BhTs = [P.buf("hT%d" % i) for i in range(6)]
        self.ident, Bident = self.sb(g, "ident", [128, 128], BF16)
        identf, Bidentf = self.sb(g, "identf", [128, 128], F32)
        self.identf, self.Bidentf = identf, Bidentf
        self.Bident = Bident
        self.gate_bc, self.Bgate = self.sb(g, "gate_bc", [128, 2, D])
        self.amod, self.Bamod = self.sb(g, "amod", [128, 8, 2])
        self.bmod, self.Bbmod = self.sb(g, "bmod", [128, 8, 2])
        self.flag, self.Bflag = self.sb(g, "flag", [128, 1])
        self.banks = [g.enter_context(nc.psum_tensor("bank%d" % i, [128, 512], F32)) for i in range(8)]
        self.Bbanks = [P.buf("bank%d" % i) for i in range(8)]
        for b in self.Bbanks:
            b.excl = True
        self.bank_sets = {"g": [0, 1, 2, 3], "a": [4, 5], "b": [6, 7]}
        self.bank_ctr = {"g": 0, "a": 0, "b": 0}
        self.yout = P.buf("yout")

        xv = x_d.rearrange("(t p) d -> p t d", p=128)
        for t in range(NT):
            P.dma("sp", lambda e, t=t: e.dma_start(out=X[:, t, :], in_=xv[:, t, :]), writes=[self.BX[t]])
        P.dma("sp", lambda e: e.dma_start(out=self.flag[:], in_=flag_d), writes=[self.Bflag])
        P.op("dve", lambda e: e.memset(identf[:], 0.0), writes=[Bidentf])
        P.op("pool", lambda e: e.affine_select(out=identf[:], in_=identf[:], pattern=[[-1, 128]],
                                               compare_op=ALU.not_equal, fill=1.0, base=0, channel_multiplier=1),
             reads=[Bidentf], writes=[Bidentf])
        P.op("dve", lambda e: e.tensor_copy(self.ident[:], identf[:]), reads=[Bidentf], writes=[Bident])
        P.emit_block()

        for li in self.layers:
            kind = li % 3
            j = li // 3
            with contextlib.ExitStack() as st:
                self.adaln(st, li, cond_d, wada_d, bada_d, normw_d)
                P.barrier()
                P.emit_block()
            with contextlib.ExitStack() as st:
                self._nm = {}
                self._optmp = {}
                if _CACHE.get("stage") == "adaln":
                    pass
                elif kind == 2:
                    self.mlp_layer(st, j, mlpwin_d, mlplnw_d, mlplnb_d, mlpws_d, mlpbs_d, mlpwout_d)
                elif kind == 0:
                    self.na_layer(st, j, nawin_d, nawout_d, naqg_d, nakg_d, nabias_d, namask_d, ctxk_d, ctxv_d, nk_d, nv_d)
                else:
                    self.ret_layer(st, j, retwin_d, retwout_d, retdl_d, retgn_d, ropec_d, ropes_d, rtab_d, stin_d, st_d)
                P.barrier()
                P.emit_block()

        yv = y_d.rearrange("(t p) d -> p t d", p=128)
        for t in range(NT):
            P.dma("sp", lambda e, t=t: e.dma_start(out=yv[:, t, :], in_=X[:, t, :]), reads=[self.BX[t]], sem_buf=self.yout)
        P.barrier()
        P.emit_block()
        return nc

    def adaln(self, st, li, cond_d, wada_d, bada_d, normw_d):
        P = self.P
        crow, Bcrow = self.sb(st, "crow", [64, D])
        crb, Bcrb = self.sb(st, "crb", [64, D], BF16)
        scT, BscT = self.sb(st, "scT", [128, 8, 64], BF16)
        modrow, Bmodrow = self.sb(st, "modrow", [64, 3 * D])
        brow, Bbrow = self.sb(st, "brow", [64, 3 * D])
        nwT, BnwT = self.sb(st, "nwT", [128, 8])
        ones, Bones = self.sb(st, "ones64", [64, 128])
        P.op("dve", lambda e: e.memset(crow[:], 0.0), writes=[Bcrow])
        P.op("dve", lambda e: e.memset(brow[:], 0.0), writes=[Bbrow])
        P.op("dve", lambda e: e.memset(ones[:], 1.0), writes=[Bones])
        P.dma("sp", lambda e: e.dma_start(out=crow[0:1, :], in_=cond_d[0:1, :]), writes=[Bcrow], sem_buf=Bcrow)
        P.dma("sp", lambda e: e.dma_start(out=crow[32:33, :], in_=cond_d[1:2, :]), writes=[Bcrow], sem_buf=Bcrow)
        P.dma("sp", lambda e: e.dma_start(out=brow[0:1, :], in_=bada_d[li:li + 1, :]), writes=[Bbrow], sem_buf=Bbrow)
        P.dma("sp", lambda e: e.dma_start(out=brow[32:33, :], in_=bada_d[li:li + 1, :]), writes=[Bbrow], sem_buf=Bbrow)
        STOP = _CACHE.get("astop", 99)
        P.op("act", lambda e: e.activation(crb[:], crow[:], AF.Silu), reads=[Bcrow], writes=[Bcrb])
        if STOP <= 0:
            return
        P.dma("sp", lambda e: e.dma_start(out=crow[1:2, :], in_=normw_d[li:li + 1, :]), reads=[Bcrb], writes=[Bcrow], sem_buf=Bcrow)
        bank, Bb = self.next_bank("g")
        bv = bank[:].bitcast(BF16)
        for kc in range(8):
            P.op("pe", lambda e, kc=kc: e.transpose(bv[:, kc * 64:(kc + 1) * 64], crb[:, kc * 128:(kc + 1) * 128], self.ident[0:64, 0:64]),
                 reads=[Bcrb, self.Bident], writes=[Bb])
        self.copy("dve", scT[:].rearrange("p k c -> p (k c)"), bv[:, 0:512], [Bb], [BscT])
        if STOP <= 1:
            return
        bank, Bb = self.next_bank("g")
        for kc in range(8):
            P.op("pe", lambda e, kc=kc: e.transpose(bank[:, kc * 64:(kc + 1) * 64], crow[:, kc * 128:(kc + 1) * 128], self.identf[0:64, 0:64]),
                 reads=[Bcrow, self.Bidentf], writes=[Bb])
        self.copy("dve", nwT[:].unsqueeze(2), bank[:, :].rearrange("p (k c) -> p k c", c=64)[:, :, 1:2], [Bb], [BnwT])
        if STOP <= 2:
            return
        srcs = [wada_d[li][:, n0:n0 + 512].rearrange("(k p) n -> p k n", p=128) for n0 in range(0, 3 * D, 512)]

        def comp(i, s, B):
            bank, Bb = self.next_bank("g")
            for kc in range(8):
                P.op("pe", lambda e, kc=kc: e.matmul(bank[0:64, :], scT[:, kc, :], s[:, kc, :], start=(kc == 0), stop=(kc == 7)),
                     reads=[BscT, B], writes=[Bb])
            P.op("dve", lambda e: e.tensor_tensor(modrow[:, i * 512:(i + 1) * 512], bank[0:64, :], brow[:, i * 512:(i + 1) * 512], ALU.add),
                 reads=[Bb, Bbrow], writes=[Bmodrow])
        self.stream(srcs, "k8", comp)
        if STOP <= 3:
            return
        for which in range(2):
            bank, Bb = self.next_bank("g")
            for kc in range(8):
                P.op("pe", lambda e, kc=kc, which=which: e.transpose(bank[:, kc * 64:(kc + 1) * 64], modrow[:, which * D + kc * 128: which * D + (kc + 1) * 128], self.identf[0:64, 0:64]),
                     reads=[Bmodrow, self.Bidentf], writes=[Bb])
            for c in range(2):
                src = bank[:, :].rearrange("p (k q) -> p k q", q=64)[:, :, 32 * c:32 * c + 1]
                if which == 0:
                    P.op("dve", lambda e, src=src, c=c: e.tensor_copy(self.bmod[:, :, c:c + 1], src), reads=[Bb], writes=[self.Bbmod])
                else:
                    P.op("dve", lambda e, src=src, c=c: e.scalar_tensor_tensor(self.amod[:, :, c:c + 1], src, 1.0, nwT[:].unsqueeze(2), ALU.add, ALU.mult),
                         reads=[Bb, BnwT], writes=[self.Bamod])
        if STOP <= 4:
            return
        ghi, Bghi = self.sb(st, "ghi", [64, D], BF16)
        glo, Bglo = self.sb(st, "glo", [64, D], BF16)
        onesb, Bonesb = self.sb(st, "onesb", [64, 2, 128], BF16)
        P.op("dve", lambda e: e.memset(onesb[:], 0.0), writes=[Bonesb])
        P.op("dve", lambda e: e.memset(onesb[0:1, 0, :], 1.0), writes=[Bonesb])
        P.op("dve", lambda e: e.memset(onesb[32:33, 1, :], 1.0), writes=[Bonesb])
        P.op("act", lambda e: e.activation(ghi[:], modrow[:, 2 * D:3 * D], AF.Identity), reads=[Bmodrow], writes=[Bghi])
        P.op("dve", lambda e: e.tensor_tensor(glo[:], modrow[:, 2 * D:3 * D], ghi[:], ALU.subtract), reads=[Bmodrow, Bghi], writes=[Bglo])
        for c in range(2):
            for h in range(2):
                bank, Bb = self.next_bank("g")
                P.op("pe", lambda e, c=c, h=h: e.matmul(bank[:, :], onesb[:, c, :], ghi[:, h * 512:(h + 1) * 512], start=True, stop=False),
                     reads=[Bonesb, Bghi], writes=[Bb])
                P.op("pe", lambda e, c=c, h=h: e.matmul(bank[:, :], onesb[:, c, :], glo[:, h * 512:(h + 1) * 512], start=False, stop=True),
                     reads=[Bonesb, Bglo], writes=[Bb])
                self.copy("act", self.gate_bc[:, c, h * 512:(h + 1) * 512], bank[:, :], [Bb], [self.Bgate])

    def norm_mod_T(self, st, tiles, tag):
        P = self.P
        X = self.X
        if not hasattr(self, "_nm"):
            self._nm = {}
        key = id(st)
        if key not in self._nm:
            xn, Bxn = self.sbs(st, "nm_xn", 2, [128, D], BF16)
            stat, Bstat = self.sbs(st, "nm_stat", 2, [128, 2])
            self._nm[key] = (xn, Bxn, stat, Bstat)
        xn, Bxn, stat, Bstat = self._nm[key]
        def stage_a(i, t, k):
            P.op("act", lambda e: e.activation(xn[k][:], X[:, t, :], AF.Square, accum_out=stat[k][:, 0:1]),
                 reads=[self.BX[t]], writes=[Bxn[k], Bstat[k]])
            P.op("dve", lambda e: e.tensor_scalar(stat[k][:, 0:1], stat[k][:, 0:1], 1.0 / D, EPS, ALU.mult, ALU.add),
                 reads=[Bstat[k]], writes=[Bstat[k]])
            P.op("act", lambda e: e.activation(stat[k][:, 1:2], stat[k][:, 0:1], AF.Sqrt), reads=[Bstat[k]], writes=[Bstat[k]])
            P.op("dve", lambda e: e.reciprocal(stat[k][:, 1:2], stat[k][:, 1:2]), reads=[Bstat[k]], writes=[Bstat[k]])
            P.op("dve", lambda e: e.tensor_scalar(xn[k][:], X[:, t, :], stat[k][:, 1:2], None, ALU.mult),
                 reads=[self.BX[t], Bstat[k]], writes=[Bxn[k]])

        def stage_b(i, t, k):
            c = 0 if t < 8 else 1
            bank, Bb = self.next_bank("g")
            bv = bank[:].bitcast(BF16)
            for kc in range(8):
                P.op("pe", lambda e, kc=kc: e.transpose(bv[:, kc * 128:(kc + 1) * 128], xn[k][:, kc * 128:(kc + 1) * 128], self.ident[:]),
                     reads=[Bxn[k], self.Bident], writes=[Bb])
            for kc in range(8):
                eng = "act" if kc % 2 == 0 else "dve"
                dst = self.hT[:, kc, i * 128:(i + 1) * 128]
                src = bv[:, kc * 128:(kc + 1) * 128]
                if eng == "act":
                    P.op("act", lambda e, dst=dst, src=src, kc=kc: e.activation(dst, src, AF.Identity, bias=self.bmod[:, kc, c:c + 1], scale=self.amod[:, kc, c:c + 1]),
                         reads=[Bb, self.Bamod, self.Bbmod], writes=[self.BhTs[i]])
                else:
                    P.op("dve", lambda e, dst=dst, src=src, kc=kc: e.tensor_scalar(dst, src, self.amod[:, kc, c:c + 1], self.bmod[:, kc, c:c + 1], ALU.mult, ALU.add),
                         reads=[Bb, self.Bamod, self.Bbmod], writes=[self.BhTs[i]])

        n = len(tiles)
        for i, t in enumerate(tiles):
            stage_a(i, t, i % 2)
            if i > 0:
                stage_b(i - 1, tiles[i - 1], (i - 1) % 2)
        stage_b(n - 1, tiles[n - 1], (n - 1) % 2)

    def out_proj_job(self, st, wout_ap, K16, oT, BoT, tiles, tag):
        P = self.P
        X = self.X
        if id(st) not in self._optmp:
            self._optmp[id(st)] = self.sbs(st, "op_tmp" + tag, 2, [128, 512])
        tmp, Btmp = self._optmp[id(st)]
        if K16 == 16:
            srcs = [wout_ap[:, n0:n0 + 256].rearrange("(k p) n -> p k n", p=128) for n0 in range(0, D, 256)]
            W = 256
            view = "k16"
        else:
            srcs = [wout_ap[:, n0:n0 + 512].rearrange("(k p) n -> p k n", p=128) for n0 in range(0, D, 512)]
            W = 512
            view = "k8"
        cnt = [0]

        def comp(i, s, B):
            for ti, t in enumerate(tiles):
                c = 0 if t < 8 else 1
                bank, Bb = self.next_bank("g")
                for kc in range(K16):
                    P.op("pe", lambda e, kc=kc, ti=ti: e.matmul(bank[:, 0:W], oT[:, kc, ti * 128:(ti + 1) * 128], s[:, kc, 0:W], start=(kc == 0), stop=(kc == K16 - 1)),
                         reads=list(BoT) + [B], writes=[Bb])
                k = cnt[0] % 2
                cnt[0] += 1
                P.op("dve", lambda e, k=k, c=c, i=i: e.tensor_tensor(tmp[k][:, 0:W], bank[:, 0:W], self.gate_bc[:, c, i * W:(i + 1) * W], ALU.mult),
                     reads=[Bb, self.Bgate], writes=[Btmp[k]])
                P.op("dve", lambda e, k=k, t=t, i=i: e.tensor_tensor(X[:, t, i * W:(i + 1) * W], X[:, t, i * W:(i + 1) * W], tmp[k][:, 0:W], ALU.add),
                     reads=[Btmp[k], self.BX[t]], writes=[self.BX[t]])
        return (srcs, view, comp, None)

    def out_proj_srcs(self, wout_ap, K16):
        if K16 == 16:
            return [wout_ap[:, n0:n0 + 256].rearrange("(k p) n -> p k n", p=128) for n0 in range(0, D, 256)], "k16"
        return [wout_ap[:, n0:n0 + 512].rearrange("(k p) n -> p k n", p=128) for n0 in range(0, D, 512)], "k8"

    def out_proj(self, st, wout_ap, K16, oT, BoT, tiles, tag, pre=None):
        self.run_jobs([self.out_proj_job(st, wout_ap, K16, oT, BoT, tiles, tag)], pre=pre)

    def proj_feat_job(self, w_ap, col0, ncols, tcol0, ntok, consume, pending=None):
        P = self.P
        srcs = [w_ap[:, col0 + n0:col0 + n0 + 512].rearrange("(k p) n -> p k n", p=128) for n0 in range(0, ncols, 512)]

        def comp(si, s, B):
            for f4 in range(4):
                bank, Bb = self.next_bank("g")
                for kc in range(8):
                    P.op("pe", lambda e, kc=kc: e.matmul(bank[:, 0:ntok], s[:, kc, f4 * 128:(f4 + 1) * 128], self.hT[:, kc, tcol0:tcol0 + ntok], start=(kc == 0), stop=(kc == 7)),
                         reads=self.BhTs[tcol0 // 128:(tcol0 + ntok) // 128] + [B], writes=[Bb])
                if pending:
                    for ent in reversed(list(pending)):
                        f = ent.pop(0)
                        if f is not None:
                            f()
                        if not ent:
                            pending.remove(ent)
                consume(si * 4 + f4, bank, Bb)
        return (srcs, "k8", comp, None)

    def proj_feat(self, *a, **kw):
        self.run_jobs([self.proj_feat_job(*a, **kw)], pre=kw.pop("pre", None) if False else None)

    def proj_tok_job(self, w_ap, col0, ncols, ntiles, consume, t0=0, pending=None):
        P = self.P
        srcs = [w_ap[:, col0 + n0:col0 + n0 + 512].rearrange("(k p) n -> p k n", p=128) for n0 in range(0, ncols, 512)]

        if pending is None:
            pending = []
        shared = getattr(pending, "shared", False)

        def step():
            for ent in reversed(list(pending)):
                f = ent.pop(0)
                if f is not None:
                    f()
                if not ent:
                    pending.remove(ent)

        def comp(si, s, B):
            for i in range(ntiles):
                bank, Bb = self.next_bank("g")
                for kc in range(8):
                    P.op("pe", lambda e, kc=kc, i=i: e.matmul(bank[:, :], self.hT[:, kc, (t0 + i) * 128:(t0 + i + 1) * 128], s[:, kc, :], start=(kc == 0), stop=(kc == 7)),
                         reads=[self.BhTs[t0 + i], B], writes=[Bb])
                step()
                later = consume(i, si * 512, bank, Bb)
                if later is not None:
                    pending.append(list(later) if isinstance(later, (list, tuple)) else [None, later])
        def fin():
            if shared:
                return
            while pending:
                step()
        return (srcs, "k8", comp, fin)

    def proj_tok(self, *a, **kw):
        self.run_jobs([self.proj_tok_job(*a, **kw)])

    def mlp_layer(self, st, j, win_d, lnw_d, lnb_d, ws_d, bs_d, wout_d):
        P = self.P
        nc = self.nc
        W2 = 2 * D
        gu, Bgu = self.sbs(st, "gu", 4, [128, W2])
        GV, _ = self.sb(st, "GV", [128, 4 * W2])
        gv = [GV[:, i * W2:(i + 1) * W2] for i in range(4)]
        Bgv = [P.buf("gv%d" % i) for i in range(4)]
        vn, Bvn = self.sbs(st, "vn", 4, [128, W2], BF16)
        lnw, Blnw = self.sb(st, "lnw", [128, W2])
        lnb, Blnb = self.sb(st, "lnb", [128, W2])
        wsl, Bwsl = self.sb(st, "wsl", [128, 8, 128], BF16)
        wsT, BwsT = self.sb(st, "wsT", [128, 8, 128], BF16)
        bs, Bbs = self.sb(st, "bs", [128, 8])
        tmp, Btmp = self.sbs(st, "mtmp", 2, [128, 512])
        mst, Bmst = self.sbs(st, "mst", 4, [128, 8])
        Bmst2 = [P.buf("mst2_%d" % i) for i in range(4)]
        ob = [GV[:, (2 + k) * W2:(2 + k) * W2 + D].bitcast(BF16) for k in range(2)]
        Bob = [Bgv[2], Bgv[3]]
        oT = GV[:, 0:2 * W2].bitcast(BF16).rearrange("p (k c) -> p k c", c=512)
        BoT = [Bgv[0], Bgv[1]]
        P.dma("sp", lambda e: e.dma_start(out=lnw[:], in_=lnw_d[j:j + 1, :].partition_broadcast(128)), writes=[Blnw])
        P.dma("sp", lambda e: e.dma_start(out=lnb[:], in_=lnb_d[j:j + 1, :].partition_broadcast(128)), writes=[Blnb])
        P.dma("pool", lambda e: e.dma_start(out=wsl[:], in_=ws_d[j].rearrange("g i k -> i g k")), writes=[Bwsl])
        bsr, Bbsr = self.sb(st, "bsr", [64, 128])
        P.op("dve", lambda e: e.memset(bsr[:], 0.0), writes=[Bbsr])
        P.dma("sp", lambda e: e.dma_start(out=bsr[0:8, :], in_=bs_d[j]), writes=[Bbsr])
        bank, Bb = self.next_bank("g")
        P.op("pe", lambda e: e.transpose(bank[:, 0:64], bsr[:], self.identf[0:64, 0:64]), reads=[Bbsr, self.Bidentf], writes=[Bb])
        self.copy("dve", bs[:], bank[:, 0:8], [Bb], [Bbs])
        bank, Bb = self.next_bank("g")
        bv = bank[:].bitcast(BF16)
        for gq in range(8):
            P.op("pe", lambda e, gq=gq: e.transpose(bv[:, gq * 128:(gq + 1) * 128], wsl[:, gq, :], self.ident[:]), reads=[Bwsl, self.Bident], writes=[Bb])
        self.copy("dve", wsT[:].rearrange("p g i -> p (g i)"), bv[:, :], [Bb], [BwsT])

        MS = _CACHE.get("mstop", 99)
        if MS <= 0:
            return
        for grp in range(3):
            tiles = [4 * grp + i for i in range(4)]
            self.norm_mod_T(st, tiles, "mlp")
            ctr = [0]
            if MS <= 1:
                continue

            GELU = AF.Identity if _CACHE.get("noact") else AF.Gelu_apprx_tanh
            SILU = AF.Identity if _CACHE.get("noact") else AF.Silu

            def consume_v(i, c0, bank, Bb):
                P.op("act", lambda e: e.activation(gv[i][:, c0:c0 + 512], bank[:, :], GELU), reads=[Bb], writes=[Bgv[i]])

            def consume_u(i, n0, bank, Bb):
                P.op("act", lambda e: e.activation(gu[i][:, n0:n0 + 512], bank[:, :], GELU), reads=[Bb], writes=[Bgu[i]])

            def consume_g(i, c0, bank, Bb):
                k = ctr[0] % 2
                ctr[0] += 1
                P.op("act", lambda e: e.activation(tmp[k][:], bank[:, :], SILU), reads=[Bb], writes=[Btmp[k]])
                P.op("dve", lambda e: e.tensor_tensor(gu[i][:, c0:c0 + 512], gu[i][:, c0:c0 + 512], tmp[k][:], ALU.mult),
                     reads=[Btmp[k], Bgu[i]], writes=[Bgu[i]])

            self.proj_tok(win_d[j], W2, W2, 4, consume_v)
            if MS <= 2:
                continue
            ln_ops = []

            def L(eng, fn, reads, writes):
                ln_ops.append(lambda: P.op(eng, fn, reads=reads, writes=writes))
            for i in range(4):
                s = mst[i]
                Bm1, Bm2 = Bmst[i], Bmst2[i]
                L("dve", lambda e, s=s, i=i: e.reduce_sum(s[:, 4:5], gv[i], AX.X), [Bgv[i]], [Bm1])
                L("act", lambda e, s=s, i=i: e.activation(vn[i][:], gv[i], AF.Square, accum_out=s[:, 5:6]), [Bgv[i]], [Bvn[i], Bm2])
                L("dve", lambda e, s=s: e.tensor_scalar(s[:, 4:5], s[:, 4:5], 1.0 / W2, None, ALU.mult), [Bm1], [Bm1])
                L("dve", lambda e, s=s: e.tensor_tensor(s[:, 6:7], s[:, 4:5], s[:, 4:5], ALU.mult), [Bm1], [Bm1])
                L("dve", lambda e, s=s: e.scalar_tensor_tensor(s[:, 5:6], s[:, 5:6], 1.0 / W2, s[:, 6:7], ALU.mult, ALU.subtract), [Bm1, Bm2], [Bm1, Bm2])
                L("dve", lambda e, s=s: e.tensor_scalar(s[:, 5:6], s[:, 5:6], EPS, None, ALU.add), [Bm1, Bm2], [Bm1, Bm2])
                L("act", lambda e, s=s: e.activation(s[:, 6:7], s[:, 5:6], AF.Sqrt), [Bm1, Bm2], [Bm1, Bm2])
                L("dve", lambda e, s=s: e.reciprocal(s[:, 6:7], s[:, 6:7]), [Bm1, Bm2], [Bm1, Bm2])
                L("dve", lambda e, s=s: e.scalar_tensor_tensor(s[:, 7:8], s[:, 4:5], -1.0, s[:, 6:7], ALU.mult, ALU.mult), [Bm1, Bm2], [Bm1, Bm2])
                L("act", lambda e, s=s, i=i: e.activation(gv[i], gv[i], AF.Identity, bias=s[:, 7:8], scale=s[:, 6:7]), [Bgv[i], Bm1, Bm2], [Bgv[i]])
                L("dve", lambda e, i=i: e.tensor_tensor(gv[i], gv[i], lnw[:], ALU.mult), [Bgv[i], Blnw], [Bgv[i]])
                L("dve", lambda e, i=i: e.tensor_tensor(vn[i][:], gv[i], lnb[:], ALU.add), [Bgv[i], Blnb], [Bvn[i]])

            def trickle(fn):
                def wrapped(i, n0, bank, Bb):
                    fn(i, n0, bank, Bb)
                    for _ in range(2):
                        if ln_ops:
                            ln_ops.pop(0)()
                return wrapped
            self.run_jobs([self.proj_tok_job(win_d[j], 0, W2, 4, trickle(consume_u)),
                           self.proj_tok_job(win_d[j], 2 * W2, W2, 4, trickle(consume_g))])
            while ln_ops:
                ln_ops.pop(0)()
            osrcs, oview = self.out_proj_srcs(wout_d[j], 16)
            opre = self.preload(osrcs, oview)
            if MS <= 3:
                continue
            for i in range(4):
                k = i % 2
                for gp in range(4):
                    bank, Bb = self.next_bank("g")
                    for h in range(2):
                        gq = 2 * gp + h
                        P.op("pe", lambda e, gq=gq, h=h, i=i: e.matmul(bank[:, h * 256:(h + 1) * 256], wsT[:, gq, :], vn[i][:, gq * 256:(gq + 1) * 256], start=True, stop=True),
                             reads=[BwsT, Bvn[i]], writes=[Bb])
                    for h in range(2):
                        gq = 2 * gp + h
                        P.op("dve", lambda e, gq=gq, h=h, i=i, k=k: e.scalar_tensor_tensor(ob[k][:, gq * 256:(gq + 1) * 256], bank[:, h * 256:(h + 1) * 256], bs[:, gq:gq + 1],
                                                                                 gu[i][:, gq * 256:(gq + 1) * 256], ALU.add, ALU.mult),
                             reads=[Bb, Bbs, Bgu[i]], writes=[Bob[k]])
                for half in range(2):
                    bank, Bb = self.next_bank("g")
                    bv = bank[:].bitcast(BF16)
                    for q in range(8):
                        kc = half * 8 + q
                        P.op("pe", lambda e, q=q, kc=kc, k=k: e.transpose(bv[:, q * 128:(q + 1) * 128], ob[k][:, kc * 128:(kc + 1) * 128], self.ident[:]),
                             reads=[Bob[k], self.Bident], writes=[Bb])
                    self.copy("act" if half == 0 else "dve", oT[:, half * 8:(half + 1) * 8, i * 128:(i + 1) * 128],
                              bv[:, :].rearrange("p (q c) -> p q c", c=128), [Bb], BoT)
            if MS <= 4:
                continue
            self.out_proj(st, wout_d[j], 16, oT, BoT, tiles, "mlp", pre=opre)

    def na_layer(self, st, j, win_d, wout_d, qg_d, kg_d, bias_d, mask_d, ctxk_d, ctxv_d, nk_d, nv_d):
        P = self.P
        kT, BkT = self.sb(st, "kT", [128, 8, 768], BF16)
        qT, BqT = self.sb(st, "qT", [128, 8, 512], BF16)
        vt, Bvt = self.sbs(st, "vt", 6, [128, 8, 192], BF16)
        vct, Bvct = self.sbs(st, "vct", 2, [128, 8, 192], BF16)
        kcT, BkcT = self.sb(st, "kcT", [128, 8, 256], BF16)
        ckb, Bckb = self.sb(st, "ckb", [128, 2, D], BF16)
        sgT, BsgT = self.sb(st, "sgT", [128, 8, 512], BF16)
        qgn, Bqgn = self.sb(st, "qgn", [128, 64])
        kgn, Bkgn = self.sb(st, "kgn", [128, 64])
        onesel, Bones = self.sb(st, "onesel", [128, 192], BF16)
        tmp, Btmp = self.sbs(st, "na_tmp", 2, [128, 512])
        kn, Bkn = self.sbs(st, "kn", 3, [128, 512])
        knb, Bknb = self.sbs(st, "knb", 3, [128, 512], BF16)
        vst, Bvst = self.sbs(st, "vst", 2, [128, 512])
        nst, Bnst = self.sbs(st, "nst", 4, [128, 16])
        bias, Bbias = self.sbs(st, "bias", 2, [128, 23 * 64], BF16)
        mask, Bmask = self.sb(st, "mask", [128, 7, 512], BF16)
        ex, Bex = self.sbs(st, "ex", 3, [128, 512], BF16)
        pT, BpT = self.sbs(st, "pT", 3, [128, 512], BF16)
        rden, Brden = self.sb(st, "rden", [128, 512])
        ogT = self.hT[:, :, 0:512]
        ident = self.ident

        P.dma("sp", lambda e: e.dma_start(out=qgn[:], in_=qg_d[j:j + 1, :].partition_broadcast(128)), writes=[Bqgn])
        P.dma("sp", lambda e: e.dma_start(out=kgn[:], in_=kg_d[j:j + 1, :].partition_broadcast(128)), writes=[Bkgn])
        P.op("dve", lambda e: e.tensor_scalar(qgn[:], qgn[:], 0.125, None, ALU.mult), reads=[Bqgn], writes=[Bqgn])
        P.op("dve", lambda e: e.memset(onesel[:], 1.0), writes=[Bones])
        P.op("dve", lambda e: e.memset(onesel[:, 64:128], 0.0), writes=[Bones])
        for i in range(6):
            P.op("dve", lambda e, i=i: e.memset(vt[i][:], 0.0), writes=[Bvt[i]])
        for c2 in range(2):
            P.op("dve", lambda e, c2=c2: e.memset(vct[c2][:], 0.0), writes=[Bvct[c2]])
        P.dma("pool", lambda e: e.dma_start(out=ckb[:], in_=ctxk_d[j].rearrange("(c p) d -> p c d", p=128)), writes=[Bckb])
        for c2 in range(2):
            bank, Bb = self.next_bank("g")
            bv = bank[:].bitcast(BF16)
            for kc in range(8):
                P.op("pe", lambda e, kc=kc: e.transpose(bv[:, kc * 128:(kc + 1) * 128], ckb[:, c2, kc * 128:(kc + 1) * 128], ident[:]),
                     reads=[Bckb, self.Bident], writes=[Bb])
            self.copy("dve", kcT[:, :, c2 * 128:(c2 + 1) * 128], bv[:, :].rearrange("p (k c) -> p k c", c=128), [Bb], [BkcT])
            src = ctxv_d[j][c2 * 128:(c2 + 1) * 128, :].rearrange("p (a t d) -> p a t d", t=2, d=64)
            P.dma("pool", lambda e: e.dma_start(out=vct[c2][:, :, 0:64], in_=src[:, :, 0, :]), writes=[Bvct[c2]], sem_buf=Bvct[c2])
            P.dma("pool", lambda e: e.dma_start(out=vct[c2][:, :, 128:192], in_=src[:, :, 1, :]), writes=[Bvct[c2]], sem_buf=Bvct[c2])

        cn = [0]
        vc = [0]
        ec = [0]
        pc = [0]
        bc = [0]

        def qk_consumer(gain, Bgain, dstT, BdstT, dram_row0):
            def consume(i, n0, bank, Bb):
                k = cn[0] % 3
                k4 = cn[0] % 4
                k2 = cn[0] % 2
                cn[0] += 1
                b3 = bank[:, :].rearrange("p (h d) -> p h d", d=64)
                k3 = kn[k][:].rearrange("p (h d) -> p h d", d=64)
                st_ = nst[k4]
                Bst = Bnst[k4]
                P.op("act", lambda e: e.activation(tmp[k2][:], bank[:, :], AF.Square), reads=[Bb], writes=[Btmp[k2]])
                P.op("dve", lambda e: e.tensor_reduce(st_[:, 0:8], tmp[k2][:].rearrange("p (h d) -> p h d", d=64), AX.X, ALU.add),
                     reads=[Btmp[k2]], writes=[Bst])
                P.op("dve", lambda e: e.tensor_scalar(st_[:, 0:8], st_[:, 0:8], 1.0 / 64, EPS, ALU.mult, ALU.add), reads=[Bst], writes=[Bst])

                def s1():
                    P.op("act", lambda e: e.activation(st_[:, 8:16], st_[:, 0:8], AF.Sqrt), reads=[Bst], writes=[Bst])
                    P.op("dve", lambda e: e.reciprocal(st_[:, 8:16], st_[:, 8:16]), reads=[Bst], writes=[Bst])
                    P.op("dve", lambda e: e.tensor_tensor(k3, b3, st_[:, 8:16].unsqueeze(2).to_broadcast([128, 8, 64]), ALU.mult),
                         reads=[Bb, Bst], writes=[Bkn[k]])
                    r0 = dram_row0(i)
                    if r0 is not None:
                        P.op("dve", lambda e: e.tensor_tensor(k3, k3, gain[:].unsqueeze(1).to_broadcast([128, 8, 64]), ALU.mult),
                             reads=[Bkn[k], Bgain], writes=[Bkn[k]])
                        P.dma("sp", lambda e: e.dma_start(out=nk_d[j, r0:r0 + 128, n0:n0 + 512], in_=kn[k][:]), reads=[Bkn[k]], sem_buf=Bkn[k])
                    else:
                        kb3 = knb[k][:].rearrange("p (h d) -> p h d", d=64)
                        P.op("dve", lambda e: e.tensor_tensor(kb3, k3, gain[:].unsqueeze(1).to_broadcast([128, 8, 64]), ALU.mult),
                             reads=[Bkn[k], Bgain], writes=[Bknb[k]])

                def s2():
                    if dram_row0(i) is not None:
                        P.op("act", lambda e: e.activation(knb[k][:], kn[k][:], AF.Identity), reads=[Bkn[k]], writes=[Bknb[k]])

                def s3():
                    bank2, Bb2 = self.next_bank("b")
                    bv2 = bank2[:].bitcast(BF16)
                    for q in range(4):
                        P.op("pe", lambda e, q=q: e.transpose(bv2[:, q * 128:(q + 1) * 128], knb[k][:, q * 128:(q + 1) * 128], ident[:]),
                             reads=[Bknb[k], self.Bident], writes=[Bb2])
                    c0 = n0 // 128
                    self.copy("act", dstT[:, c0:c0 + 4, i * 128:(i + 1) * 128], bv2[:, 0:512].rearrange("p (q c) -> p q c", c=128), [Bb2], [BdstT])
                return [s1, s2, s3]
            return consume

        def v_consumer(dram_row0):
            def consume(i, n0, bank, Bb):
                s4 = n0 // 128
                src4 = bank[:, :].rearrange("p (a t d) -> p a t d", t=2, d=64)
                veng = "dve" if "A" in _CACHE.get("nskip", "") else "act"
                self.copy(veng, vt[i][:, s4:s4 + 4, 0:64], src4[:, :, 0, :], [Bb], [Bvt[i]])
                self.copy(veng, vt[i][:, s4:s4 + 4, 128:192], src4[:, :, 1, :], [Bb], [Bvt[i]])
                r0 = dram_row0(i)
                if r0 is not None and "D" not in _CACHE.get("nskip", ""):
                    k2 = vc[0] % 2
                    vc[0] += 1
                    P.op("dve", lambda e: e.tensor_copy(vst[k2][:], bank[:, :]), reads=[Bb], writes=[Bvst[k2]])
                    P.dma("sp", lambda e: e.dma_start(out=nv_d[j, r0:r0 + 128, n0:n0 + 512], in_=vst[k2][:]), reads=[Bvst[k2]], sem_buf=Bvst[k2])
            return consume

        def g_consume(fc, bank, Bb):
            P.op("act", lambda e: e.activation(sgT[:, fc, :], bank[:, :], AF.Silu), reads=[Bb], writes=[BsgT])

        def attend(N, qc0, keys, has_bias):
            LOOK = 2
            items = [(p, h2, key) for p in range(8) for h2 in range(2) for key in keys]
            nk_ = 2 * len(keys)
            nit = len(items)
            kbs = {}
            acc = {}
            norms = []

            def load_bias(h):
                if has_bias and h < 16 and h not in kbs:
                    kb = bc[0] % 2
                    bc[0] += 1
                    kbs[h] = kb
                    P.dma("pool", lambda e: e.dma_start(out=bias[kb][:], in_=bias_d[j, h]), writes=[Bbias[kb]])

            def emit_S(n):
                p, h2, (kTs, BkTs, kc0, vtl, Bv, m0, ms) = items[n]
                if n % len(keys) == 0:
                    load_bias(2 * p + h2)
                    load_bias(2 * p + h2 + 1)
                sbank, Bs = self.next_bank("g")
                r0 = 64 * h2
                P.op("pe", lambda e: e.matmul(sbank[:, 0:N], kTs[r0:r0 + 64, p, kc0:kc0 + 128], qT[r0:r0 + 64, p, qc0:qc0 + N], start=True, stop=(m0 is None)),
                     reads=[BkTs, BqT], writes=[Bs])
                if m0 is not None:
                    kb = kbs[2 * p + h2]
                    P.op("pe", lambda e: e.matmul(sbank[:, 0:N], ident[:], bias[kb][:, m0 * 64:(m0 + 8) * 64], start=False, stop=True),
                         reads=[self.Bident, Bbias[kb]], writes=[Bs])
                return sbank, Bs
            pend = {}
            for n in range(min(LOOK, nit)):
                pend[n] = emit_S(n)
            for n in range(nit):
                p, h2, (kTs, BkTs, kc0, vtl, Bv, m0, ms) = items[n]
                if p not in acc:
                    acc[p] = (self.next_bank("a"), self.next_bank("b"))
                (ob, Bob), (db, Bdb) = acc[p]
                sbank, Bs = pend.pop(n)
                kx = ec[0] % 3
                ec[0] += 1
                P.op("act", lambda e: e.activation(ex[kx][:, 0:N], sbank[:, 0:N], AF.Exp), reads=[Bs], writes=[Bex[kx]])
                if ms is not None:
                    kp = pc[0] % 3
                    pc[0] += 1
                    P.op("dve", lambda e: e.tensor_tensor(pT[kp][:, 0:N], ex[kx][:, 0:N], mask[:, ms, 0:N], ALU.mult),
                         reads=[Bex[kx], Bmask], writes=[BpT[kp]])
                    src, Bsrc = pT[kp], BpT[kp]
                else:
                    src, Bsrc = ex[kx], Bex[kx]
                if n + LOOK < nit:
                    pend[n + LOOK] = emit_S(n + LOOK)
                lv = vtl[:, p, 0:128] if h2 == 0 else vtl[:, p, 64:192]
                lo = onesel[:, 0:128] if h2 == 0 else onesel[:, 64:192]
                idx = n % nk_
                first, last = (idx == 0), (idx == nk_ - 1)
                P.op("pe", lambda e: e.matmul(ob[:, 0:N], lv, src[:, 0:N], start=first, stop=last), reads=[Bv, Bsrc], writes=[Bob])
                P.op("pe", lambda e: e.matmul(db[:, 0:N], lo, src[:, 0:N], start=first, stop=last), reads=[Bones, Bsrc], writes=[Bdb])
                if last:
                    def norm(p=p, ob=ob, Bob=Bob, db=db, Bdb=Bdb):
                        P.op("dve", lambda e: e.reciprocal(rden[:, 0:N], db[:, 0:N]), reads=[Bdb], writes=[Brden])
                        P.op("dve", lambda e: e.tensor_tensor(rden[:, 0:N], ob[:, 0:N], rden[:, 0:N], ALU.mult), reads=[Bob, Brden], writes=[Brden])
                        P.op("dve", lambda e: e.tensor_tensor(ogT[:, p, qc0:qc0 + N], rden[:, 0:N], sgT[:, p, qc0:qc0 + N], ALU.mult),
                             reads=[Brden, BsgT], writes=self.BhTs[qc0 // 128:(qc0 + N) // 128])
                    norms.append((n + 4, norm))
                while norms and norms[0][0] <= n:
                    norms.pop(0)[1]()
            while norms:
                norms.pop(0)[1]()

        NS = _CACHE.get("nstop", 99)
        for u in range(3):
            if u < 2:
                ltiles = [2 * u + i for i in range(6)]
                own0 = 2 * u
                own_tiles = [4 * u + i for i in range(4)]
            else:
                ltiles = [8, 9, 10, 11]
                own0 = 0
                own_tiles = ltiles
            nl = len(ltiles)
            self.norm_mod_T(st, ltiles, "na")

            def row0(i, ltiles=ltiles, own_tiles=own_tiles):
                t = ltiles[i]
                return t * 128 if t in own_tiles else None
            if NS <= 0:
                continue
            pend = self.Pending()
            self.run_jobs([
                self.proj_tok_job(win_d[j], D, D, nl, qk_consumer(kgn, Bkgn, kT, BkT, row0), pending=pend),
                self.proj_tok_job(win_d[j], 2 * D, D, nl, v_consumer(row0), pending=pend),
                self.proj_tok_job(win_d[j], 0, D, 4, qk_consumer(qgn, Bqgn, qT, BqT, lambda i: None), t0=own0, pending=pend),
                self.proj_feat_job(win_d[j], 3 * D, D, own0 * 128, 512, g_consume, pending=pend)], pending=pend)
            osrcs, oview = self.out_proj_srcs(wout_d[j], 8)
            opre = self.preload(osrcs, oview)
            if NS <= 1:
                continue
            if u < 2:
                P.dma("sp", lambda e: e.dma_start(out=mask[:, 0:6, :], in_=mask_d[:, 6 * u:6 * u + 6, :]), writes=[Bmask], sem_buf=Bmask)
                P.dma("sp", lambda e: e.dma_start(out=mask[:, 6, :], in_=mask_d[:, 16, :]), writes=[Bmask], sem_buf=Bmask)
                keys = []
                for li in range(6):
                    t = ltiles[li]
                    m0 = 11 - 2 * t + 8 * u
                    keys.append((kT, BkT, li * 128, vt[li], Bvt[li], m0, li))
                for c2 in range(2):
                    keys.append((kcT, BkcT, c2 * 128, vct[c2], Bvct[c2], None, 6))
                attend(512, 0, keys, True)
            else:
                for bb in range(2):
                    keys = [(kT, BkT, (2 * bb + q) * 128, vt[2 * bb + q], Bvt[2 * bb + q], None, None) for q in range(2)]
                    attend(256, 256 * bb, keys, False)
            if NS <= 2:
                continue
            self.out_proj(st, wout_d[j], 8, ogT, self.BhTs[0:4], own_tiles, "na", pre=opre)

    def ret_layer(self, st, j, win_d, wout_d, dl_d, gn_d, ropec_d, ropes_d, rtab_d, stin_d, st_d):
        P = self.P
        KS = 1.0 / 16.0
        Wt, BWt = self.sb(st, "Wt", [128, 4, 896], BF16)
        lg, Blg = self.sb(st, "lg", [128, 8])
        wtab, Bwtab = self.sb(st, "wtab", [128, 2, 2, 4])
        dq, Bdq = self.sb(st, "dq", [128, 2, 4, 4])
        cf, Bcf = self.sb(st, "cf", [128, 2, 8])
        gnwT, BgnwT = self.sb(st, "gnwT", [128, 16])
        gst, Bgst = self.sb(st, "gst", [128, 12])
        Bgst2 = P.buf("gst2")
        ident = self.ident

        def kd(a, d):
            return KD[:, (a * 2 + d) * 1024:(a * 2 + d + 1) * 1024]

        def PT(h, jt):
            return KD[:, (h * 4 + jt) * 512:(h * 4 + jt + 1) * 512]

        with contextlib.ExitStack() as st2:
            rt, Brt = self.sb(st2, "rt", [128, 3600])
            e1, Be1 = self.sb(st2, "e1", [128, 896])
            e2, Be2 = self.sb(st2, "e2", [128, 896])
            dl, Bdl = self.sb(st2, "dl", [128, 8])
            gr, Bgr = self.sb(st2, "gr", [64, 128])
            P.dma("sp", lambda e: e.dma_start(out=rt[:], in_=rtab_d), writes=[Brt])
            P.dma("sp", lambda e: e.dma_start(out=dl[:], in_=dl_d[0:1, :].partition_broadcast(128)), writes=[Bdl])
            P.op("act", lambda e: e.activation(lg[:], dl[:], AF.Exp, scale=-1.0), reads=[Bdl], writes=[Blg])
            P.op("dve", lambda e: e.tensor_scalar(lg[:], lg[:], 1.0, None, ALU.add), reads=[Blg], writes=[Blg])
            P.op("act", lambda e: e.activation(lg[:], lg[:], AF.Ln), reads=[Blg], writes=[Blg])
            P.op("dve", lambda e: e.tensor_scalar(lg[:], lg[:], -1.0, None, ALU.mult), reads=[Blg], writes=[Blg])
            for h in range(4):
                P.op("act", lambda e, h=h: e.activation(e1[:], rt[:, 0:896], AF.Exp, scale=lg[:, h:h + 1]), reads=[Brt, Blg], writes=[Be1])
                P.op("dve", lambda e: e.tensor_tensor(e1[:], e1[:], rt[:, 1792:2688], ALU.mult), reads=[Be1, Brt], writes=[Be1])
                P.op("act", lambda e, h=h: e.activation(e2[:], rt[:, 896:1792], AF.Exp, scale=lg[:, 4 + h:5 + h]), reads=[Brt, Blg], writes=[Be2])
                P.op("dve", lambda e: e.tensor_tensor(e2[:], e2[:], rt[:, 2688:3584], ALU.mult), reads=[Be2, Brt], writes=[Be2])
                P.op("dve", lambda e: e.tensor_tensor(e1[:], e1[:], e2[:], ALU.add), reads=[Be1, Be2], writes=[Be1])
                P.op("dve", lambda e, h=h: e.tensor_scalar(Wt[:, h, :], e1[:], KS, None, ALU.mult), reads=[Be1], writes=[BWt])
            base = 3584
            for d in range(2):
                P.op("dve", lambda e, d=d: e.tensor_tensor(wtab[:, d, :, :], rt[:, base + 2 * d:base + 2 * d + 2].unsqueeze(2).to_broadcast([128, 2, 4]),
                                                         lg[:, 4 * d:4 * d + 4].unsqueeze(1).to_broadcast([128, 2, 4]), ALU.mult),
                     reads=[Brt, Blg], writes=[Bwtab])
                P.op("dve", lambda e, d=d: e.tensor_tensor(dq[:, d, :, :], rt[:, base + 4 + 4 * d:base + 8 + 4 * d].unsqueeze(2).to_broadcast([128, 4, 4]),
                                                         lg[:, 4 * d:4 * d + 4].unsqueeze(1).to_broadcast([128, 4, 4]), ALU.mult),
                     reads=[Brt, Blg], writes=[Bdq])
            wt2 = wtab[:].rearrange("p a b c -> p (a b c)")
            dq2 = dq[:].rearrange("p a b c -> p (a b c)")
            P.op("act", lambda e: e.activation(wt2, wt2, AF.Exp), reads=[Bwtab], writes=[Bwtab])
            P.op("dve", lambda e: e.tensor_scalar(wt2, wt2, KS, None, ALU.mult), reads=[Bwtab], writes=[Bwtab])
            P.op("act", lambda e: e.activation(dq2, dq2, AF.Exp), reads=[Bdq], writes=[Bdq])
            P.op("dve", lambda e: e.tensor_scalar(dq2, dq2, self.flag[:, 0:1], None, ALU.mult), reads=[Bdq, self.Bflag], writes=[Bdq])
            P.op("act", lambda e: e.activation(cf[:, 0, :], lg[:], AF.Exp, scale=512.0), reads=[Blg], writes=[Bcf])
            P.op("act", lambda e: e.activation(cf[:, 1, :], lg[:], AF.Exp, scale=256.0), reads=[Blg], writes=[Bcf])
            P.op("dve", lambda e: e.memset(gr[:], 0.0), writes=[Bgr])
            P.dma("sp", lambda e: e.dma_start(out=gr[0:16, :], in_=gn_d[j].rearrange("(k p) -> k p", p=128)), writes=[Bgr])
            bank, Bb = self.next_bank("g")
            P.op("pe", lambda e: e.transpose(bank[:, 0:64], gr[:], self.identf[0:64, 0:64]), reads=[Bgr, self.Bidentf], writes=[Bb])
            self.copy("dve", gnwT[:], bank[:, 0:16], [Bb], [BgnwT])
            P.barrier()
            P.emit_block()

        QK, BQK = self.sb(st, "QK", [128, 16, 512], BF16)
        KD, BKD = self.sb(st, "KD", [128, 8 * 1024], BF16)
        v, Bv = self.sbs(st, "rv", 4, [128, 2048], BF16)
        o2, Bo2 = self.sbs(st, "ro", 2, [128, 2048])
        SF0, BSF0 = self.sb(st, "SF0", [128, 8, 512], BF16)
        SF1, BSF1 = self.sb(st, "SF1", [128, 8, 512], BF16)
        SBb, BSBb = self.sb(st, "SBb", [128, 8, 512], BF16)
        RC, BRC = self.sb(st, "RC", [128, 4, 256])
        RS, BRS = self.sb(st, "RS", [128, 4, 256])
        yb, Byb = RC[:].rearrange("p a c -> p (a c)").bitcast(BF16), BRC
        qr, Bqr = self.sbs(st, "qr", 2, [128, 512])
        t2, Bt2 = self.sb(st, "t2", [128, 512])
        qrb, Bqrb = self.sbs(st, "qrb", 2, [128, 512], BF16)
        stg, Bstg = qr, Bqr
        sxs, Bsxs = t2, Bt2
        nstat, Bnstat = self.sbs(st, "nm_stat", 2, [128, 2])
        self._nm[id(st)] = ([t2[:].bitcast(BF16), qr[0][:].bitcast(BF16)], [Bt2, Bqr[0]], nstat, Bnstat)
        rs2 = RS[:].rearrange("p a c -> p (a c)")
        self._optmp[id(st)] = ([rs2[:, 0:512], rs2[:, 512:1024]], [BRS, BRS])
        P.dma("pool", lambda e: e.dma_start(out=SF0[:], in_=stin_d[0].rearrange("h (c p) v -> p (h c) v", p=128)), writes=[BSF0])

        qc = [0]
        sc = [0]

        def run_group(g, mode):
            tiles = [4 * g + a for a in range(4)]
            rope = g < 2
            self.norm_mod_T(st, tiles, "ret")
            if rope and mode == "full" or (rope and mode == "pre"):
                P.dma("sp", lambda e: e.dma_start(out=RC[:], in_=ropec_d[g * 512:(g + 1) * 512, :].rearrange("(a p) c -> p a c", p=128)), writes=[BRC])
                P.dma("sp", lambda e: e.dma_start(out=RS[:], in_=ropes_d[g * 512:(g + 1) * 512, :].rearrange("(a p) c -> p a c", p=128)), writes=[BRS])

            def rope_to(dst, Bdst, bank, Bb, a):
                if not rope:
                    self.copy("act", dst, bank[:, :], [Bb], [Bdst])
                    return
                d3 = dst.rearrange("p (h c) -> p h c", h=2)
                x3 = bank[:, :].rearrange("p (h c) -> p h c", h=2)
                P.op("dve", lambda e: e.tensor_tensor(d3, x3, RC[:, a, :].unsqueeze(1).to_broadcast([128, 2, 256]), ALU.mult),
                     reads=[Bb, BRC], writes=[Bdst])
                x5 = bank[:, :].rearrange("p (h b f d) -> p h b f d", h=2, b=2, f=2, d=64)
                t5 = t2[:].rearrange("p (h b f d) -> p h b f d", h=2, b=2, f=2, d=64)
                S4 = RS[:, a, :].rearrange("p (b f d) -> p b f d", b=2, f=2, d=64)
                for hf in range(2):
                    P.op("dve", lambda e, hf=hf: e.tensor_tensor(t5[:, :, :, hf, :], x5[:, :, :, 1 - hf, :],
                                                               S4[:, :, hf, :].unsqueeze(1).to_broadcast([128, 2, 2, 64]), ALU.mult),
                         reads=[Bb, BRS], writes=[Bt2])
                P.op("dve", lambda e: e.tensor_tensor(dst, dst, t2[:], ALU.add), reads=[Bt2, Bdst], writes=[Bdst])

            def qk_consumer(chunk0, is_k):
                def consume(a, n0, bank, Bb):
                    k = qc[0] % 2
                    qc[0] += 1
                    rope_to(qr[k][:], Bqr[k], bank, Bb, a)
                    P.op("act", lambda e: e.activation(qrb[k][:], qr[k][:], AF.Identity), reads=[Bqr[k]], writes=[Bqrb[k]])
                    if is_k:
                        for hh in range(2):
                            h = n0 // 256 + hh
                            for d in range(2):
                                P.op("dve", lambda e, hh=hh, h=h, d=d: e.tensor_scalar(kd(a, d)[:, n0 + hh * 256:n0 + (hh + 1) * 256], qr[k][:, hh * 256:(hh + 1) * 256],
                                                                                     wtab[:, d, a % 2, h:h + 1], None, ALU.mult),
                                     reads=[Bqr[k], Bwtab], writes=[BKD])
                    def later():
                        bank2, Bb2 = self.next_bank("g")
                        bv2 = bank2[:].bitcast(BF16)
                        for q in range(4):
                            P.op("pe", lambda e, q=q: e.transpose(bv2[:, q * 128:(q + 1) * 128], qrb[k][:, q * 128:(q + 1) * 128], ident[:]),
                                 reads=[Bqrb[k], self.Bident], writes=[Bb2])
                        c0 = chunk0 + n0 // 128
                        self.copy("act", QK[:, c0:c0 + 4, a * 128:(a + 1) * 128], bv2[:, 0:512].rearrange("p (q c) -> p q c", c=128), [Bb2], [BQK])
                    return later
                return consume

            def v_consume(a, n0, bank, Bb):
                self.copy("act", v[a][:, n0:n0 + 512], bank[:, :], [Bb], [Bv[a]])

            jobs = []
            pend = self.Pending()
            if mode == "full":
                jobs.append(self.proj_tok_job(win_d[j], 0, D, 4, qk_consumer(0, False), pending=pend))
            jobs.append(self.proj_tok_job(win_d[j], D, D, 4, qk_consumer(8, True), pending=pend))
            jobs.append(self.proj_tok_job(win_d[j], 2 * D, 2 * D, 4, v_consume, pending=pend))
            self.run_jobs(jobs, pending=pend)

            for d in ((0, 1) if mode == "full" else (1,)):
                for h in range(4):
                    for dc in range(2):
                        bk = []
                        for lb in range(2):
                            bank, Bb = self.next_bank("g")
                            for p2 in range(2):
                                a = 2 * lb + p2
                                P.op("pe", lambda e, a=a, p2=p2: e.matmul(bank[:, :], kd(a, d)[:, h * 256 + dc * 128:h * 256 + (dc + 1) * 128], v[a][:, h * 512:(h + 1) * 512],
                                                                        start=(p2 == 0), stop=(p2 == 1)),
                                     reads=[BKD, Bv[a]], writes=[Bb])
                            bk.append((bank, Bb))
                        if mode == "full":
                            for lb in range(2):
                                k = sc[0] % 2
                                sc[0] += 1
                                self.copy("act" if lb == 0 else "dve", stg[k][:], bk[lb][0][:, :], [bk[lb][1]], [Bstg[k]])
                                P.dma("sp", lambda e, lb=lb, k=k: e.dma_start(out=st_d[2 * g + lb, d, h, dc * 128:(dc + 1) * 128, :], in_=stg[k][:]),
                                      reads=[Bstg[k]], sem_buf=Bstg[k])
                        want = (mode == "full" and g == 0 and d == 0) or (mode == "pre" and d == 1)
                        if want:
                            SX, BSX = (SF1, BSF1) if d == 0 else (SBb, BSBb)
                            far, near = (bk[0], bk[1]) if d == 0 else (bk[1], bk[0])
                            P.dma("sp", lambda e: e.dma_start(out=sxs[:], in_=stin_d[d, h, dc * 128:(dc + 1) * 128, :]), writes=[Bsxs])
                            P.op("dve", lambda e: e.tensor_scalar(sxs[:], sxs[:], cf[:, 0, 4 * d + h:4 * d + h + 1], None, ALU.mult), reads=[Bsxs, Bcf], writes=[Bsxs])
                            P.op("dve", lambda e: e.scalar_tensor_tensor(sxs[:], far[0][:, :], cf[:, 1, 4 * d + h:4 * d + h + 1], sxs[:], ALU.mult, ALU.add),
                                 reads=[far[1], Bcf, Bsxs], writes=[Bsxs])
                            P.op("dve", lambda e: e.tensor_tensor(SX[:, 2 * h + dc, :], near[0][:, :], sxs[:], ALU.add), reads=[near[1], Bsxs], writes=[BSX])
            if mode == "pre":
                return

            def g_consume(fc, bank, Bb):
                k = qc[0] % 2
                qc[0] += 1
                P.op("act", lambda e: e.activation(qr[k][:], bank[:, :], AF.Silu), reads=[Bb], writes=[Bqr[k]])
                P.op("dve", lambda e: e.scalar_tensor_tensor(QK[:, fc, :], QK[:, fc, :], gnwT[:, fc:fc + 1], qr[k][:], ALU.mult, ALU.mult),
                     reads=[BQK, BgnwT, Bqr[k]], writes=[BQK])
            gjob = self.proj_feat_job(win_d[j], 4 * D, 2 * D, 0, 512, g_consume)
            gpre = self.preload(gjob[0], "k8")

            for h in range(4):
                for jt in range(4):
                    bank, Bb = self.next_bank("g")
                    for dc in range(2):
                        P.op("pe", lambda e, dc=dc: e.matmul(bank[:, :], QK[:, 8 + 2 * h + dc, jt * 128:(jt + 1) * 128], QK[:, 2 * h + dc, :], start=(dc == 0), stop=(dc == 1)),
                             reads=[BQK], writes=[Bb])
                    w0 = 384 - 128 * jt
                    for ib in range(2):
                        c0 = ib * 256
                        Wsl = Wt[:, h, w0 + c0:w0 + c0 + 256]
                        if jt // 2 == ib:
                            P.op("dve", lambda e: e.tensor_tensor(PT(h, jt)[:, c0:c0 + 256], bank[:, c0:c0 + 256], Wsl, ALU.mult), reads=[Bb, BWt], writes=[BKD])
                        elif g < 2:
                            P.op("dve", lambda e: e.scalar_tensor_tensor(PT(h, jt)[:, c0:c0 + 256], bank[:, c0:c0 + 256], self.flag[:, 0:1], Wsl, ALU.mult, ALU.mult),
                                 reads=[Bb, BWt, self.Bflag], writes=[BKD])

            SFs, BSFs = (SF0, BSF0) if g == 0 else (SF1, BSF1)
            y3 = yb[:].rearrange("p (h c) -> p h c", h=4)

            def pv_phase(it):
                o, Bo = o2[it % 2], Bo2[it % 2]
                for h in range(4):
                    hc = slice(h * 512, (h + 1) * 512)
                    ob, Bob = self.next_bank("a")
                    jts = list(range(4)) if g < 2 else [jt for jt in range(4) if jt // 2 == it // 2]
                    for n, jt in enumerate(jts):
                        P.op("pe", lambda e, n=n, jt=jt: e.matmul(ob[:, :], PT(h, jt)[:, it * 128:(it + 1) * 128], v[jt][:, hc], start=(n == 0), stop=(n == len(jts) - 1)),
                             reads=[BKD, Bv[jt]], writes=[Bob])
                    self.copy("act", o[:, hc], ob[:, :], [Bob], [Bo])
                    if g < 2:
                        for d, S_, BS_ in ((0, SFs, BSFs), (1, SBb, BSBb)):
                            cb, Bcb = self.next_bank("b")
                            for dc in range(2):
                                P.op("pe", lambda e, dc=dc: e.matmul(cb[:, :], QK[:, 2 * h + dc, it * 128:(it + 1) * 128], S_[:, 2 * h + dc, :], start=(dc == 0), stop=(dc == 1)),
                                     reads=[BQK, BS_], writes=[Bcb])
                            P.op("dve", lambda e, d=d: e.scalar_tensor_tensor(o[:, hc], cb[:, :], dq[:, d, it, h:h + 1], o[:, hc], ALU.mult, ALU.add),
                                 reads=[Bcb, Bdq, Bo], writes=[Bo])

            def gn_phase(it):
                o, Bo = o2[it % 2], Bo2[it % 2]
                o3 = o[:].rearrange("p (h c) -> p h c", h=4)
                P.op("dve", lambda e: e.tensor_reduce(gst[:, 0:4], o3, AX.X, ALU.add), reads=[Bo], writes=[Bgst])
                for h in range(4):
                    P.op("act", lambda e, h=h: e.activation(yb[:, h * 512:(h + 1) * 512], o[:, h * 512:(h + 1) * 512], AF.Square, accum_out=gst[:, 4 + h:5 + h]),
                         reads=[Bo], writes=[Byb, Bgst2])
                P.op("dve", lambda e: e.tensor_scalar(gst[:, 0:4], gst[:, 0:4], 1.0 / 512, None, ALU.mult), reads=[Bgst], writes=[Bgst])
                P.op("dve", lambda e: e.tensor_tensor(gst[:, 8:12], gst[:, 0:4], gst[:, 0:4], ALU.mult), reads=[Bgst], writes=[Bgst])
                P.op("dve", lambda e: e.scalar_tensor_tensor(gst[:, 4:8], gst[:, 4:8], 1.0 / 512, gst[:, 8:12], ALU.mult, ALU.subtract), reads=[Bgst, Bgst2], writes=[Bgst, Bgst2])
                P.op("dve", lambda e: e.tensor_scalar(gst[:, 4:8], gst[:, 4:8], EPS, None, ALU.add), reads=[Bgst, Bgst2], writes=[Bgst, Bgst2])
                P.op("act", lambda e: e.activation(gst[:, 8:12], gst[:, 4:8], AF.Sqrt), reads=[Bgst, Bgst2], writes=[Bgst, Bgst2])
                P.op("dve", lambda e: e.reciprocal(gst[:, 8:12], gst[:, 8:12]), reads=[Bgst, Bgst2], writes=[Bgst, Bgst2])
                P.op("dve", lambda e: e.scalar_tensor_tensor(gst[:, 4:8], gst[:, 0:4], -1.0, gst[:, 8:12], ALU.mult, ALU.mult), reads=[Bgst, Bgst2], writes=[Bgst, Bgst2])
                for h in range(4):
                    P.op("act", lambda e, h=h: e.activation(yb[:, h * 512:(h + 1) * 512], o[:, h * 512:(h + 1) * 512], AF.Identity,
                                                            bias=gst[:, 4 + h:5 + h], scale=gst[:, 8 + h:9 + h]),
                         reads=[Bo, Bgst, Bgst2], writes=[Byb])

            def tr_phase(it):
                for half in range(2):
                    bank, Bb = self.next_bank("g")
                    bv = bank[:].bitcast(BF16)
                    for q in range(8):
                        kc = half * 8 + q
                        P.op("pe", lambda e, q=q, kc=kc: e.transpose(bv[:, q * 128:(q + 1) * 128], yb[:, kc * 128:(kc + 1) * 128], ident[:]),
                             reads=[Byb, self.Bident], writes=[Bb])
                    self.copy("act" if half == 0 else "dve", QK[:, half * 8:(half + 1) * 8, it * 128:(it + 1) * 128],
                              bv[:, :].rearrange("p (q c) -> p q c", c=128), [Bb], [BQK])

            pv_phase(0)
            for it in range(4):
                if it + 1 < 4:
                    pv_phase(it + 1)
                gn_phase(it)
                tr_phase(it)

            self.run_jobs([gjob, self.out_proj_job(st, wout_d[j], 16, QK[:], [BQK], tiles, "ret")], pre=gpre)

        run_group(1, "pre")
        run_group(0, "full")
        P.dma("pool", lambda e: e.dma_start(out=SBb[:], in_=stin_d[1].rearrange("h (c p) v -> p (h c) v", p=128)), writes=[BSBb])
        run_group(1, "full")
        run_group(2, "full")


def core_layout(c):
    if c < 4:
        return "real", [("s", c)] * 4 + [("p", 2 * c), ("p", 2 * c + 1)]
    base = 8 + 6 * (c - 4)
    return "pseudo", [("p", base + i) for i in range(6)]


def make_inputs(c, inp, consts):
    kind, blocks = core_layout(c)
    x = np.empty((NT * 128, D), np.float32)
    for m, (gk, b) in enumerate(blocks):
        if gk == "s":
            x[m * 256:(m + 1) * 256] = inp["x_sample"][b, m * 256:(m + 1) * 256]
        else:
            x[m * 256:(m + 1) * 256] = inp["x_prompt"][b]
    real = kind == "real"
    cond = np.stack([inp["c"][c] if real else inp["c_ctx"], inp["c_ctx"]]).astype(np.float32)
    z = np.zeros
    d = {
        "x": x, "cond": cond,
        "ctxk": inp["cache_na_k"][c].reshape(2, 256, D) if real else z((2, 256, D), np.float32),
        "ctxv": inp["cache_na_v"][c].reshape(2, 256, D) if real else z((2, 256, D), np.float32),
        "stin": inp["state_ret"][c, 0] if real else z((2, 4, 256, 512), np.float32),
        "flag": np.full((128, 1), 1.0 if real else 0.0, np.float32),
        "norm_w": inp["norm_w"], "w_ada": inp["w_ada"], "b_ada": inp["b_ada"],
        "na_w_in": inp["na_w_in"], "na_w_out": inp["na_w_out"], "na_q_gain": inp["na_q_gain"], "na_k_gain": inp["na_k_gain"],
        "nabias": consts["nabias_real"] if real else consts["nabias_zero"],
        "namask": consts["namask_real"] if real else consts["namask_pseudo"],
        "ret_w_in": inp["ret_w_in"], "ret_w_out": inp["ret_w_out"],
        "ret_decay_logit": inp["ret_decay_logit"].reshape(1, 8), "ret_gn_w": inp["ret_gn_w"],
        "rope_c": consts["rope_c"] if real else consts["rope_c1"],
        "rope_s": consts["rope_s"] if real else consts["rope_s0"],
        "rtab": consts["rtab"],
        "mlp_w_in": inp["mlp_w_in"], "mlp_ln_w": inp["mlp_ln_w"], "mlp_ln_b": inp["mlp_ln_b"],
        "mlp_w_s": inp["mlp_w_s"], "mlp_b_s": inp["mlp_b_s"], "mlp_w_out": inp["mlp_w_out"],
    }
    return {k: np.ascontiguousarray(v) for k, v in d.items()}


def make_consts(inp):
    consts = {}
    rpb = inp["na_rpb"]
    kc = np.arange(64)[:, None]
    qc = np.arange(64)[None, :]
    cidx = np.clip(kc - qc, -15, 15) + 15
    M0 = 11
    tab = np.zeros((2, 16, 128, 23, 64), np.float32)
    for m in range(23):
        for half in range(2):
            dr = M0 - m + half
            if -7 <= dr <= 7:
                tab[:, :, half * 64:(half + 1) * 64, m, :] = rpb[:, :, dr + 7][:, :, cidx]
    consts["nabias_real"] = tab.reshape(2, 16, 128, 23 * 64)
    consts["nabias_zero"] = np.zeros_like(consts["nabias_real"])
    R = 16
    r = np.arange(R)
    rstart = np.clip(r - 4, 0, R - 8)
    cq = np.arange(64)
    cstart = np.clip(cq - 8, 0, 48)
    ck = np.arange(64)
    col_in = (ck[None, :] >= cstart[:, None]) & (ck[None, :] < cstart[:, None] + 16)

    def build_mask(real):
        m = np.zeros((128, 17, 512), np.float32)
        for gi in range(2):
            for ti in range(6):
                t = ti + 2 * gi
                for e in range(2):
                    krow = 2 * t + e
                    for i in range(8):
                        qrow = 8 * gi + i
                        if real:
                            ok = rstart[qrow] <= krow < rstart[qrow] + 8
                            blk = col_in.T.astype(np.float32) if ok else 0.0
                        else:
                            blk = 1.0 if (krow // 4 == qrow // 4) else 0.0
                        m[e * 64:(e + 1) * 64, gi * 6 + ti, i * 64:(i + 1) * 64] = blk
        m[:, 16, :] = 1.0 if real else 0.0
        return m.astype(ml_dtypes.bfloat16)
    consts["namask_real"] = build_mask(True)
    consts["namask_pseudo"] = build_mask(False)
    t = np.arange(1024)
    row = (t // 64).astype(np.float32)
    col = (t % 64).astype(np.float32)
    inv = (10000.0 ** (-np.arange(0, 128, 2, dtype=np.float32) / 128)).astype(np.float32)
    ar = row[:, None] * inv[None, :]
    ac = col[:, None] * inv[None, :]
    consts["rope_c"] = np.concatenate([np.cos(ar), np.cos(ar), np.cos(ac), np.cos(ac)], 1).astype(np.float32)
    consts["rope_s"] = np.concatenate([-np.sin(ar), np.sin(ar), -np.sin(ac), np.sin(ac)], 1).astype(np.float32)
    consts["rope_c1"] = np.ones((1024, 256), np.float32)
    consts["rope_s0"] = np.zeros((1024, 256), np.float32)
    jj = np.arange(128)[:, None]
    cc = np.arange(896)[None, :]
    diff = (cc - 384) - jj
    rt = np.zeros((128, 4 * 896 + 16), np.float32)
    rt[:, 0:896] = np.maximum(diff, 0)
    rt[:, 896:1792] = np.maximum(-diff, 0)
    rt[:, 1792:2688] = (diff >= 0)
    rt[:, 2688:3584] = (diff <= 0)
    base = 3584
    for p2 in range(2):
        rt[:, base + p2] = 255 - (128 * p2 + np.arange(128))
        rt[:, base + 2 + p2] = 128 * p2 + np.arange(128)
    for it in range(4):
        rt[:, base + 4 + it] = 128 * it + np.arange(128) + 1
        rt[:, base + 8 + it] = 512 - 128 * it - np.arange(128)
    consts["rtab"] = rt
    return consts


_CACHE = {}


def kernel(**inp):
    inp = {k: np.asarray(v) for k, v in inp.items()}
    layers = _CACHE.get("layers", (0, 1, 2, 3))
    if "nc" not in _CACHE:
        _CACHE["nc"] = K(layers).build()
    nc = _CACHE["nc"]
    consts = make_consts(inp)
    in_maps = [make_inputs(c, inp, consts) for c in range(8)]
    sel = _CACHE.get("core_sel")
    if sel is None:
        res = run_bass_kernel_spmd(nc, in_maps, core_ids=list(range(8)))
        R = res.results
    else:
        res = run_bass_kernel_spmd(nc, [in_maps[c] for c in sel], core_ids=list(range(len(sel))))
        R = [res.results[sel.index(c)] if c in sel else res.results[0] for c in range(8)]
    y_p = np.empty((32, 256, D), np.float32)
    y_s = np.empty((4, 1024, D), np.float32)
    nk = np.empty((32, 2, 256, 16, 64), np.float32)
    nv = np.empty((32, 2, 256, 16, 64), np.float32)
    nr = np.empty((32, 1, 2, 4, 256, 512), np.float32)
    for c in range(8):
        kind, blocks = core_layout(c)
        r = R[c]
        for m, (gk, b) in enumerate(blocks):
            sl = slice(m * 256, (m + 1) * 256)
            if gk == "s":
                y_s[b, sl] = r["y"][sl]
            else:
                y_p[b] = r["y"][sl]
                nk[b] = r["nk"][:, sl].reshape(2, 256, 16, 64)
                nv[b] = r["nv"][:, sl].reshape(2, 256, 16, 64)
                nr[b, 0] = r["st"][m]
    return (y_p, y_s, nk, nv, nr)
```

```python
import contextlib
import numpy as np
import ml_dtypes
import concourse.bass as bass
import concourse.mybir as mybir
from concourse.bass_utils import run_bass_kernel_spmd

F32 = mybir.dt.float32
BF16 = mybir.dt.bfloat16
AF = mybir.ActivationFunctionType
ALU = mybir.AluOpType
AX = mybir.AxisListType

D = 1024
NT = 12
EPS = 1e-6
COMPUTE = ("pe", "dve", "act", "pool")
NSEM_POOL = 90


import types


def freeze(fn):
    if fn is None or fn.__closure__ is None:
        return fn
    cells = []
    for c in fn.__closure__:
        try:
            cells.append(types.CellType(c.cell_contents))
        except ValueError:
            cells.append(c)
    return types.FunctionType(fn.__code__, fn.__globals__, fn.__name__, fn.__defaults__, tuple(cells))


class Buf:
    __slots__ = ("name", "w", "rs", "dsem", "dcnt", "excl")

    def __init__(self, name):
        self.name = name
        self.excl = False
        self.w = None
        self.rs = []
        self.dsem = None
        self.dcnt = 0


class Prog:
    def __init__(self, nc, st):
        self.nc = nc
        self.ops = {e: [] for e in ("pe", "dve", "act", "pool", "sp")}
        self.seq = {e: 0 for e in self.ops}
        self.waited = {e: {} for e in self.ops}
        self.sems = {}
        for e in COMPUTE:
            self.sems["E_" + e] = st.enter_context(nc.semaphore("E_" + e))
        self.free_keys = {"pool": [], "sp": []}
        self.sem_cnt = {}
        for i in range(NSEM_POOL):
            self.sems["D%d" % i] = st.enter_context(nc.semaphore("D%d" % i))
            self.free_keys["pool" if i < 28 else "sp"].append("D%d" % i)
            self.sem_cnt["D%d" % i] = 0
        self.dma_bufs = []

    def buf(self, name):
        return Buf(name)

    def _need(self, eng, tok, waits):
        if tok is None:
            return
        sk, val = tok
        if self.waited[eng].get(sk, 0) >= val:
            return
        self.waited[eng][sk] = val
        waits.append((sk, val))

    def _deps(self, eng, reads, writes):
        waits = []
        for b in reads:
            self._need(eng, b.w, waits)
            if b.excl:
                for t in b.rs:
                    if t[0] != "E_" + eng:
                        self._need(eng, t, waits)
        for b in writes:
            self._need(eng, b.w, waits)
            for t in b.rs:
                self._need(eng, t, waits)
        return waits

    def _record(self, tok, reads, writes):
        for b in reads:
            b.rs.append(tok)
            if len(b.rs) > 64:
                best = {}
                for sk, v in b.rs:
                    if best.get(sk, 0) < v:
                        best[sk] = v
                b.rs = list(best.items())
        for b in writes:
            b.w = tok
            b.rs = []

    def op(self, eng, fn, reads=(), writes=()):
        reads = [b for b in reads if b is not None]
        writes = [b for b in writes if b is not None]
        waits = self._deps(eng, reads, writes)
        sk = "E_" + eng
        if eng == "pe":
            waits = [w for w in waits if w[0] != sk]
        self.seq[eng] += 1
        tok = (sk, self.seq[eng])
        self.ops[eng].append((waits, freeze(fn), (sk, 1)))
        self._record(tok, reads, writes)
        return tok

    def dma(self, queue, fn, reads=(), writes=(), sem_buf=None):
        reads = [b for b in reads if b is not None]
        writes = [b for b in writes if b is not None]
        waits = self._deps(queue, reads, writes)
        sb = sem_buf or (writes[0] if writes else reads[0])
        if sb.dsem is None:
            sb.dsem = {}
        if queue not in sb.dsem:
            sb.dsem[queue] = self.free_keys[queue].pop()
            self.dma_bufs.append((sb, queue))
        key = sb.dsem[queue]
        self.sem_cnt[key] += 16
        tok = (key, self.sem_cnt[key])
        self.ops[queue].append((waits, freeze(fn), (key, 16)))
        self._record(tok, reads, writes)
        return tok

    def barrier(self):
        toks = [("E_" + e, self.seq[e]) for e in COMPUTE if self.seq[e] > 0]
        toks += [(k, v) for k, v in self.sem_cnt.items() if v > 0]
        for eng in self.ops:
            waits = []
            for t in toks:
                if t[0] == "E_" + eng:
                    continue
                self._need(eng, t, waits)
            if waits:
                self.ops[eng].append((waits, None, None))
        for b, q in self.dma_bufs:
            self.free_keys[q].append(b.dsem.pop(q))
        self.dma_bufs = []

    def emit_block(self):
        nc = self.nc
        sems = self.sems
        with nc.Block() as block:
            def run(engname):
                lst = self.ops[engname]

                def body(e):
                    for waits, fn, inc in lst:
                        for sk, val in waits:
                            e.wait_ge(sems[sk], val)
                        if fn is not None:
                            fn(e).then_inc(sems[inc[0]], inc[1])
                return body

            block.tensor(run("pe"))
            block.vector(run("dve"))
            block.scalar(run("act"))
            block.gpsimd(run("pool"))
            block.sync(run("sp"))
        for e in self.ops:
            self.ops[e] = []


class K:
    def __init__(self, layers=(0, 1, 2, 3)):
        self.layers = layers
        self.nc = bass.Bass("TRN2", target_bir_lowering=False)
        self.gst = contextlib.ExitStack()
        self.P = None
        self.dram = {}

    def din(self, name, shape, dt=F32):
        t = self.nc.dram_tensor(name, list(shape), dt, kind="ExternalInput").ap()
        self.dram[name] = t
        return t

    def dout(self, name, shape, dt=F32):
        t = self.nc.dram_tensor(name, list(shape), dt, kind="ExternalOutput").ap()
        self.dram[name] = t
        return t

    def sb(self, st, name, shape, dt=F32):
        self._uid = getattr(self, "_uid", 0) + 1
        t = st.enter_context(self.nc.sbuf_tensor("s%d_%s" % (self._uid, name), list(shape), dt))
        return t, self.P.buf(name)

    def sbs(self, st, name, n, shape, dt=F32):
        ts, bs = [], []
        for i in range(n):
            t, b = self.sb(st, "%s%d" % (name, i), shape, dt)
            ts.append(t)
            bs.append(b)
        return ts, bs

    def next_bank(self, which="g"):
        lst = self.bank_sets[which]
        i = self.bank_ctr[which] % len(lst)
        self.bank_ctr[which] += 1
        return self.banks[lst[i]], self.Bbanks[lst[i]]

    def alt(self):
        self._alt = 1 - self._alt
        return "dve" if self._alt else "act"

    def copy(self, eng, out, in_, reads, writes):
        if eng == "act":
            self.P.op("act", lambda e: e.activation(out, in_, AF.Identity), reads=reads, writes=writes)
        else:
            self.P.op(eng, lambda e: e.tensor_copy(out, in_), reads=reads, writes=writes)

    def slab_load(self, src, view):
        i = self.slab_ctr % len(self.slabs)
        self.slab_ctr += 1
        t = self.slabs[i]
        if view == "k8":
            dst = t[:, :, :]
        else:
            dst = self.slabs16[i]
        shp = src.shape
        d = dst[:, 0:shp[1], 0:shp[2]]
        self.P.dma("pool", lambda e: e.dma_start(out=d, in_=src), writes=[self.Bslabs[i]])
        return dst, self.Bslabs[i]

    def stream(self, srcs, view, compute, pre=None):
        items = [x if isinstance(x, tuple) else (x, view) for x in srcs]
        loaded = list(pre) if pre else []
        depth = len(self.slabs) - 1
        n = len(items)
        while len(loaded) < min(depth, n):
            loaded.append(self.slab_load(*items[len(loaded)]))
        for i in range(n):
            if len(loaded) < n and len(loaded) <= i + depth:
                loaded.append(self.slab_load(*items[len(loaded)]))
            sl, B = loaded[i]
            compute(i, sl, B)

    def preload(self, srcs, view):
        depth = len(self.slabs) - 1
        return [self.slab_load(x, view) for x in srcs[:depth]]

    class Pending(list):
        shared = True

    @staticmethod
    def flush_pending(pending):
        while pending:
            for ent in reversed(list(pending)):
                f = ent.pop(0)
                if f is not None:
                    f()
                if not ent:
                    pending.remove(ent)

    def run_jobs(self, jobs, pre=None, pending=None):
        items, owner = [], []
        for ji, (srcs, view, comp, fin) in enumerate(jobs):
            for li, x in enumerate(srcs):
                items.append((x, view))
                owner.append((ji, li))

        def comp_all(idx, sl, B):
            ji, li = owner[idx]
            jobs[ji][2](li, sl, B)
            if li == len(jobs[ji][0]) - 1 and jobs[ji][3] is not None:
                jobs[ji][3]()
        self.stream(items, None, comp_all, pre=pre)
        if pending is not None:
            self.flush_pending(pending)

    def build(self):
        nc = self.nc
        g = self.gst
        din, dout = self.din, self.dout
        x_d = din("x", [NT * 128, D])
        cond_d = din("cond", [2, D])
        ctxk_d = din("ctxk", [2, 256, D])
        ctxv_d = din("ctxv", [2, 256, D])
        stin_d = din("stin", [2, 4, 256, 512])
        flag_d = din("flag", [128, 1])
        normw_d = din("norm_w", [4, D])
        wada_d = din("w_ada", [4, D, 3 * D])
        bada_d = din("b_ada", [4, 3 * D])
        nawin_d = din("na_w_in", [2, D, 4 * D])
        nawout_d = din("na_w_out", [2, D, D])
        naqg_d = din("na_q_gain", [2, 64])
        nakg_d = din("na_k_gain", [2, 64])
        nabias_d = din("nabias", [2, 16, 128, 23 * 64])
        namask_d = din("namask", [128, 17, 512], BF16)
        retwin_d = din("ret_w_in", [1, D, 6 * D])
        retwout_d = din("ret_w_out", [1, 2 * D, D])
        retdl_d = din("ret_decay_logit", [1, 8])
        retgn_d = din("ret_gn_w", [1, 2 * D])
        ropec_d = din("rope_c", [8 * 128, 256])
        ropes_d = din("rope_s", [8 * 128, 256])
        rtab_d = din("rtab", [128, 4 * 896 + 16])
        mlpwin_d = din("mlp_w_in", [1, D, 6 * D])
        mlplnw_d = din("mlp_ln_w", [1, 2 * D])
        mlplnb_d = din("mlp_ln_b", [1, 2 * D])
        mlpws_d = din("mlp_w_s", [1, 8, 128, 128])
        mlpbs_d = din("mlp_b_s", [1, 8, 128])
        mlpwout_d = din("mlp_w_out", [1, 2 * D, D])
        y_d = dout("y", [NT * 128, D])
        nk_d = dout("nk", [2, NT * 128, D])
        nv_d = dout("nv", [2, NT * 128, D])
        st_d = dout("st", [6, 2, 4, 256, 512])
        self.P = P = Prog(nc, g)
        self._alt = 0

        self.X, _ = self.sb(g, "X", [128, NT, D])
        self.BX = [P.buf("X%d" % t) for t in range(NT)]
        X = self.X
        self.slabs, self.Bslabs = self.sbs(g, "slab", 3, [128, 8, 512], BF16)
        self.slabs16 = [t[:].rearrange("p a (h c) -> p (a h) c", h=2) for t in self.slabs]
        self.slab_ctr = 0
        self.hT, _ = self.sb(g, "hT", [128, 8, 768], BF16)
        self.BhTs = [P.buf("hT%d" % i) for i in range(6)]
        self.ident, Bident = self.sb(g, "ident", [128, 128], BF16)
        identf, Bidentf = self.sb(g, "identf", [128, 128], F32)
        self.identf, self.Bidentf = identf, Bidentf
        self.Bident = Bident
        self.gate_bc, self.Bgate = self.sb(g, "gate_bc", [128, 2, D])
        self.amod, self.Bamod = self.sb(g, "amod", [128, 8, 2])
        self.bmod, self.Bbmod = self.sb(g, "bmod", [128, 8, 2])
        self.flag, self.Bflag = self.sb(g, "flag", [128, 1])
        self.banks = [g.enter_context(nc.psum_tensor("bank%d" % i, [128, 512], F32)) for i in range(8)]
        self.Bbanks = [P.buf("bank%d" % i) for i in range(8)]
        for b in self.Bbanks:
            b.excl = True
        self.bank_sets = {"g": [0, 1, 2, 3], "a": [4, 5], "b": [6, 7]}
        self.bank_ctr = {"g": 0, "a": 0, "b": 0}
        self.yout = P.buf("yout")

        xv = x_d.rearrange("(t p) d -> p t d", p=128)
        for t in range(NT):
            P.dma("sp", lambda e, t=t: e.dma_start(out=X[:, t, :], in_=xv[:, t, :]), writes=[self.BX[t]])
        P.dma("sp", lambda e: e.dma_start(out=self.flag[:], in_=flag_d), writes=[self.Bflag])
        P.op("dve", lambda e: e.memset(identf[:], 0.0), writes=[Bidentf])
        P.op("pool", lambda e: e.affine_select(out=identf[:], in_=identf[:], pattern=[[-1, 128]],
                                               compare_op=ALU.not_equal, fill=1.0, base=0, channel_multiplier=1),
             reads=[Bidentf], writes=[Bidentf])
        P.op("dve", lambda e: e.tensor_copy(self.ident[:], identf[:]), reads=[Bidentf], writes=[Bident])
        P.emit_block()

        for li in self.layers:
            kind = li % 3
            j = li // 3
            with contextlib.ExitStack() as st:
                self.adaln(st, li, cond_d, wada_d, bada_d, normw_d)
                P.barrier()
                P.emit_block()
            with contextlib.ExitStack() as st:
                self._nm = {}
                self._optmp = {}
                if _CACHE.get("stage") == "adaln":
                    pass
                elif kind == 2:
                    self.mlp_layer(st, j, mlpwin_d, mlplnw_d, mlplnb_d, mlpws_d, mlpbs_d, mlpwout_d)
                elif kind == 0:
                    self.na_layer(st, j, nawin_d, nawout_d, naqg_d, nakg_d, nabias_d, namask_d, ctxk_d, ctxv_d, nk_d, nv_d)
                else:
                    self.ret_layer(st, j, retwin_d, retwout_d, retdl_d, retgn_d, ropec_d, ropes_d, rtab_d, stin_d, st_d)
                P.barrier()
                P.emit_block()

        yv = y_d.rearrange("(t p) d -> p t d", p=128)
        for t in range(NT):
            P.dma("sp", lambda e, t=t: e.dma_start(out=yv[:, t, :], in_=X[:, t, :]), reads=[self.BX[t]], sem_buf=self.yout)
        P.barrier()
        P.emit_block()
        return nc

    def adaln(self, st, li, cond_d, wada_d, bada_d, normw_d):
        P = self.P
        crow, Bcrow = self.sb(st, "crow", [64, D])
        crb, Bcrb = self.sb(st, "crb", [64, D], BF16)
        scT, BscT = self.sb(st, "scT", [128, 8, 64], BF16)
        modrow, Bmodrow = self.sb(st, "modrow", [64, 3 * D])
        brow, Bbrow = self.sb(st, "brow", [64, 3 * D])
        nwT, BnwT = self.sb(st, "nwT", [128, 8])
        ones, Bones = self.sb(st, "ones64", [64, 128])
        P.op("dve", lambda e: e.memset(crow[:], 0.0), writes=[Bcrow])
        P.op("dve", lambda e: e.memset(brow[:], 0.0), writes=[Bbrow])
        P.op("dve", lambda e: e.memset(ones[:], 1.0), writes=[Bones])
        P.dma("sp", lambda e: e.dma_start(out=crow[0:1, :], in_=cond_d[0:1, :]), writes=[Bcrow], sem_buf=Bcrow)
        P.dma("sp", lambda e: e.dma_start(out=crow[32:33, :], in_=cond_d[1:2, :]), writes=[Bcrow], sem_buf=Bcrow)
        P.dma("sp", lambda e: e.dma_start(out=brow[0:1, :], in_=bada_d[li:li + 1, :]), writes=[Bbrow], sem_buf=Bbrow)
        P.dma("sp", lambda e: e.dma_start(out=brow[32:33, :], in_=bada_d[li:li + 1, :]), writes=[Bbrow], sem_buf=Bbrow)
        STOP = _CACHE.get("astop", 99)
        P.op("act", lambda e: e.activation(crb[:], crow[:], AF.Silu), reads=[Bcrow], writes=[Bcrb])
        if STOP <= 0:
            return
        P.dma("sp", lambda e: e.dma_start(out=crow[1:2, :], in_=normw_d[li:li + 1, :]), reads=[Bcrb], writes=[Bcrow], sem_buf=Bcrow)
        bank, Bb = self.next_bank("g")
        bv = bank[:].bitcast(BF16)
        for kc in range(8):
            P.op("pe", lambda e, kc=kc: e.transpose(bv[:, kc * 64:(kc + 1) * 64], crb[:, kc * 128:(kc + 1) * 128], self.ident[0:64, 0:64]),
                 reads=[Bcrb, self.Bident], writes=[Bb])
        self.copy("dve", scT[:].rearrange("p k c -> p (k c)"), bv[:, 0:512], [Bb], [BscT])
        if STOP <= 1:
            return
        bank, Bb = self.next_bank("g")
        for kc in range(8):
            P.op("pe", lambda e, kc=kc: e.transpose(bank[:, kc * 64:(kc + 1) * 64], crow[:, kc * 128:(kc + 1) * 128], self.identf[0:64, 0:64]),
                 reads=[Bcrow, self.Bidentf], writes=[Bb])
        self.copy("dve", nwT[:].unsqueeze(2), bank[:, :].rearrange("p (k c) -> p k c", c=64)[:, :, 1:2], [Bb], [BnwT])
        if STOP <= 2:
            return
        srcs = [wada_d[li][:, n0:n0 + 512].rearrange("(k p) n -> p k n", p=128) for n0 in range(0, 3 * D, 512)]

        def comp(i, s, B):
            bank, Bb = self.next_bank("g")
            for kc in range(8):
                P.op("pe", lambda e, kc=kc: e.matmul(bank[0:64, :], scT[:, kc, :], s[:, kc, :], start=(kc == 0), stop=(kc == 7)),
                     reads=[BscT, B], writes=[Bb])
            P.op("dve", lambda e: e.tensor_tensor(modrow[:, i * 512:(i + 1) * 512], bank[0:64, :], brow[:, i * 512:(i + 1) * 512], ALU.add),
                 reads=[Bb, Bbrow], writes=[Bmodrow])
        self.stream(srcs, "k8", comp)
        if STOP <= 3:
            return
        for which in range(2):
            bank, Bb = self.next_bank("g")
            for kc in range(8):
                P.op("pe", lambda e, kc=kc, which=which: e.transpose(bank[:, kc * 64:(kc + 1) * 64], modrow[:, which * D + kc * 128: which * D + (kc + 1) * 128], self.identf[0:64, 0:64]),
                     reads=[Bmodrow, self.Bidentf], writes=[Bb])
            for c in range(2):
                src = bank[:, :].rearrange("p (k q) -> p k q", q=64)[:, :, 32 * c:32 * c + 1]
                if which == 0:
                    P.op("dve", lambda e, src=src, c=c: e.tensor_copy(self.bmod[:, :, c:c + 1], src), reads=[Bb], writes=[self.Bbmod])
                else:
                    P.op("dve", lambda e, src=src, c=c: e.scalar_tensor_tensor(self.amod[:, :, c:c + 1], src, 1.0, nwT[:].unsqueeze(2), ALU.add, ALU.mult),
                         reads=[Bb, BnwT], writes=[self.Bamod])
        if STOP <= 4:
            return
        ghi, Bghi = self.sb(st, "ghi", [64, D], BF16)
        glo, Bglo = self.sb(st, "glo", [64, D], BF16)
        onesb, Bonesb = self.sb(st, "onesb", [64, 2, 128], BF16)
        P.op("dve", lambda e: e.memset(onesb[:], 0.0), writes=[Bonesb])
        P.op("dve", lambda e: e.memset(onesb[0:1, 0, :], 1.0), writes=[Bonesb])
        P.op("dve", lambda e: e.memset(onesb[32:33, 1, :], 1.0), writes=[Bonesb])
        P.op("act", lambda e: e.activation(ghi[:], modrow[:, 2 * D:3 * D], AF.Identity), reads=[Bmodrow], writes=[Bghi])
        P.op("dve", lambda e: e.tensor_tensor(glo[:], modrow[:, 2 * D:3 * D], ghi[:], ALU.subtract), reads=[Bmodrow, Bghi], writes=[Bglo])
        for c in range(2):
            for h in range(2):
                bank, Bb = self.next_bank("g")
                P.op("pe", lambda e, c=c, h=h: e.matmul(bank[:, :], onesb[:, c, :], ghi[:, h * 512:(h + 1) * 512], start=True, stop=False),
                     reads=[Bonesb, Bghi], writes=[Bb])
                P.op("pe", lambda e, c=c, h=h: e.matmul(bank[:, :], onesb[:, c, :], glo[:, h * 512:(h + 1) * 512], start=False, stop=True),
                     reads=[Bonesb, Bglo], writes=[Bb])
                self.copy("act", self.gate_bc[:, c, h * 512:(h + 1) * 512], bank[:, :], [Bb], [self.Bgate])

    def norm_mod_T(self, st, tiles, tag):
        P = self.P
        X = self.X
        if not hasattr(self, "_nm"):
            self._nm = {}
        key = id(st)
        if key not in self._nm:
            xn, Bxn = self.sbs(st, "nm_xn", 2, [128, D], BF16)
            stat, Bstat = self.sbs(st, "nm_stat", 2, [128, 2])
            self._nm[key] = (xn, Bxn, stat, Bstat)
        xn, Bxn, stat, Bstat = self._nm[key]
        def stage_a(i, t, k):
            P.op("act", lambda e: e.activation(xn[k][:], X[:, t, :], AF.Square, accum_out=stat[k][:, 0:1]),
                 reads=[self.BX[t]], writes=[Bxn[k], Bstat[k]])
            P.op("dve", lambda e: e.tensor_scalar(stat[k][:, 0:1], stat[k][:, 0:1], 1.0 / D, EPS, ALU.mult, ALU.add),
                 reads=[Bstat[k]], writes=[Bstat[k]])
            P.op("act", lambda e: e.activation(stat[k][:, 1:2], stat[k][:, 0:1], AF.Sqrt), reads=[Bstat[k]], writes=[Bstat[k]])
            P.op("dve", lambda e: e.reciprocal(stat[k][:, 1:2], stat[k][:, 1:2]), reads=[Bstat[k]], writes=[Bstat[k]])
            P.op("dve", lambda e: e.tensor_scalar(xn[k][:], X[:, t, :], stat[k][:, 1:2], None, ALU.mult),
                 reads=[self.BX[t], Bstat[k]], writes=[Bxn[k]])

        def stage_b(i, t, k):
            c = 0 if t < 8 else 1
            bank, Bb = self.next_bank("g")
            bv = bank[:].bitcast(BF16)
            for kc in range(8):
                P.op("pe", lambda e, kc=kc: e.transpose(bv[:, kc * 128:(kc + 1) * 128], xn[k][:, kc * 128:(kc + 1) * 128], self.ident[:]),
                     reads=[Bxn[k], self.Bident], writes=[Bb])
            for kc in range(8):
                eng = "act" if kc % 2 == 0 else "dve"
                dst = self.hT[:, kc, i * 128:(i + 1) * 128]
                src = bv[:, kc * 128:(kc + 1) * 128]
                if eng == "act":
                    P.op("act", lambda e, dst=dst, src=src, kc=kc: e.activation(dst, src, AF.Identity, bias=self.bmod[:, kc, c:c + 1], scale=self.amod[:, kc, c:c + 1]),
                         reads=[Bb, self.Bamod, self.Bbmod], writes=[self.BhTs[i]])
                else:
                    P.op("dve", lambda e, dst=dst, src=src, kc=kc: e.tensor_scalar(dst, src, self.amod[:, kc, c:c + 1], self.bmod[:, kc, c:c + 1], ALU.mult, ALU.add),
                         reads=[Bb, self.Bamod, self.Bbmod], writes=[self.BhTs[i]])

        n = len(tiles)
        for i, t in enumerate(tiles):
            stage_a(i, t, i % 2)
            if i > 0:
                stage_b(i - 1, tiles[i - 1], (i - 1) % 2)
        stage_b(n - 1, tiles[n - 1], (n - 1) % 2)

    def out_proj_job(self, st, wout_ap, K16, oT, BoT, tiles, tag):
        P = self.P
        X = self.X
        if id(st) not in self._optmp:
            self._optmp[id(st)] = self.sbs(st, "op_tmp" + tag, 2, [128, 512])
        tmp, Btmp = self._optmp[id(st)]
        if K16 == 16:
            srcs = [wout_ap[:, n0:n0 + 256].rearrange("(k p) n -> p k n", p=128) for n0 in range(0, D, 256)]
            W = 256
            view = "k16"
        else:
            srcs = [wout_ap[:, n0:n0 + 512].rearrange("(k p) n -> p k n", p=128) for n0 in range(0, D, 512)]
            W = 512
            view = "k8"
        cnt = [0]

        def comp(i, s, B):
            for ti, t in enumerate(tiles):
                c = 0 if t < 8 else 1
                bank, Bb = self.next_bank("g")
                for kc in range(K16):
                    P.op("pe", lambda e, kc=kc, ti=ti: e.matmul(bank[:, 0:W], oT[:, kc, ti * 128:(ti + 1) * 128], s[:, kc, 0:W], start=(kc == 0), stop=(kc == K16 - 1)),
                         reads=list(BoT) + [B], writes=[Bb])
                k = cnt[0] % 2
                cnt[0] += 1
                P.op("dve", lambda e, k=k, c=c, i=i: e.tensor_tensor(tmp[k][:, 0:W], bank[:, 0:W], self.gate_bc[:, c, i * W:(i + 1) * W], ALU.mult),
                     reads=[Bb, self.Bgate], writes=[Btmp[k]])
                P.op("dve", lambda e, k=k, t=t, i=i: e.tensor_tensor(X[:, t, i * W:(i + 1) * W], X[:, t, i * W:(i + 1) * W], tmp[k][:, 0:W], ALU.add),
                     reads=[Btmp[k], self.BX[t]], writes=[self.BX[t]])
        return (srcs, view, comp, None)

    def out_proj_srcs(self, wout_ap, K16):
        if K16 == 16:
            return [wout_ap[:, n0:n0 + 256].rearrange("(k p) n -> p k n", p=128) for n0 in range(0, D, 256)], "k16"
        return [wout_ap[:, n0:n0 + 512].rearrange("(k p) n -> p k n", p=128) for n0 in range(0, D, 512)], "k8"

    def out_proj(self, st, wout_ap, K16, oT, BoT, tiles, tag, pre=None):
        self.run_jobs([self.out_proj_job(st, wout_ap, K16, oT, BoT, tiles, tag)], pre=pre)

    def proj_feat_job(self, w_ap, col0, ncols, tcol0, ntok, consume, pending=None):
        P = self.P
        srcs = [w_ap[:, col0 + n0:col0 + n0 + 512].rearrange("(k p) n -> p k n", p=128) for n0 in range(0, ncols, 512)]

        def comp(si, s, B):
            for f4 in range(4):
                bank, Bb = self.next_bank("g")
                for kc in range(8):
                    P.op("pe", lambda e, kc=kc: e.matmul(bank[:, 0:ntok], s[:, kc, f4 * 128:(f4 + 1) * 128], self.hT[:, kc, tcol0:tcol0 + ntok], start=(kc == 0), stop=(kc == 7)),
                         reads=self.BhTs[tcol0 // 128:(tcol0 + ntok) // 128] + [B], writes=[Bb])
                if pending:
                    for ent in reversed(list(pending)):
                        f = ent.pop(0)
                        if f is not None:
                            f()
                        if not ent:
                            pending.remove(ent)
                consume(si * 4 + f4, bank, Bb)
        return (srcs, "k8", comp, None)

    def proj_feat(self, *a, **kw):
        self.run_jobs([self.proj_feat_job(*a, **kw)], pre=kw.pop("pre", None) if False else None)

    def proj_tok_job(self, w_ap, col0, ncols, ntiles, consume, t0=0, pending=None):
        P = self.P
        srcs = [w_ap[:, col0 + n0:col0 + n0 + 512].rearrange("(k p) n -> p k n", p=128) for n0 in range(0, ncols, 512)]

        if pending is None:
            pending = []
        shared = getattr(pending, "shared", False)

        def step():
            for ent in reversed(list(pending)):
                f = ent.pop(0)
                if f is not None:
                    f()
                if not ent:
                    pending.remove(ent)

        def comp(si, s, B):
            for i in range(ntiles):
                bank, Bb = self.next_bank("g")
                for kc in range(8):
                    P.op("pe", lambda e, kc=kc, i=i: e.matmul(bank[:, :], self.hT[:, kc, (t0 + i) * 128:(t0 + i + 1) * 128], s[:, kc, :], start=(kc == 0), stop=(kc == 7)),
                         reads=[self.BhTs[t0 + i], B], writes=[Bb])
                step()
                later = consume(i, si * 512, bank, Bb)
                if later is not None:
                    pending.append(list(later) if isinstance(later, (list, tuple)) else [None, later])
        def fin():
            if shared:
                return
            while pending:
                step()
        return (srcs, "k8", comp, fin)

    def proj_tok(self, *a, **kw):
        self.run_jobs([self.proj_tok_job(*a, **kw)])

    def mlp_layer(self, st, j, win_d, lnw_d, lnb_d, ws_d, bs_d, wout_d):
        P = self.P
        nc = self.nc
        W2 = 2 * D
        gu, Bgu = self.sbs(st, "gu", 4, [128, W2])
        GV, _ = self.sb(st, "GV", [128, 4 * W2])
        gv = [GV[:, i * W2:(i + 1) * W2] for i in range(4)]
        Bgv = [P.buf("gv%d" % i) for i in range(4)]
        vn, Bvn = self.sbs(st, "vn", 4, [128, W2], BF16)
        lnw, Blnw = self.sb(st, "lnw", [128, W2])
        lnb, Blnb = self.sb(st, "lnb", [128, W2])
        wsl, Bwsl = self.sb(st, "wsl", [128, 8, 128], BF16)
        wsT, BwsT = self.sb(st, "wsT", [128, 8, 128], BF16)
        bs, Bbs = self.sb(st, "bs", [128, 8])
        tmp, Btmp = self.sbs(st, "mtmp", 2, [128, 512])
        mst, Bmst = self.sbs(st, "mst", 4, [128, 8])
        Bmst2 = [P.buf("mst2_%d" % i) for i in range(4)]
        ob = [GV[:, (2 + k) * W2:(2 + k) * W2 + D].bitcast(BF16) for k in range(2)]
        Bob = [Bgv[2], Bgv[3]]
        oT = GV[:, 0:2 * W2].bitcast(BF16).rearrange("p (k c) -> p k c", c=512)
        BoT = [Bgv[0], Bgv[1]]
        P.dma("sp", lambda e: e.dma_start(out=lnw[:], in_=lnw_d[j:j + 1, :].partition_broadcast(128)), writes=[Blnw])
        P.dma("sp", lambda e: e.dma_start(out=lnb[:], in_=lnb_d[j:j + 1, :].partition_broadcast(128)), writes=[Blnb])
        P.dma("pool", lambda e: e.dma_start(out=wsl[:], in_=ws_d[j].rearrange("g i k -> i g k")), writes=[Bwsl])
        bsr, Bbsr = self.sb(st, "bsr", [64, 128])
        P.op("dve", lambda e: e.memset(bsr[:], 0.0), writes=[Bbsr])
        P.dma("sp", lambda e: e.dma_start(out=bsr[0:8, :], in_=bs_d[j]), writes=[Bbsr])
        bank, Bb = self.next_bank("g")
        P.op("pe", lambda e: e.transpose(bank[:, 0:64], bsr[:], self.identf[0:64, 0:64]), reads=[Bbsr, self.Bidentf], writes=[Bb])
        self.copy("dve", bs[:], bank[:, 0:8], [Bb], [Bbs])
        bank, Bb = self.next_bank("g")
        bv = bank[:].bitcast(BF16)
        for gq in range(8):
            P.op("pe", lambda e, gq=gq: e.transpose(bv[:, gq * 128:(gq + 1) * 128], wsl[:, gq, :], self.ident[:]), reads=[Bwsl, self.Bident], writes=[Bb])
        self.copy("dve", wsT[:].rearrange("p g i -> p (g i)"), bv[:, :], [Bb], [BwsT])

        MS = _CACHE.get("mstop", 99)
        if MS <= 0:
            return
        for grp in range(3):
            tiles = [4 * grp + i for i in range(4)]
            self.norm_mod_T(st, tiles, "mlp")
            ctr = [0]
            if MS <= 1:
                continue

            GELU = AF.Identity if _CACHE.get("noact") else AF.Gelu_apprx_tanh
            SILU = AF.Identity if _CACHE.get("noact") else AF.Silu

            def consume_v(i, c0, bank, Bb):
                P.op("act", lambda e: e.activation(gv[i][:, c0:c0 + 512], bank[:, :], GELU), reads=[Bb], writes=[Bgv[i]])

            def consume_u(i, n0, bank, Bb):
                P.op("act", lambda e: e.activation(gu[i][:, n0:n0 + 512], bank[:, :], GELU), reads=[Bb], writes=[Bgu[i]])

            def consume_g(i, c0, bank, Bb):
                k = ctr[0] % 2
                ctr[0] += 1
                P.op("act", lambda e: e.activation(tmp[k][:], bank[:, :], SILU), reads=[Bb], writes=[Btmp[k]])
                P.op("dve", lambda e: e.tensor_tensor(gu[i][:, c0:c0 + 512], gu[i][:, c0:c0 + 512], tmp[k][:], ALU.mult),
                     reads=[Btmp[k], Bgu[i]], writes=[Bgu[i]])

            self.proj_tok(win_d[j], W2, W2, 4, consume_v)
            if MS <= 2:
                continue
            ln_ops = []

            def L(eng, fn, reads, writes):
                ln_ops.append(lambda: P.op(eng, fn, reads=reads, writes=writes))
            for i in range(4):
                s = mst[i]
                Bm1, Bm2 = Bmst[i], Bmst2[i]
                L("dve", lambda e, s=s, i=i: e.reduce_sum(s[:, 4:5], gv[i], AX.X), [Bgv[i]], [Bm1])
                L("act", lambda e, s=s, i=i: e.activation(vn[i][:], gv[i], AF.Square, accum_out=s[:, 5:6]), [Bgv[i]], [Bvn[i], Bm2])
                L("dve", lambda e, s=s: e.tensor_scalar(s[:, 4:5], s[:, 4:5], 1.0 / W2, None, ALU.mult), [Bm1], [Bm1])
                L("dve", lambda e, s=s: e.tensor_tensor(s[:, 6:7], s[:, 4:5], s[:, 4:5], ALU.mult), [Bm1], [Bm1])
                L("dve", lambda e, s=s: e.scalar_tensor_tensor(s[:, 5:6], s[:, 5:6], 1.0 / W2, s[:, 6:7], ALU.mult, ALU.subtract), [Bm1, Bm2], [Bm1, Bm2])
                L("dve", lambda e, s=s: e.tensor_scalar(s[:, 5:6], s[:, 5:6], EPS, None, ALU.add), [Bm1, Bm2], [Bm1, Bm2])
                L("act", lambda e, s=s: e.activation(s[:, 6:7], s[:, 5:6], AF.Sqrt), [Bm1, Bm2], [Bm1, Bm2])
                L("dve", lambda e, s=s: e.reciprocal(s[:, 6:7], s[:, 6:7]), [Bm1, Bm2], [Bm1, Bm2])
                L("dve", lambda e, s=s: e.scalar_tensor_tensor(s[:, 7:8], s[:, 4:5], -1.0, s[:, 6:7], ALU.mult, ALU.mult), [Bm1, Bm2], [Bm1, Bm2])
                L("act", lambda e, s=s, i=i: e.activation(gv[i], gv[i], AF.Identity, bias=s[:, 7:8], scale=s[:, 6:7]), [Bgv[i], Bm1, Bm2], [Bgv[i]])
                L("dve", lambda e, i=i: e.tensor_tensor(gv[i], gv[i], lnw[:], ALU.mult), [Bgv[i], Blnw], [Bgv[i]])
                L("dve", lambda e, i=i: e.tensor_tensor(vn[i][:], gv[i], lnb[:], ALU.add), [Bgv[i], Blnb], [Bvn[i]])

            def trickle(fn):
                def wrapped(i, n0, bank, Bb):
                    fn(i, n0, bank, Bb)
                    for _ in range(2):
                        if ln_ops:
                            ln_ops.pop(0)()
                return wrapped
            self.run_jobs([self.proj_tok_job(win_d[j], 0, W2, 4, trickle(consume_u)),
                           self.proj_tok_job(win_d[j], 2 * W2, W2, 4, trickle(consume_g))])
            while ln_ops:
                ln_ops.pop(0)()
            osrcs, oview = self.out_proj_srcs(wout_d[j], 16)
            opre = self.preload(osrcs, oview)
            if MS <= 3:
                continue
            for i in range(4):
                k = i % 2
                for gp in range(4):
                    bank, Bb = self.next_bank("g")
                    for h in range(2):
                        gq = 2 * gp + h
                        P.op("pe", lambda e, gq=gq, h=h, i=i: e.matmul(bank[:, h * 256:(h + 1) * 256], wsT[:, gq, :], vn[i][:, gq * 256:(gq + 1) * 256], start=True, stop=True),
                             reads=[BwsT, Bvn[i]], writes=[Bb])
                    for h in range(2):
                        gq = 2 * gp + h
                        P.op("dve", lambda e, gq=gq, h=h, i=i, k=k: e.scalar_tensor_tensor(ob[k][:, gq * 256:(gq + 1) * 256], bank[:, h * 256:(h + 1) * 256], bs[:, gq:gq + 1],
                                                                                 gu[i][:, gq * 256:(gq + 1) * 256], ALU.add, ALU.mult),
                             reads=[Bb, Bbs, Bgu[i]], writes=[Bob[k]])
                for half in range(2):
                    bank, Bb = self.next_bank("g")
                    bv = bank[:].bitcast(BF16)
                    for q in range(8):
                        kc = half * 8 + q
                        P.op("pe", lambda e, q=q, kc=kc, k=k: e.transpose(bv[:, q * 128:(q + 1) * 128], ob[k][:, kc * 128:(kc + 1) * 128], self.ident[:]),
                             reads=[Bob[k], self.Bident], writes=[Bb])
                    self.copy("act" if half == 0 else "dve", oT[:, half * 8:(half + 1) * 8, i * 128:(i + 1) * 128],
                              bv[:, :].rearrange("p (q c) -> p q c", c=128), [Bb], BoT)
            if MS <= 4:
                continue
            self.out_proj(st, wout_d[j], 16, oT, BoT, tiles, "mlp", pre=opre)

    def na_layer(self, st, j, win_d, wout_d, qg_d, kg_d, bias_d, mask_d, ctxk_d, ctxv_d, nk_d, nv_d):
        P = self.P
        kT, BkT = self.sb(st, "kT", [128, 8, 768], BF16)
        qT, BqT = self.sb(st, "qT", [128, 8, 512], BF16)
        vt, Bvt = self.sbs(st, "vt", 6, [128, 8, 192], BF16)
        vct, Bvct = self.sbs(st, "vct", 2, [128, 8, 192], BF16)
        kcT, BkcT = self.sb(st, "kcT", [128, 8, 256], BF16)
        ckb, Bckb = self.sb(st, "ckb", [128, 2, D], BF16)
        sgT, BsgT = self.sb(st, "sgT", [128, 8, 512], BF16)
        qgn, Bqgn = self.sb(st, "qgn", [128, 64])
        kgn, Bkgn = self.sb(st, "kgn", [128, 64])
        onesel, Bones = self.sb(st, "onesel", [128, 192], BF16)
        tmp, Btmp = self.sbs(st, "na_tmp", 2, [128, 512])
        kn, Bkn = self.sbs(st, "kn", 3, [128, 512])
        knb, Bknb = self.sbs(st, "knb", 3, [128, 512], BF16)
        vst, Bvst = self.sbs(st, "vst", 2, [128, 512])
        nst, Bnst = self.sbs(st, "nst", 4, [128, 16])
        bias, Bbias = self.sbs(st, "bias", 2, [128, 23 * 64], BF16)
        mask, Bmask = self.sb(st, "mask", [128, 7, 512], BF16)
        ex, Bex = self.sbs(st, "ex", 3, [128, 512], BF16)
        pT, BpT = self.sbs(st, "pT", 3, [128, 512], BF16)
        rden, Brden = self.sb(st, "rden", [128, 512])
        ogT = self.hT[:, :, 0:512]
        ident = self.ident

        P.dma("sp", lambda e: e.dma_start(out=qgn[:], in_=qg_d[j:j + 1, :].partition_broadcast(128)), writes=[Bqgn])
        P.dma("sp", lambda e: e.dma_start(out=kgn[:], in_=kg_d[j:j + 1, :].partition_broadcast(128)), writes=[Bkgn])
        P.op("dve", lambda e: e.tensor_scalar(qgn[:], qgn[:], 0.125, None, ALU.mult), reads=[Bqgn], writes=[Bqgn])
        P.op("dve", lambda e: e.memset(onesel[:], 1.0), writes=[Bones])
        P.op("dve", lambda e: e.memset(onesel[:, 64:128], 0.0), writes=[Bones])
        for i in range(6):
            P.op("dve", lambda e, i=i: e.memset(vt[i][:], 0.0), writes=[Bvt[i]])
        for c2 in range(2):
            P.op("dve", lambda e, c2=c2: e.memset(vct[c2][:], 0.0), writes=[Bvct[c2]])
        P.dma("pool", lambda e: e.dma_start(out=ckb[:], in_=ctxk_d[j].rearrange("(c p) d -> p c d", p=128)), writes=[Bckb])
        for c2 in range(2):
            bank, Bb = self.next_bank("g")
            bv = bank[:].bitcast(BF16)
            for kc in range(8):
                P.op("pe", lambda e, kc=kc: e.transpose(bv[:, kc * 128:(kc + 1) * 128], ckb[:, c2, kc * 128:(kc + 1) * 128], ident[:]),
                     reads=[Bckb, self.Bident], writes=[Bb])
            self.copy("dve", kcT[:, :, c2 * 128:(c2 + 1) * 128], bv[:, :].rearrange("p (k c) -> p k c", c=128), [Bb], [BkcT])
            src = ctxv_d[j][c2 * 128:(c2 + 1) * 128, :].rearrange("p (a t d) -> p a t d", t=2, d=64)
            P.dma("pool", lambda e: e.dma_start(out=vct[c2][:, :, 0:64], in_=src[:, :, 0, :]), writes=[Bvct[c2]], sem_buf=Bvct[c2])
            P.dma("pool", lambda e: e.dma_start(out=vct[c2][:, :, 128:192], in_=src[:, :, 1, :]), writes=[Bvct[c2]], sem_buf=Bvct[c2])

        cn = [0]
        vc = [0]
        ec = [0]
        pc = [0]
        bc = [0]

        def qk_consumer(gain, Bgain, dstT, BdstT, dram_row0):
            def consume(i, n0, bank, Bb):
                k = cn[0] % 3
                k4 = cn[0] % 4
                k2 = cn[0] % 2
                cn[0] += 1
                b3 = bank[:, :].rearrange("p (h d) -> p h d", d=64)
                k3 = kn[k][:].rearrange("p (h d) -> p h d", d=64)
                st_ = nst[k4]
                Bst = Bnst[k4]
                P.op("act", lambda e: e.activation(tmp[k2][:], bank[:, :], AF.Square), reads=[Bb], writes=[Btmp[k2]])
                P.op("dve", lambda e: e.tensor_reduce(st_[:, 0:8], tmp[k2][:].rearrange("p (h d) -> p h d", d=64), AX.X, ALU.add),
                     reads=[Btmp[k2]], writes=[Bst])
                P.op("dve", lambda e: e.tensor_scalar(st_[:, 0:8], st_[:, 0:8], 1.0 / 64, EPS, ALU.mult, ALU.add), reads=[Bst], writes=[Bst])

                def s1():
                    P.op("act", lambda e: e.activation(st_[:, 8:16], st_[:, 0:8], AF.Sqrt), reads=[Bst], writes=[Bst])
                    P.op("dve", lambda e: e.reciprocal(st_[:, 8:16], st_[:, 8:16]), reads=[Bst], writes=[Bst])
                    P.op("dve", lambda e: e.tensor_tensor(k3, b3, st_[:, 8:16].unsqueeze(2).to_broadcast([128, 8, 64]), ALU.mult),
                         reads=[Bb, Bst], writes=[Bkn[k]])
                    r0 = dram_row0(i)
                    if r0 is not None:
                        P.op("dve", lambda e: e.tensor_tensor(k3, k3, gain[:].unsqueeze(1).to_broadcast([128, 8, 64]), ALU.mult),
                             reads=[Bkn[k], Bgain], writes=[Bkn[k]])
                        P.dma("sp", lambda e: e.dma_start(out=nk_d[j, r0:r0 + 128, n0:n0 + 512], in_=kn[k][:]), reads=[Bkn[k]], sem_buf=Bkn[k])
                    else:
                        kb3 = knb[k][:].rearrange("p (h d) -> p h d", d=64)
                        P.op("dve", lambda e: e.tensor_tensor(kb3, k3, gain[:].unsqueeze(1).to_broadcast([128, 8, 64]), ALU.mult),
                             reads=[Bkn[k], Bgain], writes=[Bknb[k]])

                def s2():
                    if dram_row0(i) is not None:
                        P.op("act", lambda e: e.activation(knb[k][:], kn[k][:], AF.Identity), reads=[Bkn[k]], writes=[Bknb[k]])

                def s3():
                    bank2, Bb2 = self.next_bank("b")
                    bv2 = bank2[:].bitcast(BF16)
                    for q in range(4):
                        P.op("pe", lambda e, q=q: e.transpose(bv2[:, q * 128:(q + 1) * 128], knb[k][:, q * 128:(q + 1) * 128], ident[:]),
                             reads=[Bknb[k], self.Bident], writes=[Bb2])
                    c0 = n0 // 128
                    self.copy("act", dstT[:, c0:c0 + 4, i * 128:(i + 1) * 128], bv2[:, 0:512].rearrange("p (q c) -> p q c", c=128), [Bb2], [BdstT])
                return [s1, s2, s3]
            return consume

        def v_consumer(dram_row0):
            def consume(i, n0, bank, Bb):
                s4 = n0 // 128
                src4 = bank[:, :].rearrange("p (a t d) -> p a t d", t=2, d=64)
                veng = "dve" if "A" in _CACHE.get("nskip", "") else "act"
                self.copy(veng, vt[i][:, s4:s4 + 4, 0:64], src4[:, :, 0, :], [Bb], [Bvt[i]])
                self.copy(veng, vt[i][:, s4:s4 + 4, 128:192], src4[:, :, 1, :], [Bb], [Bvt[i]])
                r0 = dram_row0(i)
                if r0 is not None and "D" not in _CACHE.get("nskip", ""):
                    k2 = vc[0] % 2
                    vc[0] += 1
                    P.op("dve", lambda e: e.tensor_copy(vst[k2][:], bank[:, :]), reads=[Bb], writes=[Bvst[k2]])
                    P.dma("sp", lambda e: e.dma_start(out=nv_d[j, r0:r0 + 128, n0:n0 + 512], in_=vst[k2][:]), reads=[Bvst[k2]], sem_buf=Bvst[k2])
            return consume

        def g_consume(fc, bank, Bb):
            P.op("act", lambda e: e.activation(sgT[:, fc, :], bank[:, :], AF.Silu), reads=[Bb], writes=[BsgT])

        def attend(N, qc0, keys, has_bias):
            LOOK = 3
            items = [(p, h2, key) for p in range(8) for h2 in range(2) for key in keys]
            nk_ = 2 * len(keys)
            nit = len(items)
            kbs = {}
            acc = {}
            norms = []

            def load_bias(h):
                if has_bias and h < 16 and h not in kbs:
                    kb = bc[0] % 2
                    bc[0] += 1
                    kbs[h] = kb
                    P.dma("pool", lambda e: e.dma_start(out=bias[kb][:], in_=bias_d[j, h]), writes=[Bbias[kb]])

            def emit_S(n):
                p, h2, (kTs, BkTs, kc0, vtl, Bv, m0, ms) = items[n]
                if n % len(keys) == 0:
                    load_bias(2 * p + h2)
                    load_bias(2 * p + h2 + 1)
                sbank, Bs = self.next_bank("g")
                r0 = 64 * h2
                P.op("pe", lambda e: e.matmul(sbank[:, 0:N], kTs[r0:r0 + 64, p, kc0:kc0 + 128], qT[r0:r0 + 64, p, qc0:qc0 + N], start=True, stop=(m0 is None)),
                     reads=[BkTs, BqT], writes=[Bs])
                if m0 is not None:
                    kb = kbs[2 * p + h2]
                    P.op("pe", lambda e: e.matmul(sbank[:, 0:N], ident[:], bias[kb][:, m0 * 64:(m0 + 8) * 64], start=False, stop=True),
                         reads=[self.Bident, Bbias[kb]], writes=[Bs])
                return sbank, Bs
            pend = {}
            for n in range(min(LOOK, nit)):
                pend[n] = emit_S(n)
            for n in range(nit):
                p, h2, (kTs, BkTs, kc0, vtl, Bv, m0, ms) = items[n]
                if p not in acc:
                    acc[p] = (self.next_bank("a"), self.next_bank("b"))
                (ob, Bob), (db, Bdb) = acc[p]
                sbank, Bs = pend.pop(n)
                kx = ec[0] % 3
                ec[0] += 1
                P.op("act", lambda e: e.activation(ex[kx][:, 0:N], sbank[:, 0:N], AF.Exp), reads=[Bs], writes=[Bex[kx]])
                if ms is not None:
                    kp = pc[0] % 3
                    pc[0] += 1
                    P.op("dve", lambda e: e.tensor_tensor(pT[kp][:, 0:N], ex[kx][:, 0:N], mask[:, ms, 0:N], ALU.mult),
                         reads=[Bex[kx], Bmask], writes=[BpT[kp]])
                    src, Bsrc = pT[kp], BpT[kp]
                else:
                    src, Bsrc = ex[kx], Bex[kx]
                if n + LOOK < nit:
                    pend[n + LOOK] = emit_S(n + LOOK)
                lv = vtl[:, p, 0:128] if h2 == 0 else vtl[:, p, 64:192]
                lo = onesel[:, 0:128] if h2 == 0 else onesel[:, 64:192]
                idx = n % nk_
                first, last = (idx == 0), (idx == nk_ - 1)
                P.op("pe", lambda e: e.matmul(ob[:, 0:N], lv, src[:, 0:N], start=first, stop=last), reads=[Bv, Bsrc], writes=[Bob])
                P.op("pe", lambda e: e.matmul(db[:, 0:N], lo, src[:, 0:N], start=first, stop=last), reads=[Bones, Bsrc], writes=[Bdb])
                if last:
                    def norm(p=p, ob=ob, Bob=Bob, db=db, Bdb=Bdb):
                        P.op("dve", lambda e: e.reciprocal(rden[:, 0:N], db[:, 0:N]), reads=[Bdb], writes=[Brden])
                        P.op("dve", lambda e: e.tensor_tensor(rden[:, 0:N], ob[:, 0:N], rden[:, 0:N], ALU.mult), reads=[Bob, Brden], writes=[Brden])
                        P.op("dve", lambda e: e.tensor_tensor(ogT[:, p, qc0:qc0 + N], rden[:, 0:N], sgT[:, p, qc0:qc0 + N], ALU.mult),
                             reads=[Brden, BsgT], writes=self.BhTs[qc0 // 128:(qc0 + N) // 128])
                    norms.append((n + 4, norm))
                while norms and norms[0][0] <= n:
                    norms.pop(0)[1]()
            while norms:
                norms.pop(0)[1]()

        NS = _CACHE.get("nstop", 99)
        for u in range(3):
            if u < 2:
                ltiles = [2 * u + i for i in range(6)]
                own0 = 2 * u
                own_tiles = [4 * u + i for i in range(4)]
            else:
                ltiles = [8, 9, 10, 11]
                own0 = 0
                own_tiles = ltiles
            nl = len(ltiles)
            self.norm_mod_T(st, ltiles, "na")

            def row0(i, ltiles=ltiles, own_tiles=own_tiles):
                t = ltiles[i]
                return t * 128 if t in own_tiles else None
            if NS <= 0:
                continue
            pend = self.Pending()
            self.run_jobs([
                self.proj_tok_job(win_d[j], D, D, nl, qk_consumer(kgn, Bkgn, kT, BkT, row0), pending=pend),
                self.proj_tok_job(win_d[j], 2 * D, D, nl, v_consumer(row0), pending=pend),
                self.proj_tok_job(win_d[j], 0, D, 4, qk_consumer(qgn, Bqgn, qT, BqT, lambda i: None), t0=own0, pending=pend),
                self.proj_feat_job(win_d[j], 3 * D, D, own0 * 128, 512, g_consume, pending=pend)], pending=pend)
            osrcs, oview = self.out_proj_srcs(wout_d[j], 8)
            opre = self.preload(osrcs, oview)
            if NS <= 1:
                continue
            if u < 2:
                P.dma("sp", lambda e: e.dma_start(out=mask[:, 0:6, :], in_=mask_d[:, 6 * u:6 * u + 6, :]), writes=[Bmask], sem_buf=Bmask)
                P.dma("sp", lambda e: e.dma_start(out=mask[:, 6, :], in_=mask_d[:, 16, :]), writes=[Bmask], sem_buf=Bmask)
                keys = []
                for li in range(6):
                    t = ltiles[li]
                    m0 = 11 - 2 * t + 8 * u
                    keys.append((kT, BkT, li * 128, vt[li], Bvt[li], m0, li))
                for c2 in range(2):
                    keys.append((kcT, BkcT, c2 * 128, vct[c2], Bvct[c2], None, 6))
                attend(512, 0, keys, True)
            else:
                for bb in range(2):
                    keys = [(kT, BkT, (2 * bb + q) * 128, vt[2 * bb + q], Bvt[2 * bb + q], None, None) for q in range(2)]
                    attend(256, 256 * bb, keys, False)
            if NS <= 2:
                continue
            self.out_proj(st, wout_d[j], 8, ogT, self.BhTs[0:4], own_tiles, "na", pre=opre)

    def ret_layer(self, st, j, win_d, wout_d, dl_d, gn_d, ropec_d, ropes_d, rtab_d, stin_d, st_d):
        P = self.P
        KS = 1.0 / 16.0
        Wt, BWt = self.sb(st, "Wt", [128, 4, 896], BF16)
        lg, Blg = self.sb(st, "lg", [128, 8])
        wtab, Bwtab = self.sb(st, "wtab", [128, 2, 2, 4])
        dq, Bdq = self.sb(st, "dq", [128, 2, 4, 4])
        cf, Bcf = self.sb(st, "cf", [128, 2, 8])
        gnwT, BgnwT = self.sb(st, "gnwT", [128, 16])
        gst, Bgst = self.sb(st, "gst", [128, 12])
        Bgst2 = P.buf("gst2")
        ident = self.ident

        def kd(a, d):
            return KD[:, (a * 2 + d) * 1024:(a * 2 + d + 1) * 1024]

        def PT(h, jt):
            return KD[:, (h * 4 + jt) * 512:(h * 4 + jt + 1) * 512]

        with contextlib.ExitStack() as st2:
            rt, Brt = self.sb(st2, "rt", [128, 3600])
            e1, Be1 = self.sb(st2, "e1", [128, 896])
            e2, Be2 = self.sb(st2, "e2", [128, 896])
            dl, Bdl = self.sb(st2, "dl", [128, 8])
            gr, Bgr = self.sb(st2, "gr", [64, 128])
            P.dma("sp", lambda e: e.dma_start(out=rt[:], in_=rtab_d), writes=[Brt])
            P.dma("sp", lambda e: e.dma_start(out=dl[:], in_=dl_d[0:1, :].partition_broadcast(128)), writes=[Bdl])
            P.op("act", lambda e: e.activation(lg[:], dl[:], AF.Exp, scale=-1.0), reads=[Bdl], writes=[Blg])
            P.op("dve", lambda e: e.tensor_scalar(lg[:], lg[:], 1.0, None, ALU.add), reads=[Blg], writes=[Blg])
            P.op("act", lambda e: e.activation(lg[:], lg[:], AF.Ln), reads=[Blg], writes=[Blg])
            P.op("dve", lambda e: e.tensor_scalar(lg[:], lg[:], -1.0, None, ALU.mult), reads=[Blg], writes=[Blg])
            for h in range(4):
                P.op("act", lambda e, h=h: e.activation(e1[:], rt[:, 0:896], AF.Exp, scale=lg[:, h:h + 1]), reads=[Brt, Blg], writes=[Be1])
                P.op("dve", lambda e: e.tensor_tensor(e1[:], e1[:], rt[:, 1792:2688], ALU.mult), reads=[Be1, Brt], writes=[Be1])
                P.op("act", lambda e, h=h: e.activation(e2[:], rt[:, 896:1792], AF.Exp, scale=lg[:, 4 + h:5 + h]), reads=[Brt, Blg], writes=[Be2])
                P.op("dve", lambda e: e.tensor_tensor(e2[:], e2[:], rt[:, 2688:3584], ALU.mult), reads=[Be2, Brt], writes=[Be2])
                P.op("dve", lambda e: e.tensor_tensor(e1[:], e1[:], e2[:], ALU.add), reads=[Be1, Be2], writes=[Be1])
                P.op("dve", lambda e, h=h: e.tensor_scalar(Wt[:, h, :], e1[:], KS, None, ALU.mult), reads=[Be1], writes=[BWt])
            base = 3584
            for d in range(2):
                P.op("dve", lambda e, d=d: e.tensor_tensor(wtab[:, d, :, :], rt[:, base + 2 * d:base + 2 * d + 2].unsqueeze(2).to_broadcast([128, 2, 4]),
                                                         lg[:, 4 * d:4 * d + 4].unsqueeze(1).to_broadcast([128, 2, 4]), ALU.mult),
                     reads=[Brt, Blg], writes=[Bwtab])
                P.op("dve", lambda e, d=d: e.tensor_tensor(dq[:, d, :, :], rt[:, base + 4 + 4 * d:base + 8 + 4 * d].unsqueeze(2).to_broadcast([128, 4, 4]),
                                                         lg[:, 4 * d:4 * d + 4].unsqueeze(1).to_broadcast([128, 4, 4]), ALU.mult),
                     reads=[Brt, Blg], writes=[Bdq])
            wt2 = wtab[:].rearrange("p a b c -> p (a b c)")
            dq2 = dq[:].rearrange("p a b c -> p (a b c)")
            P.op("act", lambda e: e.activation(wt2, wt2, AF.Exp), reads=[Bwtab], writes=[Bwtab])
            P.op("dve", lambda e: e.tensor_scalar(wt2, wt2, KS, None, ALU.mult), reads=[Bwtab], writes=[Bwtab])
            P.op("act", lambda e: e.activation(dq2, dq2, AF.Exp), reads=[Bdq], writes=[Bdq])
            P.op("dve", lambda e: e.tensor_scalar(dq2, dq2, self.flag[:, 0:1], None, ALU.mult), reads=[Bdq, self.Bflag], writes=[Bdq])
            P.op("act", lambda e: e.activation(cf[:, 0, :], lg[:], AF.Exp, scale=512.0), reads=[Blg], writes=[Bcf])
            P.op("act", lambda e: e.activation(cf[:, 1, :], lg[:], AF.Exp, scale=256.0), reads=[Blg], writes=[Bcf])
            P.op("dve", lambda e: e.memset(gr[:], 0.0), writes=[Bgr])
            P.dma("sp", lambda e: e.dma_start(out=gr[0:16, :], in_=gn_d[j].rearrange("(k p) -> k p", p=128)), writes=[Bgr])
            bank, Bb = self.next_bank("g")
            P.op("pe", lambda e: e.transpose(bank[:, 0:64], gr[:], self.identf[0:64, 0:64]), reads=[Bgr, self.Bidentf], writes=[Bb])
            self.copy("dve", gnwT[:], bank[:, 0:16], [Bb], [BgnwT])
            P.barrier()
            P.emit_block()

        QK, BQK = self.sb(st, "QK", [128, 16, 512], BF16)
        KD, BKD = self.sb(st, "KD", [128, 8 * 1024], BF16)
        v, Bv = self.sbs(st, "rv", 4, [128, 2048], BF16)
        o2, Bo2 = self.sbs(st, "ro", 2, [128, 2048])
        SF0, BSF0 = self.sb(st, "SF0", [128, 8, 512], BF16)
        SF1, BSF1 = self.sb(st, "SF1", [128, 8, 512], BF16)
        SBb, BSBb = self.sb(st, "SBb", [128, 8, 512], BF16)
        RC, BRC = self.sb(st, "RC", [128, 4, 256])
        RS, BRS = self.sb(st, "RS", [128, 4, 256])
        yb, Byb = RC[:].rearrange("p a c -> p (a c)").bitcast(BF16), BRC
        qr, Bqr = self.sbs(st, "qr", 2, [128, 512])
        t2, Bt2 = self.sb(st, "t2", [128, 512])
        qrb, Bqrb = self.sbs(st, "qrb", 2, [128, 512], BF16)
        stg, Bstg = qr, Bqr
        sxs, Bsxs = t2, Bt2
        nstat, Bnstat = self.sbs(st, "nm_stat", 2, [128, 2])
        self._nm[id(st)] = ([t2[:].bitcast(BF16), qr[0][:].bitcast(BF16)], [Bt2, Bqr[0]], nstat, Bnstat)
        rs2 = RS[:].rearrange("p a c -> p (a c)")
        self._optmp[id(st)] = ([rs2[:, 0:512], rs2[:, 512:1024]], [BRS, BRS])
        P.dma("pool", lambda e: e.dma_start(out=SF0[:], in_=stin_d[0].rearrange("h (c p) v -> p (h c) v", p=128)), writes=[BSF0])

        qc = [0]
        sc = [0]

        def run_group(g, mode):
            tiles = [4 * g + a for a in range(4)]
            rope = g < 2
            self.norm_mod_T(st, tiles, "ret")
            if rope and mode == "full" or (rope and mode == "pre"):
                P.dma("sp", lambda e: e.dma_start(out=RC[:], in_=ropec_d[g * 512:(g + 1) * 512, :].rearrange("(a p) c -> p a c", p=128)), writes=[BRC])
                P.dma("sp", lambda e: e.dma_start(out=RS[:], in_=ropes_d[g * 512:(g + 1) * 512, :].rearrange("(a p) c -> p a c", p=128)), writes=[BRS])

            def rope_to(dst, Bdst, bank, Bb, a):
                if not rope:
                    self.copy("act", dst, bank[:, :], [Bb], [Bdst])
                    return
                d3 = dst.rearrange("p (h c) -> p h c", h=2)
                x3 = bank[:, :].rearrange("p (h c) -> p h c", h=2)
                P.op("dve", lambda e: e.tensor_tensor(d3, x3, RC[:, a, :].unsqueeze(1).to_broadcast([128, 2, 256]), ALU.mult),
                     reads=[Bb, BRC], writes=[Bdst])
                x5 = bank[:, :].rearrange("p (h b f d) -> p h b f d", h=2, b=2, f=2, d=64)
                t5 = t2[:].rearrange("p (h b f d) -> p h b f d", h=2, b=2, f=2, d=64)
                S4 = RS[:, a, :].rearrange("p (b f d) -> p b f d", b=2, f=2, d=64)
                for hf in range(2):
                    P.op("dve", lambda e, hf=hf: e.tensor_tensor(t5[:, :, :, hf, :], x5[:, :, :, 1 - hf, :],
                                                               S4[:, :, hf, :].unsqueeze(1).to_broadcast([128, 2, 2, 64]), ALU.mult),
                         reads=[Bb, BRS], writes=[Bt2])
                P.op("dve", lambda e: e.tensor_tensor(dst, dst, t2[:], ALU.add), reads=[Bt2, Bdst], writes=[Bdst])

            def qk_consumer(chunk0, is_k):
                def consume(a, n0, bank, Bb):
                    k = qc[0] % 2
                    qc[0] += 1
                    rope_to(qr[k][:], Bqr[k], bank, Bb, a)
                    P.op("act", lambda e: e.activation(qrb[k][:], qr[k][:], AF.Identity), reads=[Bqr[k]], writes=[Bqrb[k]])
                    if is_k:
                        for hh in range(2):
                            h = n0 // 256 + hh
                            for d in range(2):
                                P.op("dve", lambda e, hh=hh, h=h, d=d: e.tensor_scalar(kd(a, d)[:, n0 + hh * 256:n0 + (hh + 1) * 256], qr[k][:, hh * 256:(hh + 1) * 256],
                                                                                     wtab[:, d, a % 2, h:h + 1], None, ALU.mult),
                                     reads=[Bqr[k], Bwtab], writes=[BKD])
                    def later():
                        bank2, Bb2 = self.next_bank("g")
                        bv2 = bank2[:].bitcast(BF16)
                        for q in range(4):
                            P.op("pe", lambda e, q=q: e.transpose(bv2[:, q * 128:(q + 1) * 128], qrb[k][:, q * 128:(q + 1) * 128], ident[:]),
                                 reads=[Bqrb[k], self.Bident], writes=[Bb2])
                        c0 = chunk0 + n0 // 128
                        self.copy("act", QK[:, c0:c0 + 4, a * 128:(a + 1) * 128], bv2[:, 0:512].rearrange("p (q c) -> p q c", c=128), [Bb2], [BQK])
                    return later
                return consume

            def v_consume(a, n0, bank, Bb):
                self.copy("act", v[a][:, n0:n0 + 512], bank[:, :], [Bb], [Bv[a]])

            jobs = []
            pend = self.Pending()
            if mode == "full":
                jobs.append(self.proj_tok_job(win_d[j], 0, D, 4, qk_consumer(0, False), pending=pend))
            jobs.append(self.proj_tok_job(win_d[j], D, D, 4, qk_consumer(8, True), pending=pend))
            jobs.append(self.proj_tok_job(win_d[j], 2 * D, 2 * D, 4, v_consume, pending=pend))
            self.run_jobs(jobs, pending=pend)

            for d in ((0, 1) if mode == "full" else (1,)):
                for h in range(4):
                    for dc in range(2):
                        bk = []
                        for lb in range(2):
                            bank, Bb = self.next_bank("g")
                            for p2 in range(2):
                                a = 2 * lb + p2
                                P.op("pe", lambda e, a=a, p2=p2: e.matmul(bank[:, :], kd(a, d)[:, h * 256 + dc * 128:h * 256 + (dc + 1) * 128], v[a][:, h * 512:(h + 1) * 512],
                                                                        start=(p2 == 0), stop=(p2 == 1)),
                                     reads=[BKD, Bv[a]], writes=[Bb])
                            bk.append((bank, Bb))
                        if mode == "full":
                            for lb in range(2):
                                k = sc[0] % 2
                                sc[0] += 1
                                self.copy("act" if lb == 0 else "dve", stg[k][:], bk[lb][0][:, :], [bk[lb][1]], [Bstg[k]])
                                P.dma("sp", lambda e, lb=lb, k=k: e.dma_start(out=st_d[2 * g + lb, d, h, dc * 128:(dc + 1) * 128, :], in_=stg[k][:]),
                                      reads=[Bstg[k]], sem_buf=Bstg[k])
                        want = (mode == "full" and g == 0 and d == 0) or (mode == "pre" and d == 1)
                        if want:
                            SX, BSX = (SF1, BSF1) if d == 0 else (SBb, BSBb)
                            far, near = (bk[0], bk[1]) if d == 0 else (bk[1], bk[0])
                            P.dma("sp", lambda e: e.dma_start(out=sxs[:], in_=stin_d[d, h, dc * 128:(dc + 1) * 128, :]), writes=[Bsxs])
                            P.op("dve", lambda e: e.tensor_scalar(sxs[:], sxs[:], cf[:, 0, 4 * d + h:4 * d + h + 1], None, ALU.mult), reads=[Bsxs, Bcf], writes=[Bsxs])
                            P.op("dve", lambda e: e.scalar_tensor_tensor(sxs[:], far[0][:, :], cf[:, 1, 4 * d + h:4 * d + h + 1], sxs[:], ALU.mult, ALU.add),
                                 reads=[far[1], Bcf, Bsxs], writes=[Bsxs])
                            P.op("dve", lambda e: e.tensor_tensor(SX[:, 2 * h + dc, :], near[0][:, :], sxs[:], ALU.add), reads=[near[1], Bsxs], writes=[BSX])
            if mode == "pre":
                return

            def g_consume(fc, bank, Bb):
                k = qc[0] % 2
                qc[0] += 1
                P.op("act", lambda e: e.activation(qr[k][:], bank[:, :], AF.Silu), reads=[Bb], writes=[Bqr[k]])
                P.op("dve", lambda e: e.scalar_tensor_tensor(QK[:, fc, :], QK[:, fc, :], gnwT[:, fc:fc + 1], qr[k][:], ALU.mult, ALU.mult),
                     reads=[BQK, BgnwT, Bqr[k]], writes=[BQK])
            gjob = self.proj_feat_job(win_d[j], 4 * D, 2 * D, 0, 512, g_consume)
            gpre = self.preload(gjob[0], "k8")

            for h in range(4):
                for jt in range(4):
                    bank, Bb = self.next_bank("g")
                    for dc in range(2):
                        P.op("pe", lambda e, dc=dc: e.matmul(bank[:, :], QK[:, 8 + 2 * h + dc, jt * 128:(jt + 1) * 128], QK[:, 2 * h + dc, :], start=(dc == 0), stop=(dc == 1)),
                             reads=[BQK], writes=[Bb])
                    w0 = 384 - 128 * jt
                    for ib in range(2):
                        c0 = ib * 256
                        Wsl = Wt[:, h, w0 + c0:w0 + c0 + 256]
                        if jt // 2 == ib:
                            P.op("dve", lambda e: e.tensor_tensor(PT(h, jt)[:, c0:c0 + 256], bank[:, c0:c0 + 256], Wsl, ALU.mult), reads=[Bb, BWt], writes=[BKD])
                        elif g < 2:
                            P.op("dve", lambda e: e.scalar_tensor_tensor(PT(h, jt)[:, c0:c0 + 256], bank[:, c0:c0 + 256], self.flag[:, 0:1], Wsl, ALU.mult, ALU.mult),
                                 reads=[Bb, BWt, self.Bflag], writes=[BKD])

            SFs, BSFs = (SF0, BSF0) if g == 0 else (SF1, BSF1)
            y3 = yb[:].rearrange("p (h c) -> p h c", h=4)

            def pv_phase(it):
                o, Bo = o2[it % 2], Bo2[it % 2]
                for h in range(4):
                    hc = slice(h * 512, (h + 1) * 512)
                    ob, Bob = self.next_bank("a")
                    jts = list(range(4)) if g < 2 else [jt for jt in range(4) if jt // 2 == it // 2]
                    for n, jt in enumerate(jts):
                        P.op("pe", lambda e, n=n, jt=jt: e.matmul(ob[:, :], PT(h, jt)[:, it * 128:(it + 1) * 128], v[jt][:, hc], start=(n == 0), stop=(n == len(jts) - 1)),
                             reads=[BKD, Bv[jt]], writes=[Bob])
                    self.copy("act", o[:, hc], ob[:, :], [Bob], [Bo])
                    if g < 2:
                        for d, S_, BS_ in ((0, SFs, BSFs), (1, SBb, BSBb)):
                            cb, Bcb = self.next_bank("b")
                            for dc in range(2):
                                P.op("pe", lambda e, dc=dc: e.matmul(cb[:, :], QK[:, 2 * h + dc, it * 128:(it + 1) * 128], S_[:, 2 * h + dc, :], start=(dc == 0), stop=(dc == 1)),
                                     reads=[BQK, BS_], writes=[Bcb])
                            P.op("dve", lambda e, d=d: e.scalar_tensor_tensor(o[:, hc], cb[:, :], dq[:, d, it, h:h + 1], o[:, hc], ALU.mult, ALU.add),
                                 reads=[Bcb, Bdq, Bo], writes=[Bo])

            def gn_phase(it):
                o, Bo = o2[it % 2], Bo2[it % 2]
                o3 = o[:].rearrange("p (h c) -> p h c", h=4)
                P.op("dve", lambda e: e.tensor_reduce(gst[:, 0:4], o3, AX.X, ALU.add), reads=[Bo], writes=[Bgst])
                for h in range(4):
                    P.op("act", lambda e, h=h: e.activation(yb[:, h * 512:(h + 1) * 512], o[:, h * 512:(h + 1) * 512], AF.Square, accum_out=gst[:, 4 + h:5 + h]),
                         reads=[Bo], writes=[Byb, Bgst2])
                P.op("dve", lambda e: e.tensor_scalar(gst[:, 0:4], gst[:, 0:4], 1.0 / 512, None, ALU.mult), reads=[Bgst], writes=[Bgst])
                P.op("dve", lambda e: e.tensor_tensor(gst[:, 8:12], gst[:, 0:4], gst[:, 0:4], ALU.mult), reads=[Bgst], writes=[Bgst])
                P.op("dve", lambda e: e.scalar_tensor_tensor(gst[:, 4:8], gst[:, 4:8], 1.0 / 512, gst[:, 8:12], ALU.mult, ALU.subtract), reads=[Bgst, Bgst2], writes=[Bgst, Bgst2])
                P.op("dve", lambda e: e.tensor_scalar(gst[:, 4:8], gst[:, 4:8], EPS, None, ALU.add), reads=[Bgst, Bgst2], writes=[Bgst, Bgst2])
                P.op("act", lambda e: e.activation(gst[:, 8:12], gst[:, 4:8], AF.Sqrt), reads=[Bgst, Bgst2], writes=[Bgst, Bgst2])
                P.op("dve", lambda e: e.reciprocal(gst[:, 8:12], gst[:, 8:12]), reads=[Bgst, Bgst2], writes=[Bgst, Bgst2])
                P.op("dve", lambda e: e.scalar_tensor_tensor(gst[:, 4:8], gst[:, 0:4], -1.0, gst[:, 8:12], ALU.mult, ALU.mult), reads=[Bgst, Bgst2], writes=[Bgst, Bgst2])
                for h in range(4):
                    P.op("act", lambda e, h=h: e.activation(yb[:, h * 512:(h + 1) * 512], o[:, h * 512:(h + 1) * 512], AF.Identity,
                                                            bias=gst[:, 4 + h:5 + h], scale=gst[:, 8 + h:9 + h]),
                         reads=[Bo, Bgst, Bgst2], writes=[Byb])

            def tr_phase(it):
                for half in range(2):
                    bank, Bb = self.next_bank("g")
                    bv = bank[:].bitcast(BF16)
                    for q in range(8):
                        kc = half * 8 + q
                        P.op("pe", lambda e, q=q, kc=kc: e.transpose(bv[:, q * 128:(q + 1) * 128], yb[:, kc * 128:(kc + 1) * 128], ident[:]),
                             reads=[Byb, self.Bident], writes=[Bb])
                    self.copy("act" if half == 0 else "dve", QK[:, half * 8:(half + 1) * 8, it * 128:(it + 1) * 128],
                              bv[:, :].rearrange("p (q c) -> p q c", c=128), [Bb], [BQK])

            pv_phase(0)
            for it in range(4):
                if it + 1 < 4:
                    pv_phase(it + 1)
                gn_phase(it)
                tr_phase(it)

            self.run_jobs([gjob, self.out_proj_job(st, wout_d[j], 16, QK[:], [BQK], tiles, "ret")], pre=gpre)

        run_group(1, "pre")
        run_group(0, "full")
        P.dma("pool", lambda e: e.dma_start(out=SBb[:], in_=stin_d[1].rearrange("h (c p) v -> p (h c) v", p=128)), writes=[BSBb])
        run_group(1, "full")
        run_group(2, "full")


def core_layout(c):
    if c < 4:
        return "real", [("s", c)] * 4 + [("p", 2 * c), ("p", 2 * c + 1)]
    base = 8 + 6 * (c - 4)
    return "pseudo", [("p", base + i) for i in range(6)]


def make_inputs(c, inp, consts):
    kind, blocks = core_layout(c)
    x = np.empty((NT * 128, D), np.float32)
    for m, (gk, b) in enumerate(blocks):
        if gk == "s":
            x[m * 256:(m + 1) * 256] = inp["x_sample"][b, m * 256:(m + 1) * 256]
        else:
            x[m * 256:(m + 1) * 256] = inp["x_prompt"][b]
    real = kind == "real"
    cond = np.stack([inp["c"][c] if real else inp["c_ctx"], inp["c_ctx"]]).astype(np.float32)
    z = np.zeros
    d = {
        "x": x, "cond": cond,
        "ctxk": inp["cache_na_k"][c].reshape(2, 256, D) if real else z((2, 256, D), np.float32),
        "ctxv": inp["cache_na_v"][c].reshape(2, 256, D) if real else z((2, 256, D), np.float32),
        "stin": inp["state_ret"][c, 0] if real else z((2, 4, 256, 512), np.float32),
        "flag": np.full((128, 1), 1.0 if real else 0.0, np.float32),
        "norm_w": inp["norm_w"], "w_ada": inp["w_ada"], "b_ada": inp["b_ada"],
        "na_w_in": inp["na_w_in"], "na_w_out": inp["na_w_out"], "na_q_gain": inp["na_q_gain"], "na_k_gain": inp["na_k_gain"],
        "nabias": consts["nabias_real"] if real else consts["nabias_zero"],
        "namask": consts["namask_real"] if real else consts["namask_pseudo"],
        "ret_w_in": inp["ret_w_in"], "ret_w_out": inp["ret_w_out"],
        "ret_decay_logit": inp["ret_decay_logit"].reshape(1, 8), "ret_gn_w": inp["ret_gn_w"],
        "rope_c": consts["rope_c"] if real else consts["rope_c1"],
        "rope_s": consts["rope_s"] if real else consts["rope_s0"],
        "rtab": consts["rtab"],
        "mlp_w_in": inp["mlp_w_in"], "mlp_ln_w": inp["mlp_ln_w"], "mlp_ln_b": inp["mlp_ln_b"],
        "mlp_w_s": inp["mlp_w_s"], "mlp_b_s": inp["mlp_b_s"], "mlp_w_out": inp["mlp_w_out"],
    }
    return {k: np.ascontiguousarray(v) for k, v in d.items()}


def make_consts(inp):
    consts = {}
    rpb = inp["na_rpb"]
    kc = np.arange(64)[:, None]
    qc = np.arange(64)[None, :]
    cidx = np.clip(kc - qc, -15, 15) + 15
    M0 = 11
    tab = np.zeros((2, 16, 128, 23, 64), np.float32)
    for m in range(23):
        for half in range(2):
            dr = M0 - m + half
            if -7 <= dr <= 7:
                tab[:, :, half * 64:(half + 1) * 64, m, :] = rpb[:, :, dr + 7][:, :, cidx]
    consts["nabias_real"] = tab.reshape(2, 16, 128, 23 * 64)
    consts["nabias_zero"] = np.zeros_like(consts["nabias_real"])
    R = 16
    r = np.arange(R)
    rstart = np.clip(r - 4, 0, R - 8)
    cq = np.arange(64)
    cstart = np.clip(cq - 8, 0, 48)
    ck = np.arange(64)
    col_in = (ck[None, :] >= cstart[:, None]) & (ck[None, :] < cstart[:, None] + 16)

    def build_mask(real):
        m = np.zeros((128, 17, 512), np.float32)
        for gi in range(2):
            for ti in range(6):
                t = ti + 2 * gi
                for e in range(2):
                    krow = 2 * t + e
                    for i in range(8):
                        qrow = 8 * gi + i
                        if real:
                            ok = rstart[qrow] <= krow < rstart[qrow] + 8
                            blk = col_in.T.astype(np.float32) if ok else 0.0
                        else:
                            blk = 1.0 if (krow // 4 == qrow // 4) else 0.0
                        m[e * 64:(e + 1) * 64, gi * 6 + ti, i * 64:(i + 1) * 64] = blk
        m[:, 16, :] = 1.0 if real else 0.0
        return m.astype(ml_dtypes.bfloat16)
    consts["namask_real"] = build_mask(True)
    consts["namask_pseudo"] = build_mask(False)
    t = np.arange(1024)
    row = (t // 64).astype(np.float32)
    col = (t % 64).astype(np.float32)
    inv = (10000.0 ** (-np.arange(0, 128, 2, dtype=np.float32) / 128)).astype(np.float32)
    ar = row[:, None] * inv[None, :]
    ac = col[:, None] * inv[None, :]
    consts["rope_c"] = np.concatenate([np.cos(ar), np.cos(ar), np.cos(ac), np.cos(ac)], 1).astype(np.float32)
    consts["rope_s"] = np.concatenate([-np.sin(ar), np.sin(ar), -np.sin(ac), np.sin(ac)], 1).astype(np.float32)
    consts["rope_c1"] = np.ones((1024, 256), np.float32)
    consts["rope_s0"] = np.zeros((1024, 256), np.float32)
    jj = np.arange(128)[:, None]
    cc = np.arange(896)[None, :]
    diff = (cc - 384) - jj
    rt = np.zeros((128, 4 * 896 + 16), np.float32)
    rt[:, 0:896] = np.maximum(diff, 0)
    rt[:, 896:1792] = np.maximum(-diff, 0)
    rt[:, 1792:2688] = (diff >= 0)
    rt[:, 2688:3584] = (diff <= 0)
    base = 3584
    for p2 in range(2):
        rt[:, base + p2] = 255 - (128 * p2 + np.arange(128))
        rt[:, base + 2 + p2] = 128 * p2 + np.arange(128)
    for it in range(4):
        rt[:, base + 4 + it] = 128 * it + np.arange(128) + 1
        rt[:, base + 8 + it] = 512 - 128 * it - np.arange(128)
    consts["rtab"] = rt
    return consts


_CACHE = {}


def kernel(**inp):
    inp = {k: np.asarray(v) for k, v in inp.items()}
    layers = _CACHE.get("layers", (0, 1, 2, 3))
    if "nc" not in _CACHE:
        _CACHE["nc"] = K(layers).build()
    nc = _CACHE["nc"]
    consts = make_consts(inp)
    in_maps = [make_inputs(c, inp, consts) for c in range(8)]
    sel = _CACHE.get("core_sel")
    if sel is None:
        res = run_bass_kernel_spmd(nc, in_maps, core_ids=list(range(8)))
        R = res.results
    else:
        res = run_bass_kernel_spmd(nc, [in_maps[c] for c in sel], core_ids=list(range(len(sel))))
        R = [res.results[sel.index(c)] if c in sel else res.results[0] for c in range(8)]
    y_p = np.empty((32, 256, D), np.float32)
    y_s = np.empty((4, 1024, D), np.float32)
    nk = np.empty((32, 2, 256, 16, 64), np.float32)
    nv = np.empty((32, 2, 256, 16, 64), np.float32)
    nr = np.empty((32, 1, 2, 4, 256, 512), np.float32)
    for c in range(8):
        kind, blocks = core_layout(c)
        r = R[c]
        for m, (gk, b) in enumerate(blocks):
            sl = slice(m * 256, (m + 1) * 256)
            if gk == "s":
                y_s[b, sl] = r["y"][sl]
            else:
                y_p[b] = r["y"][sl]
                nk[b] = r["nk"][:, sl].reshape(2, 256, 16, 64)
                nv[b] = r["nv"][:, sl].reshape(2, 256, 16, 64)
                nr[b, 0] = r["st"][m]
    return (y_p, y_s, nk, nv, nr)
```

```python
import contextlib
import numpy as np
import ml_dtypes
import concourse.bass as bass
import concourse.mybir as mybir
from concourse.bass_utils import run_bass_kernel_spmd

F32 = mybir.dt.float32
BF16 = mybir.dt.bfloat16
AF = mybir.ActivationFunctionType
ALU = mybir.AluOpType
AX = mybir.AxisListType

D = 1024
NT = 12
EPS = 1e-6
COMPUTE = ("pe", "dve", "act", "pool")
NSEM_POOL = 90


import types


def freeze(fn):
    if fn is None or fn.__closure__ is None:
        return fn
    cells = []
    for c in fn.__closure__:
        try:
            cells.append(types.CellType(c.cell_contents))
        except ValueError:
            cells.append(c)
    return types.FunctionType(fn.__code__, fn.__globals__, fn.__name__, fn.__defaults__, tuple(cells))


class Buf:
    __slots__ = ("name", "w", "rs", "dsem", "dcnt", "excl")

    def __init__(self, name):
        self.name = name
        self.excl = False
        self.w = None
        self.rs = []
        self.dsem = None
        self.dcnt = 0


class Prog:
    def __init__(self, nc, st):
        self.nc = nc
        self.ops = {e: [] for e in ("pe", "dve", "act", "pool", "sp")}
        self.seq = {e: 0 for e in self.ops}
        self.waited = {e: {} for e in self.ops}
        self.sems = {}
        for e in COMPUTE:
            self.sems["E_" + e] = st.enter_context(nc.semaphore("E_" + e))
        self.free_keys = {"pool": [], "sp": []}
        self.sem_cnt = {}
        for i in range(NSEM_POOL):
            self.sems["D%d" % i] = st.enter_context(nc.semaphore("D%d" % i))
            self.free_keys["pool" if i < 28 else "sp"].append("D%d" % i)
            self.sem_cnt["D%d" % i] = 0
        self.dma_bufs = []

    def buf(self, name):
        return Buf(name)

    def _need(self, eng, tok, waits):
        if tok is None:
            return
        sk, val = tok
        if self.waited[eng].get(sk, 0) >= val:
            return
        self.waited[eng][sk] = val
        waits.append((sk, val))

    def _deps(self, eng, reads, writes):
        waits = []
        for b in reads:
            self._need(eng, b.w, waits)
            if b.excl:
                for t in b.rs:
                    if t[0] != "E_" + eng:
                        self._need(eng, t, waits)
        for b in writes:
            self._need(eng, b.w, waits)
            for t in b.rs:
                self._need(eng, t, waits)
        return waits

    def _record(self, tok, reads, writes):
        for b in reads:
            b.rs.append(tok)
            if len(b.rs) > 64:
                best = {}
                for sk, v in b.rs:
                    if best.get(sk, 0) < v:
                        best[sk] = v
                b.rs = list(best.items())
        for b in writes:
            b.w = tok
            b.rs = []

    def op(self, eng, fn, reads=(), writes=()):
        reads = [b for b in reads if b is not None]
        writes = [b for b in writes if b is not None]
        waits = self._deps(eng, reads, writes)
        sk = "E_" + eng
        if eng == "pe":
            waits = [w for w in waits if w[0] != sk]
        self.seq[eng] += 1
        tok = (sk, self.seq[eng])
        self.ops[eng].append((waits, freeze(fn), (sk, 1)))
        self._record(tok, reads, writes)
        return tok

    def dma(self, queue, fn, reads=(), writes=(), sem_buf=None):
        reads = [b for b in reads if b is not None]
        writes = [b for b in writes if b is not None]
        waits = self._deps(queue, reads, writes)
        sb = sem_buf or (writes[0] if writes else reads[0])
        if sb.dsem is None:
            sb.dsem = {}
        if queue not in sb.dsem:
            sb.dsem[queue] = self.free_keys[queue].pop()
            self.dma_bufs.append((sb, queue))
        key = sb.dsem[queue]
        self.sem_cnt[key] += 16
        tok = (key, self.sem_cnt[key])
        self.ops[queue].append((waits, freeze(fn), (key, 16)))
        self._record(tok, reads, writes)
        return tok

    def barrier(self):
        toks = [("E_" + e, self.seq[e]) for e in COMPUTE if self.seq[e] > 0]
        toks += [(k, v) for k, v in self.sem_cnt.items() if v > 0]
        for eng in self.ops:
            waits = []
            for t in toks:
                if t[0] == "E_" + eng:
                    continue
                self._need(eng, t, waits)
            if waits:
                self.ops[eng].append((waits, None, None))
        for b, q in self.dma_bufs:
            self.free_keys[q].append(b.dsem.pop(q))
        self.dma_bufs = []

    def emit_block(self):
        nc = self.nc
        sems = self.sems
        with nc.Block() as block:
            def run(engname):
                lst = self.ops[engname]

                def body(e):
                    for waits, fn, inc in lst:
                        for sk, val in waits:
                            e.wait_ge(sems[sk], val)
                        if fn is not None:
                            fn(e).then_inc(sems[inc[0]], inc[1])
                return body

            block.tensor(run("pe"))
            block.vector(run("dve"))
            block.scalar(run("act"))
            block.gpsimd(run("pool"))
            block.sync(run("sp"))
        for e in self.ops:
            self.ops[e] = []


class K:
    def __init__(self, layers=(0, 1, 2, 3)):
        self.layers = layers
        self.nc = bass.Bass("TRN2", target_bir_lowering=False)
        self.gst = contextlib.ExitStack()
        self.P = None
        self.dram = {}

    def din(self, name, shape, dt=F32):
        t = self.nc.dram_tensor(name, list(shape), dt, kind="ExternalInput").ap()
        self.dram[name] = t
        return t

    def dout(self, name, shape, dt=F32):
        t = self.nc.dram_tensor(name, list(shape), dt, kind="ExternalOutput").ap()
        self.dram[name] = t
        return t

    def sb(self, st, name, shape, dt=F32):
        self._uid = getattr(self, "_uid", 0) + 1
        t = st.enter_context(self.nc.sbuf_tensor("s%d_%s" % (self._uid, name), list(shape), dt))
        return t, self.P.buf(name)

    def sbs(self, st, name, n, shape, dt=F32):
        ts, bs = [], []
        for i in range(n):
            t, b = self.sb(st, "%s%d" % (name, i), shape, dt)
            ts.append(t)
            bs.append(b)
        return ts, bs

    def next_bank(self, which="g"):
        lst = self.bank_sets[which]
        i = self.bank_ctr[which] % len(lst)
        self.bank_ctr[which] += 1
        return self.banks[lst[i]], self.Bbanks[lst[i]]

    def alt(self):
        self._alt = 1 - self._alt
        return "dve" if self._alt else "act"

    def copy(self, eng, out, in_, reads, writes):
        if eng == "act":
            self.P.op("act", lambda e: e.activation(out, in_, AF.Identity), reads=reads, writes=writes)
        else:
            self.P.op(eng, lambda e: e.tensor_copy(out, in_), reads=reads, writes=writes)

    def slab_load(self, src, view):
        i = self.slab_ctr % len(self.slabs)
        self.slab_ctr += 1
        t = self.slabs[i]
        if view == "k8":
            dst = t[:, :, :]
        else:
            dst = self.slabs16[i]
        shp = src.shape
        d = dst[:, 0:shp[1], 0:shp[2]]
        self.P.dma("pool", lambda e: e.dma_start(out=d, in_=src), writes=[self.Bslabs[i]])
        return dst, self.Bslabs[i]

    def stream(self, srcs, view, compute, pre=None):
        items = [x if isinstance(x, tuple) else (x, view) for x in srcs]
        loaded = list(pre) if pre else []
        depth = len(self.slabs) - 1
        n = len(items)
        while len(loaded) < min(depth, n):
            loaded.append(self.slab_load(*items[len(loaded)]))
        for i in range(n):
            if len(loaded) < n and len(loaded) <= i + depth:
                loaded.append(self.slab_load(*items[len(loaded)]))
            sl, B = loaded[i]
            compute(i, sl, B)

    def preload(self, srcs, view):
        depth = len(self.slabs) - 1
        return [self.slab_load(x, view) for x in srcs[:depth]]

    class Pending(list):
        shared = True

    @staticmethod
    def flush_pending(pending):
        while pending:
            for ent in reversed(list(pending)):
                f = ent.pop(0)
                if f is not None:
                    f()
                if not ent:
                    pending.remove(ent)

    def run_jobs(self, jobs, pre=None, pending=None):
        items, owner = [], []
        for ji, (srcs, view, comp, fin) in enumerate(jobs):
            for li, x in enumerate(srcs):
                items.append((x, view))
                owner.append((ji, li))

        def comp_all(idx, sl, B):
            ji, li = owner[idx]
            jobs[ji][2](li, sl, B)
            if li == len(jobs[ji][0]) - 1 and jobs[ji][3] is not None:
                jobs[ji][3]()
        self.stream(items, None, comp_all, pre=pre)
        if pending is not None:
            self.flush_pending(pending)

    def build(self):
        nc = self.nc
        g = self.gst
        din, dout = self.din, self.dout
        x_d = din("x", [NT * 128, D])
        cond_d = din("cond", [2, D])
        ctxk_d = din("ctxk", [2, 256, D])
        ctxv_d = din("ctxv", [2, 256, D])
        stin_d = din("stin", [2, 4, 256, 512])
        flag_d = din("flag", [128, 1])
        normw_d = din("norm_w", [4, D])
        wada_d = din("w_ada", [4, D, 3 * D])
        bada_d = din("b_ada", [4, 3 * D])
        nawin_d = din("na_w_in", [2, D, 4 * D])
        nawout_d = din("na_w_out", [2, D, D])
        naqg_d = din("na_q_gain", [2, 64])
        nakg_d = din("na_k_gain", [2, 64])
        nabias_d = din("nabias", [2, 16, 128, 23 * 64])
        namask_d = din("namask", [128, 17, 512], BF16)
        retwin_d = din("ret_w_in", [1, D, 6 * D])
        retwout_d = din("ret_w_out", [1, 2 * D, D])
        retdl_d = din("ret_decay_logit", [1, 8])
        retgn_d = din("ret_gn_w", [1, 2 * D])
        ropec_d = din("rope_c", [8 * 128, 256])
        ropes_d = din("rope_s", [8 * 128, 256])
        rtab_d = din("rtab", [128, 4 * 896 + 16])
        mlpwin_d = din("mlp_w_in", [1, D, 6 * D])
        mlplnw_d = din("mlp_ln_w", [1, 2 * D])
        mlplnb_d = din("mlp_ln_b", [1, 2 * D])
        mlpws_d = din("mlp_w_s", [1, 8, 128, 128])
        mlpbs_d = din("mlp_b_s", [1, 8, 128])
        mlpwout_d = din("mlp_w_out", [1, 2 * D, D])
        y_d = dout("y", [NT * 128, D])
        nk_d = dout("nk", [2, NT * 128, D])
        nv_d = dout("nv", [2, NT * 128, D])
        st_d = dout("st", [6, 2, 4, 256, 512])
        self.P = P = Prog(nc, g)
        self._alt = 0

        self.X, _ = self.sb(g, "X", [128, NT, D])
        self.BX = [P.buf("X%d" % t) for t in range(NT)]
        X = self.X
        self.slabs, self.Bslabs = self.sbs(g, "slab", 3, [128, 8, 512], BF16)
        self.slabs16 = [t[:].rearrange("p a (h c) -> p (a h) c", h=2) for t in self.slabs]
        self.slab_ctr = 0
        self.hT, _ = self.sb(g, "hT", [128, 8, 768], BF16)
        self.BhTs = [P.buf("hT%d" % i) for i in range(6)]
        self.ident, Bident = self.sb(g, "ident", [128, 128], BF16)
        identf, Bidentf = self.sb(g, "identf", [128, 128], F32)
        self.identf, self.Bidentf = identf, Bidentf
        self.Bident = Bident
        self.gate_bc, self.Bgate = self.sb(g, "gate_bc", [128, 2, D])
        self.amod, self.Bamod = self.sb(g, "amod", [128, 8, 2])
        self.bmod, self.Bbmod = self.sb(g, "bmod", [128, 8, 2])
        self.flag, self.Bflag = self.sb(g, "flag", [128, 1])
        self.banks = [g.enter_context(nc.psum_tensor("bank%d" % i, [128, 512], F32)) for i in range(8)]
        self.Bbanks = [P.buf("bank%d" % i) for i in range(8)]
        for b in self.Bbanks:
            b.excl = True
        self.bank_sets = {"g": [0, 1, 2, 3], "a": [4, 5], "b": [6, 7]}
        self.bank_ctr = {"g": 0, "a": 0, "b": 0}
        self.yout = P.buf("yout")

        xv = x_d.rearrange("(t p) d -> p t d", p=128)
        for t in range(NT):
            P.dma("sp", lambda e, t=t: e.dma_start(out=X[:, t, :], in_=xv[:, t, :]), writes=[self.BX[t]])
        P.dma("sp", lambda e: e.dma_start(out=self.flag[:], in_=flag_d), writes=[self.Bflag])
        P.op("dve", lambda e: e.memset(identf[:], 0.0), writes=[Bidentf])
        P.op("pool", lambda e: e.affine_select(out=identf[:], in_=identf[:], pattern=[[-1, 128]],
                                               compare_op=ALU.not_equal, fill=1.0, base=0, channel_multiplier=1),
             reads=[Bidentf], writes=[Bidentf])
        P.op("dve", lambda e: e.tensor_copy(self.ident[:], identf[:]), reads=[Bidentf], writes=[Bident])
        P.emit_block()

        for li in self.layers:
            kind = li % 3
            j = li // 3
            with contextlib.ExitStack() as st:
                self.adaln(st, li, cond_d, wada_d, bada_d, normw_d)
                P.barrier()
                P.emit_block()
            with contextlib.ExitStack() as st:
                self._nm = {}
                self._optmp = {}
                if _CACHE.get("stage") == "adaln":
                    pass
                elif kind == 2:
                    self.mlp_layer(st, j, mlpwin_d, mlplnw_d, mlplnb_d, mlpws_d, mlpbs_d, mlpwout_d)
                elif kind == 0:
                    self.na_layer(st, j, nawin_d, nawout_d, naqg_d, nakg_d, nabias_d, namask_d, ctxk_d, ctxv_d, nk_d, nv_d)
                else:
                    self.ret_layer(st, j, retwin_d, retwout_d, retdl_d, retgn_d, ropec_d, ropes_d, rtab_d, stin_d, st_d)
                P.barrier()
                P.emit_block()

        yv = y_d.rearrange("(t p) d -> p t d", p=128)
        for t in range(NT):
            P.dma("sp", lambda e, t=t: e.dma_start(out=yv[:, t, :], in_=X[:, t, :]), reads=[self.BX[t]], sem_buf=self.yout)
        P.barrier()
        P.emit_block()
        return nc

    def adaln(self, st, li, cond_d, wada_d, bada_d, normw_d):
        P = self.P
        crow, Bcrow = self.sb(st, "crow", [64, D])
        crb, Bcrb = self.sb(st, "crb", [64, D], BF16)
        scT, BscT = self.sb(st, "scT", [128, 8, 64], BF16)
        modrow, Bmodrow = self.sb(st, "modrow", [64, 3 * D])
        brow, Bbrow = self.sb(st, "brow", [64, 3 * D])
        nwT, BnwT = self.sb(st, "nwT", [128, 8])
        ones, Bones = self.sb(st, "ones64", [64, 128])
        P.op("dve", lambda e: e.memset(crow[:], 0.0), writes=[Bcrow])
        P.op("dve", lambda e: e.memset(brow[:], 0.0), writes=[Bbrow])
        P.op("dve", lambda e: e.memset(ones[:], 1.0), writes=[Bones])
        P.dma("sp", lambda e: e.dma_start(out=crow[0:1, :], in_=cond_d[0:1, :]), writes=[Bcrow], sem_buf=Bcrow)
        P.dma("sp", lambda e: e.dma_start(out=crow[32:33, :], in_=cond_d[1:2, :]), writes=[Bcrow], sem_buf=Bcrow)
        P.dma("sp", lambda e: e.dma_start(out=brow[0:1, :], in_=bada_d[li:li + 1, :]), writes=[Bbrow], sem_buf=Bbrow)
        P.dma("sp", lambda e: e.dma_start(out=brow[32:33, :], in_=bada_d[li:li + 1, :]), writes=[Bbrow], sem_buf=Bbrow)
        STOP = _CACHE.get("astop", 99)
        P.op("act", lambda e: e.activation(crb[:], crow[:], AF.Silu), reads=[Bcrow], writes=[Bcrb])
        if STOP <= 0:
            return
        P.dma("sp", lambda e: e.dma_start(out=crow[1:2, :], in_=normw_d[li:li + 1, :]), reads=[Bcrb], writes=[Bcrow], sem_buf=Bcrow)
        bank, Bb = self.next_bank("g")
        bv = bank[:].bitcast(BF16)
        for kc in range(8):
            P.op("pe", lambda e, kc=kc: e.transpose(bv[:, kc * 64:(kc + 1) * 64], crb[:, kc * 128:(kc + 1) * 128], self.ident[0:64, 0:64]),
                 reads=[Bcrb, self.Bident], writes=[Bb])
        self.copy("dve", scT[:].rearrange("p k c -> p (k c)"), bv[:, 0:512], [Bb], [BscT])
        if STOP <= 1:
            return
        bank, Bb = self.next_bank("g")
        for kc in range(8):
            P.op("pe", lambda e, kc=kc: e.transpose(bank[:, kc * 64:(kc + 1) * 64], crow[:, kc * 128:(kc + 1) * 128], self.identf[0:64, 0:64]),
                 reads=[Bcrow, self.Bidentf], writes=[Bb])
        self.copy("dve", nwT[:].unsqueeze(2), bank[:, :].rearrange("p (k c) -> p k c", c=64)[:, :, 1:2], [Bb], [BnwT])
        if STOP <= 2:
            return
        srcs = [wada_d[li][:, n0:n0 + 512].rearrange("(k p) n -> p k n", p=128) for n0 in range(0, 3 * D, 512)]

        def comp(i, s, B):
            bank, Bb = self.next_bank("g")
            for kc in range(8):
                P.op("pe", lambda e, kc=kc: e.matmul(bank[0:64, :], scT[:, kc, :], s[:, kc, :], start=(kc == 0), stop=(kc == 7)),
                     reads=[BscT, B], writes=[Bb])
            P.op("dve", lambda e: e.tensor_tensor(modrow[:, i * 512:(i + 1) * 512], bank[0:64, :], brow[:, i * 512:(i + 1) * 512], ALU.add),
                 reads=[Bb, Bbrow], writes=[Bmodrow])
        self.stream(srcs, "k8", comp)
        if STOP <= 3:
            return
        for which in range(2):
            bank, Bb = self.next_bank("g")
            for kc in range(8):
                P.op("pe", lambda e, kc=kc, which=which: e.transpose(bank[:, kc * 64:(kc + 1) * 64], modrow[:, which * D + kc * 128: which * D + (kc + 1) * 128], self.identf[0:64, 0:64]),
                     reads=[Bmodrow, self.Bidentf], writes=[Bb])
            for c in range(2):
                src = bank[:, :].rearrange("p (k q) -> p k q", q=64)[:, :, 32 * c:32 * c + 1]
                if which == 0:
                    P.op("dve", lambda e, src=src, c=c: e.tensor_copy(self.bmod[:, :, c:c + 1], src), reads=[Bb], writes=[self.Bbmod])
                else:
                    P.op("dve", lambda e, src=src, c=c: e.scalar_tensor_tensor(self.amod[:, :, c:c + 1], src, 1.0, nwT[:].unsqueeze(2), ALU.add, ALU.mult),
                         reads=[Bb, BnwT], writes=[self.Bamod])
        if STOP <= 4:
            return
        ghi, Bghi = self.sb(st, "ghi", [64, D], BF16)
        glo, Bglo = self.sb(st, "glo", [64, D], BF16)
        onesb, Bonesb = self.sb(st, "onesb", [64, 2, 128], BF16)
        P.op("dve", lambda e: e.memset(onesb[:], 0.0), writes=[Bonesb])
        P.op("dve", lambda e: e.memset(onesb[0:1, 0, :], 1.0), writes=[Bonesb])
        P.op("dve", lambda e: e.memset(onesb[32:33, 1, :], 1.0), writes=[Bonesb])
        P.op("act", lambda e: e.activation(ghi[:], modrow[:, 2 * D:3 * D], AF.Identity), reads=[Bmodrow], writes=[Bghi])
        P.op("dve", lambda e: e.tensor_tensor(glo[:], modrow[:, 2 * D:3 * D], ghi[:], ALU.subtract), reads=[Bmodrow, Bghi], writes=[Bglo])
        for c in range(2):
            for h in range(2):
                bank, Bb = self.next_bank("g")
                P.op("pe", lambda e, c=c, h=h: e.matmul(bank[:, :], onesb[:, c, :], ghi[:, h * 512:(h + 1) * 512], start=True, stop=False),
                     reads=[Bonesb, Bghi], writes=[Bb])
                P.op("pe", lambda e, c=c, h=h: e.matmul(bank[:, :], onesb[:, c, :], glo[:, h * 512:(h + 1) * 512], start=False, stop=True),
                     reads=[Bonesb, Bglo], writes=[Bb])
                self.copy("act", self.gate_bc[:, c, h * 512:(h + 1) * 512], bank[:, :], [Bb], [self.Bgate])

    def norm_mod_T(self, st, tiles, tag):
        P = self.P
        X = self.X
        if not hasattr(self, "_nm"):
            self._nm = {}
        key = id(st)
        if key not in self._nm:
            xn, Bxn = self.sbs(st, "nm_xn", 2, [128, D], BF16)
            stat, Bstat = self.sbs(st, "nm_stat", 2, [128, 2])
            self._nm[key] = (xn, Bxn, stat, Bstat)
        xn, Bxn, stat, Bstat = self._nm[key]
        def stage_a(i, t, k):
            P.op("act", lambda e: e.activation(xn[k][:], X[:, t, :], AF.Square, accum_out=stat[k][:, 0:1]),
                 reads=[self.BX[t]], writes=[Bxn[k], Bstat[k]])
            P.op("dve", lambda e: e.tensor_scalar(stat[k][:, 0:1], stat[k][:, 0:1], 1.0 / D, EPS, ALU.mult, ALU.add),
                 reads=[Bstat[k]], writes=[Bstat[k]])
            P.op("act", lambda e: e.activation(stat[k][:, 1:2], stat[k][:, 0:1], AF.Sqrt), reads=[Bstat[k]], writes=[Bstat[k]])
            P.op("dve", lambda e: e.reciprocal(stat[k][:, 1:2], stat[k][:, 1:2]), reads=[Bstat[k]], writes=[Bstat[k]])
            P.op("dve", lambda e: e.tensor_scalar(xn[k][:], X[:, t, :], stat[k][:, 1:2], None, ALU.mult),
                 reads=[self.BX[t], Bstat[k]], writes=[Bxn[k]])

        def stage_b(i, t, k):
            c = 0 if t < 8 else 1
            bank, Bb = self.next_bank("g")
            bv = bank[:].bitcast(BF16)
            for kc in range(8):
                P.op("pe", lambda e, kc=kc: e.transpose(bv[:, kc * 128:(kc + 1) * 128], xn[k][:, kc * 128:(kc + 1) * 128], self.ident[:]),
                     reads=[Bxn[k], self.Bident], writes=[Bb])
            for kc in range(8):
                eng = "act" if kc % 2 == 0 else "dve"
                dst = self.hT[:, kc, i * 128:(i + 1) * 128]
                src = bv[:, kc * 128:(kc + 1) * 128]
                if eng == "act":
                    P.op("act", lambda e, dst=dst, src=src, kc=kc: e.activation(dst, src, AF.Identity, bias=self.bmod[:, kc, c:c + 1], scale=self.amod[:, kc, c:c + 1]),
                         reads=[Bb, self.Bamod, self.Bbmod], writes=[self.BhTs[i]])
                else:
                    P.op("dve", lambda e, dst=dst, src=src, kc=kc: e.tensor_scalar(dst, src, self.amod[:, kc, c:c + 1], self.bmod[:, kc, c:c + 1], ALU.mult, ALU.add),
                         reads=[Bb, self.Bamod, self.Bbmod], writes=[self.BhTs[i]])

        n = len(tiles)
        for i, t in enumerate(tiles):
            stage_a(i, t, i % 2)
            if i > 0:
                stage_b(i - 1, tiles[i - 1], (i - 1) % 2)
        stage_b(n - 1, tiles[n - 1], (n - 1) % 2)

    def out_proj_job(self, st, wout_ap, K16, oT, BoT, tiles, tag):
        P = self.P
        X = self.X
        if id(st) not in self._optmp:
            self._optmp[id(st)] = self.sbs(st, "op_tmp" + tag, 2, [128, 512])
        tmp, Btmp = self._optmp[id(st)]
        if K16 == 16:
            srcs = [wout_ap[:, n0:n0 + 256].rearrange("(k p) n -> p k n", p=128) for n0 in range(0, D, 256)]
            W = 256
            view = "k16"
        else:
            srcs = [wout_ap[:, n0:n0 + 512].rearrange("(k p) n -> p k n", p=128) for n0 in range(0, D, 512)]
            W = 512
            view = "k8"
        cnt = [0]

        def comp(i, s, B):
            for ti, t in enumerate(tiles):
                c = 0 if t < 8 else 1
                bank, Bb = self.next_bank("g")
                for kc in range(K16):
                    P.op("pe", lambda e, kc=kc, ti=ti: e.matmul(bank[:, 0:W], oT[:, kc, ti * 128:(ti + 1) * 128], s[:, kc, 0:W], start=(kc == 0), stop=(kc == K16 - 1)),
                         reads=list(BoT) + [B], writes=[Bb])
                k = cnt[0] % 2
                cnt[0] += 1
                P.op("dve", lambda e, k=k, c=c, i=i: e.tensor_tensor(tmp[k][:, 0:W], bank[:, 0:W], self.gate_bc[:, c, i * W:(i + 1) * W], ALU.mult),
                     reads=[Bb, self.Bgate], writes=[Btmp[k]])
                P.op("dve", lambda e, k=k, t=t, i=i: e.tensor_tensor(X[:, t, i * W:(i + 1) * W], X[:, t, i * W:(i + 1) * W], tmp[k][:, 0:W], ALU.add),
                     reads=[Btmp[k], self.BX[t]], writes=[self.BX[t]])
        return (srcs, view, comp, None)

    def out_proj_srcs(self, wout_ap, K16):
        if K16 == 16:
            return [wout_ap[:, n0:n0 + 256].rearrange("(k p) n -> p k n", p=128) for n0 in range(0, D, 256)], "k16"
        return [wout_ap[:, n0:n0 + 512].rearrange("(k p) n -> p k n", p=128) for n0 in range(0, D, 512)], "k8"

    def out_proj(self, st, wout_ap, K16, oT, BoT, tiles, tag, pre=None):
        self.run_jobs([self.out_proj_job(st, wout_ap, K16, oT, BoT, tiles, tag)], pre=pre)

    def proj_feat_job(self, w_ap, col0, ncols, tcol0, ntok, consume, pending=None):
        P = self.P
        srcs = [w_ap[:, col0 + n0:col0 + n0 + 512].rearrange("(k p) n -> p k n", p=128) for n0 in range(0, ncols, 512)]

        def comp(si, s, B):
            for f4 in range(4):
                bank, Bb = self.next_bank("g")
                for kc in range(8):
                    P.op("pe", lambda e, kc=kc: e.matmul(bank[:, 0:ntok], s[:, kc, f4 * 128:(f4 + 1) * 128], self.hT[:, kc, tcol0:tcol0 + ntok], start=(kc == 0), stop=(kc == 7)),
                         reads=self.BhTs[tcol0 // 128:(tcol0 + ntok) // 128] + [B], writes=[Bb])
                if pending:
                    for ent in reversed(list(pending)):
                        f = ent.pop(0)
                        if f is not None:
                            f()
                        if not ent:
                            pending.remove(ent)
                consume(si * 4 + f4, bank, Bb)
        return (srcs, "k8", comp, None)

    def proj_feat(self, *a, **kw):
        self.run_jobs([self.proj_feat_job(*a, **kw)], pre=kw.pop("pre", None) if False else None)

    def proj_tok_job(self, w_ap, col0, ncols, ntiles, consume, t0=0, pending=None):
        P = self.P
        srcs = [w_ap[:, col0 + n0:col0 + n0 + 512].rearrange("(k p) n -> p k n", p=128) for n0 in range(0, ncols, 512)]

        if pending is None:
            pending = []
        shared = getattr(pending, "shared", False)

        def step():
            for ent in reversed(list(pending)):
                f = ent.pop(0)
                if f is not None:
                    f()
                if not ent:
                    pending.remove(ent)

        def comp(si, s, B):
            for i in range(ntiles):
                bank, Bb = self.next_bank("g")
                for kc in range(8):
                    P.op("pe", lambda e, kc=kc, i=i: e.matmul(bank[:, :], self.hT[:, kc, (t0 + i) * 128:(t0 + i + 1) * 128], s[:, kc, :], start=(kc == 0), stop=(kc == 7)),
                         reads=[self.BhTs[t0 + i], B], writes=[Bb])
                step()
                later = consume(i, si * 512, bank, Bb)
                if later is not None:
                    pending.append(list(later) if isinstance(later, (list, tuple)) else [None, later])
        def fin():
            if shared:
                return
            while pending:
                step()
        return (srcs, "k8", comp, fin)

    def proj_tok(self, *a, **kw):
        self.run_jobs([self.proj_tok_job(*a, **kw)])

    def mlp_layer(self, st, j, win_d, lnw_d, lnb_d, ws_d, bs_d, wout_d):
        P = self.P
        nc = self.nc
        W2 = 2 * D
        gu, Bgu = self.sbs(st, "gu", 4, [128, W2])
        GV, _ = self.sb(st, "GV", [128, 4 * W2])
        gv = [GV[:, i * W2:(i + 1) * W2] for i in range(4)]
        Bgv = [P.buf("gv%d" % i) for i in range(4)]
        vn, Bvn = self.sbs(st, "vn", 4, [128, W2], BF16)
        lnw, Blnw = self.sb(st, "lnw", [128, W2])
        lnb, Blnb = self.sb(st, "lnb", [128, W2])
        wsl, Bwsl = self.sb(st, "wsl", [128, 8, 128], BF16)
        wsT, BwsT = self.sb(st, "wsT", [128, 8, 128], BF16)
        bs, Bbs = self.sb(st, "bs", [128, 8])
        tmp, Btmp = self.sbs(st, "mtmp", 2, [128, 512])
        mst, Bmst = self.sbs(st, "mst", 4, [128, 8])
        Bmst2 = [P.buf("mst2_%d" % i) for i in range(4)]
        ob = [GV[:, (2 + k) * W2:(2 + k) * W2 + D].bitcast(BF16) for k in range(2)]
        Bob = [Bgv[2], Bgv[3]]
        oT = GV[:, 0:2 * W2].bitcast(BF16).rearrange("p (k c) -> p k c", c=512)
        BoT = [Bgv[0], Bgv[1]]
        P.dma("sp", lambda e: e.dma_start(out=lnw[:], in_=lnw_d[j:j + 1, :].partition_broadcast(128)), writes=[Blnw])
        P.dma("sp", lambda e: e.dma_start(out=lnb[:], in_=lnb_d[j:j + 1, :].partition_broadcast(128)), writes=[Blnb])
        P.dma("pool", lambda e: e.dma_start(out=wsl[:], in_=ws_d[j].rearrange("g i k -> i g k")), writes=[Bwsl])
        bsr, Bbsr = self.sb(st, "bsr", [64, 128])
        P.op("dve", lambda e: e.memset(bsr[:], 0.0), writes=[Bbsr])
        P.dma("sp", lambda e: e.dma_start(out=bsr[0:8, :], in_=bs_d[j]), writes=[Bbsr])
        bank, Bb = self.next_bank("g")
        P.op("pe", lambda e: e.transpose(bank[:, 0:64], bsr[:], self.identf[0:64, 0:64]), reads=[Bbsr, self.Bidentf], writes=[Bb])
        self.copy("dve", bs[:], bank[:, 0:8], [Bb], [Bbs])
        bank, Bb = self.next_bank("g")
        bv = bank[:].bitcast(BF16)
        for gq in range(8):
            P.op("pe", lambda e, gq=gq: e.transpose(bv[:, gq * 128:(gq + 1) * 128], wsl[:, gq, :], self.ident[:]), reads=[Bwsl, self.Bident], writes=[Bb])
        self.copy("dve", wsT[:].rearrange("p g i -> p (g i)"), bv[:, :], [Bb], [BwsT])

        MS = _CACHE.get("mstop", 99)
        if MS <= 0:
            return
        for grp in range(3):
            tiles = [4 * grp + i for i in range(4)]
            self.norm_mod_T(st, tiles, "mlp")
            ctr = [0]
            if MS <= 1:
                continue

            GELU = AF.Identity if _CACHE.get("noact") else AF.Gelu_apprx_tanh
            SILU = AF.Identity if _CACHE.get("noact") else AF.Silu

            def consume_v(i, c0, bank, Bb):
                P.op("act", lambda e: e.activation(gv[i][:, c0:c0 + 512], bank[:, :], GELU), reads=[Bb], writes=[Bgv[i]])

            def consume_u(i, n0, bank, Bb):
                P.op("act", lambda e: e.activation(gu[i][:, n0:n0 + 512], bank[:, :], GELU), reads=[Bb], writes=[Bgu[i]])

            def consume_g(i, c0, bank, Bb):
                k = ctr[0] % 2
                ctr[0] += 1
                P.op("act", lambda e: e.activation(tmp[k][:], bank[:, :], SILU), reads=[Bb], writes=[Btmp[k]])
                P.op("dve", lambda e: e.tensor_tensor(gu[i][:, c0:c0 + 512], gu[i][:, c0:c0 + 512], tmp[k][:], ALU.mult),
                     reads=[Btmp[k], Bgu[i]], writes=[Bgu[i]])

            self.proj_tok(win_d[j], W2, W2, 4, consume_v)
            if MS <= 2:
                continue
            ln_ops = []

            def L(eng, fn, reads, writes):
                ln_ops.append(lambda: P.op(eng, fn, reads=reads, writes=writes))
            for i in range(4):
                s = mst[i]
                Bm1, Bm2 = Bmst[i], Bmst2[i]
                L("dve", lambda e, s=s, i=i: e.reduce_sum(s[:, 4:5], gv[i], AX.X), [Bgv[i]], [Bm1])
                L("act", lambda e, s=s, i=i: e.activation(vn[i][:], gv[i], AF.Square, accum_out=s[:, 5:6]), [Bgv[i]], [Bvn[i], Bm2])
                L("dve", lambda e, s=s: e.tensor_scalar(s[:, 4:5], s[:, 4:5], 1.0 / W2, None, ALU.mult), [Bm1], [Bm1])
                L("dve", lambda e, s=s: e.tensor_tensor(s[:, 6:7], s[:, 4:5], s[:, 4:5], ALU.mult), [Bm1], [Bm1])
                L("dve", lambda e, s=s: e.scalar_tensor_tensor(s[:, 5:6], s[:, 5:6], 1.0 / W2, s[:, 6:7], ALU.mult, ALU.subtract), [Bm1, Bm2], [Bm1, Bm2])
                L("dve", lambda e, s=s: e.tensor_scalar(s[:, 5:6], s[:, 5:6], EPS, None, ALU.add), [Bm1, Bm2], [Bm1, Bm2])
                L("act", lambda e, s=s: e.activation(s[:, 6:7], s[:, 5:6], AF.Sqrt), [Bm1, Bm2], [Bm1, Bm2])
                L("dve", lambda e, s=s: e.reciprocal(s[:, 6:7], s[:, 6:7]), [Bm1, Bm2], [Bm1, Bm2])
                L("dve", lambda e, s=s: e.scalar_tensor_tensor(s[:, 7:8], s[:, 4:5], -1.0, s[:, 6:7], ALU.mult, ALU.mult), [Bm1, Bm2], [Bm1, Bm2])
                L("act", lambda e, s=s, i=i: e.activation(gv[i], gv[i], AF.Identity, bias=s[:, 7:8], scale=s[:, 6:7]), [Bgv[i], Bm1, Bm2], [Bgv[i]])
                L("dve", lambda e, i=i: e.tensor_tensor(gv[i], gv[i], lnw[:], ALU.mult), [Bgv[i], Blnw], [Bgv[i]])
                L("dve", lambda e, i=i: e.tensor_tensor(vn[i][:], gv[i], lnb[:], ALU.add), [Bgv[i], Blnb], [Bvn[i]])

            def trickle(fn):
                def wrapped(i, n0, bank, Bb):
                    fn(i, n0, bank, Bb)
                    for _ in range(2):
                        if ln_ops:
                            ln_ops.pop(0)()
                return wrapped
            self.run_jobs([self.proj_tok_job(win_d[j], 0, W2, 4, trickle(consume_u)),
                           self.proj_tok_job(win_d[j], 2 * W2, W2, 4, trickle(consume_g))])
            while ln_ops:
                ln_ops.pop(0)()
            osrcs, oview = self.out_proj_srcs(wout_d[j], 16)
            opre = self.preload(osrcs, oview)
            if MS <= 3:
                continue
            for i in range(4):
                k = i % 2
                for gp in range(4):
                    bank, Bb = self.next_bank("g")
                    for h in range(2):
                        gq = 2 * gp + h
                        P.op("pe", lambda e, gq=gq, h=h, i=i: e.matmul(bank[:, h * 256:(h + 1) * 256], wsT[:, gq, :], vn[i][:, gq * 256:(gq + 1) * 256], start=True, stop=True),
                             reads=[BwsT, Bvn[i]], writes=[Bb])
                    for h in range(2):
                        gq = 2 * gp + h
                        P.op("dve", lambda e, gq=gq, h=h, i=i, k=k: e.scalar_tensor_tensor(ob[k][:, gq * 256:(gq + 1) * 256], bank[:, h * 256:(h + 1) * 256], bs[:, gq:gq + 1],
                                                                                 gu[i][:, gq * 256:(gq + 1) * 256], ALU.add, ALU.mult),
                             reads=[Bb, Bbs, Bgu[i]], writes=[Bob[k]])
                for half in range(2):
                    bank, Bb = self.next_bank("g")
                    bv = bank[:].bitcast(BF16)
                    for q in range(8):
                        kc = half * 8 + q
                        P.op("pe", lambda e, q=q, kc=kc, k=k: e.transpose(bv[:, q * 128:(q + 1) * 128], ob[k][:, kc * 128:(kc + 1) * 128], self.ident[:]),
                             reads=[Bob[k], self.Bident], writes=[Bb])
                    self.copy("act" if half == 0 else "dve", oT[:, half * 8:(half + 1) * 8, i * 128:(i + 1) * 128],
                              bv[:, :].rearrange("p (q c) -> p q c", c=128), [Bb], BoT)
            if MS <= 4:
                continue
            self.out_proj(st, wout_d[j], 16, oT, BoT, tiles, "mlp", pre=opre)

    def na_layer(self, st, j, win_d, wout_d, qg_d, kg_d, bias_d, mask_d, ctxk_d, ctxv_d, nk_d, nv_d):
        P = self.P
        kT, BkT = self.sb(st, "kT", [128, 8, 768], BF16)
        qT, BqT = self.sb(st, "qT", [128, 8, 512], BF16)
        vt, Bvt = self.sbs(st, "vt", 6, [128, 8, 192], BF16)
        vct, Bvct = self.sbs(st, "vct", 2, [128, 8, 192], BF16)
        kcT, BkcT = self.sb(st, "kcT", [128, 8, 256], BF16)
        ckb, Bckb = self.sb(st, "ckb", [128, 2, D], BF16)
        sgT, BsgT = self.sb(st, "sgT", [128, 8, 512], BF16)
        qgn, Bqgn = self.sb(st, "qgn", [128, 64])
        kgn, Bkgn = self.sb(st, "kgn", [128, 64])
        onesel, Bones = self.sb(st, "onesel", [128, 192], BF16)
        tmp, Btmp = self.sbs(st, "na_tmp", 2, [128, 512])
        kn, Bkn = self.sbs(st, "kn", 3, [128, 512])
        knb, Bknb = self.sbs(st, "knb", 3, [128, 512], BF16)
        vst, Bvst = self.sbs(st, "vst", 2, [128, 512])
        nst, Bnst = self.sbs(st, "nst", 4, [128, 16])
        bias, Bbias = self.sbs(st, "bias", 2, [128, 23 * 64], BF16)
        mask, Bmask = self.sb(st, "mask", [128, 7, 512], BF16)
        ex, Bex = self.sbs(st, "ex", 3, [128, 512], BF16)
        pT, BpT = self.sbs(st, "pT", 3, [128, 512], BF16)
        rden, Brden = self.sb(st, "rden", [128, 512])
        ogT = self.hT[:, :, 0:512]
        ident = self.ident

        P.dma("sp", lambda e: e.dma_start(out=qgn[:], in_=qg_d[j:j + 1, :].partition_broadcast(128)), writes=[Bqgn])
        P.dma("sp", lambda e: e.dma_start(out=kgn[:], in_=kg_d[j:j + 1, :].partition_broadcast(128)), writes=[Bkgn])
        P.op("dve", lambda e: e.tensor_scalar(qgn[:], qgn[:], 0.125, None, ALU.mult), reads=[Bqgn], writes=[Bqgn])
        P.op("dve", lambda e: e.memset(onesel[:], 1.0), writes=[Bones])
        P.op("dve", lambda e: e.memset(onesel[:, 64:128], 0.0), writes=[Bones])
        for i in range(6):
            P.op("dve", lambda e, i=i: e.memset(vt[i][:], 0.0), writes=[Bvt[i]])
        for c2 in range(2):
            P.op("dve", lambda e, c2=c2: e.memset(vct[c2][:], 0.0), writes=[Bvct[c2]])
        P.dma("pool", lambda e: e.dma_start(out=ckb[:], in_=ctxk_d[j].rearrange("(c p) d -> p c d", p=128)), writes=[Bckb])
        for c2 in range(2):
            bank, Bb = self.next_bank("g")
            bv = bank[:].bitcast(BF16)
            for kc in range(8):
                P.op("pe", lambda e, kc=kc: e.transpose(bv[:, kc * 128:(kc + 1) * 128], ckb[:, c2, kc * 128:(kc + 1) * 128], ident[:]),
                     reads=[Bckb, self.Bident], writes=[Bb])
            self.copy("dve", kcT[:, :, c2 * 128:(c2 + 1) * 128], bv[:, :].rearrange("p (k c) -> p k c", c=128), [Bb], [BkcT])
            src = ctxv_d[j][c2 * 128:(c2 + 1) * 128, :].rearrange("p (a t d) -> p a t d", t=2, d=64)
            P.dma("pool", lambda e: e.dma_start(out=vct[c2][:, :, 0:64], in_=src[:, :, 0, :]), writes=[Bvct[c2]], sem_buf=Bvct[c2])
            P.dma("pool", lambda e: e.dma_start(out=vct[c2][:, :, 128:192], in_=src[:, :, 1, :]), writes=[Bvct[c2]], sem_buf=Bvct[c2])

        cn = [0]
        vc = [0]
        ec = [0]
        pc = [0]
        bc = [0]

        def qk_consumer(gain, Bgain, dstT, BdstT, dram_row0):
            def consume(i, n0, bank, Bb):
                k = cn[0] % 3
                k4 = cn[0] % 4
                k2 = cn[0] % 2
                cn[0] += 1
                b3 = bank[:, :].rearrange("p (h d) -> p h d", d=64)
                k3 = kn[k][:].rearrange("p (h d) -> p h d", d=64)
                st_ = nst[k4]
                Bst = Bnst[k4]
                P.op("act", lambda e: e.activation(tmp[k2][:], bank[:, :], AF.Square), reads=[Bb], writes=[Btmp[k2]])
                P.op("dve", lambda e: e.tensor_reduce(st_[:, 0:8], tmp[k2][:].rearrange("p (h d) -> p h d", d=64), AX.X, ALU.add),
                     reads=[Btmp[k2]], writes=[Bst])
                P.op("dve", lambda e: e.tensor_scalar(st_[:, 0:8], st_[:, 0:8], 1.0 / 64, EPS, ALU.mult, ALU.add), reads=[Bst], writes=[Bst])

                def s1():
                    P.op("act", lambda e: e.activation(st_[:, 8:16], st_[:, 0:8], AF.Sqrt), reads=[Bst], writes=[Bst])
                    P.op("dve", lambda e: e.reciprocal(st_[:, 8:16], st_[:, 8:16]), reads=[Bst], writes=[Bst])
                    P.op("dve", lambda e: e.tensor_tensor(k3, b3, st_[:, 8:16].unsqueeze(2).to_broadcast([128, 8, 64]), ALU.mult),
                         reads=[Bb, Bst], writes=[Bkn[k]])
                    r0 = dram_row0(i)
                    if r0 is not None:
                        P.op("dve", lambda e: e.tensor_tensor(k3, k3, gain[:].unsqueeze(1).to_broadcast([128, 8, 64]), ALU.mult),
                             reads=[Bkn[k], Bgain], writes=[Bkn[k]])
                        P.dma("sp", lambda e: e.dma_start(out=nk_d[j, r0:r0 + 128, n0:n0 + 512], in_=kn[k][:]), reads=[Bkn[k]], sem_buf=Bkn[k])
                    else:
                        kb3 = knb[k][:].rearrange("p (h d) -> p h d", d=64)
                        P.op("dve", lambda e: e.tensor_tensor(kb3, k3, gain[:].unsqueeze(1).to_broadcast([128, 8, 64]), ALU.mult),
                             reads=[Bkn[k], Bgain], writes=[Bknb[k]])

                def s2():
                    if dram_row0(i) is not None:
                        P.op("dve", lambda e: e.tensor_copy(knb[k][:], kn[k][:]), reads=[Bkn[k]], writes=[Bknb[k]])

                def s3():
                    bank2, Bb2 = self.next_bank("b")
                    bv2 = bank2[:].bitcast(BF16)
                    for q in range(4):
                        P.op("pe", lambda e, q=q: e.transpose(bv2[:, q * 128:(q + 1) * 128], knb[k][:, q * 128:(q + 1) * 128], ident[:]),
                             reads=[Bknb[k], self.Bident], writes=[Bb2])
                    c0 = n0 // 128
                    self.copy("act", dstT[:, c0:c0 + 4, i * 128:(i + 1) * 128], bv2[:, 0:512].rearrange("p (q c) -> p q c", c=128), [Bb2], [BdstT])
                return [s1, s2, s3]
            return consume

        def v_consumer(dram_row0):
            def consume(i, n0, bank, Bb):
                s4 = n0 // 128
                src4 = bank[:, :].rearrange("p (a t d) -> p a t d", t=2, d=64)
                veng = "dve" if "A" in _CACHE.get("nskip", "") else "act"
                self.copy(veng, vt[i][:, s4:s4 + 4, 0:64], src4[:, :, 0, :], [Bb], [Bvt[i]])
                self.copy(veng, vt[i][:, s4:s4 + 4, 128:192], src4[:, :, 1, :], [Bb], [Bvt[i]])
                r0 = dram_row0(i)
                if r0 is not None and "D" not in _CACHE.get("nskip", ""):
                    k2 = vc[0] % 2
                    vc[0] += 1
                    P.op("dve", lambda e: e.tensor_copy(vst[k2][:], bank[:, :]), reads=[Bb], writes=[Bvst[k2]])
                    P.dma("sp", lambda e: e.dma_start(out=nv_d[j, r0:r0 + 128, n0:n0 + 512], in_=vst[k2][:]), reads=[Bvst[k2]], sem_buf=Bvst[k2])
            return consume

        def g_consume(fc, bank, Bb):
            P.op("act", lambda e: e.activation(sgT[:, fc, :], bank[:, :], AF.Silu), reads=[Bb], writes=[BsgT])

        def attend(N, qc0, keys, has_bias):
            LOOK = 3
            items = [(p, h2, key) for p in range(8) for h2 in range(2) for key in keys]
            nk_ = 2 * len(keys)
            nit = len(items)
            kbs = {}
            acc = {}
            norms = []

            def load_bias(h):
                if has_bias and h < 16 and h not in kbs:
                    kb = bc[0] % 2
                    bc[0] += 1
                    kbs[h] = kb
                    P.dma("pool", lambda e: e.dma_start(out=bias[kb][:], in_=bias_d[j, h]), writes=[Bbias[kb]])

            def emit_S(n):
                p, h2, (kTs, BkTs, kc0, vtl, Bv, m0, ms) = items[n]
                if n % len(keys) == 0:
                    load_bias(2 * p + h2)
                    load_bias(2 * p + h2 + 1)
                sbank, Bs = self.next_bank("g")
                r0 = 64 * h2
                P.op("pe", lambda e: e.matmul(sbank[:, 0:N], kTs[r0:r0 + 64, p, kc0:kc0 + 128], qT[r0:r0 + 64, p, qc0:qc0 + N], start=True, stop=(m0 is None)),
                     reads=[BkTs, BqT], writes=[Bs])
                if m0 is not None:
                    kb = kbs[2 * p + h2]
                    P.op("pe", lambda e: e.matmul(sbank[:, 0:N], ident[:], bias[kb][:, m0 * 64:(m0 + 8) * 64], start=False, stop=True),
                         reads=[self.Bident, Bbias[kb]], writes=[Bs])
                return sbank, Bs
            pend = {}
            for n in range(min(LOOK, nit)):
                pend[n] = emit_S(n)
            for n in range(nit):
                p, h2, (kTs, BkTs, kc0, vtl, Bv, m0, ms) = items[n]
                if p not in acc:
                    acc[p] = (self.next_bank("a"), self.next_bank("b"))
                (ob, Bob), (db, Bdb) = acc[p]
                sbank, Bs = pend.pop(n)
                kx = ec[0] % 3
                ec[0] += 1
                P.op("act", lambda e: e.activation(ex[kx][:, 0:N], sbank[:, 0:N], AF.Exp), reads=[Bs], writes=[Bex[kx]])
                if ms is not None:
                    kp = pc[0] % 3
                    pc[0] += 1
                    P.op("dve", lambda e: e.tensor_tensor(pT[kp][:, 0:N], ex[kx][:, 0:N], mask[:, ms, 0:N], ALU.mult),
                         reads=[Bex[kx], Bmask], writes=[BpT[kp]])
                    src, Bsrc = pT[kp], BpT[kp]
                else:
                    src, Bsrc = ex[kx], Bex[kx]
                if n + LOOK < nit:
                    pend[n + LOOK] = emit_S(n + LOOK)
                lv = vtl[:, p, 0:128] if h2 == 0 else vtl[:, p, 64:192]
                lo = onesel[:, 0:128] if h2 == 0 else onesel[:, 64:192]
                idx = n % nk_
                first, last = (idx == 0), (idx == nk_ - 1)
                P.op("pe", lambda e: e.matmul(ob[:, 0:N], lv, src[:, 0:N], start=first, stop=last), reads=[Bv, Bsrc], writes=[Bob])
                P.op("pe", lambda e: e.matmul(db[:, 0:N], lo, src[:, 0:N], start=first, stop=last), reads=[Bones, Bsrc], writes=[Bdb])
                if last:
                    def norm(p=p, ob=ob, Bob=Bob, db=db, Bdb=Bdb):
                        P.op("dve", lambda e: e.reciprocal(rden[:, 0:N], db[:, 0:N]), reads=[Bdb], writes=[Brden])
                        P.op("dve", lambda e: e.tensor_tensor(rden[:, 0:N], ob[:, 0:N], rden[:, 0:N], ALU.mult), reads=[Bob, Brden], writes=[Brden])
                        P.op("dve", lambda e: e.tensor_tensor(ogT[:, p, qc0:qc0 + N], rden[:, 0:N], sgT[:, p, qc0:qc0 + N], ALU.mult),
                             reads=[Brden, BsgT], writes=self.BhTs[qc0 // 128:(qc0 + N) // 128])
                    norms.append((n + 4, norm))
                while norms and norms[0][0] <= n:
                    norms.pop(0)[1]()
            while norms:
                norms.pop(0)[1]()

        NS = _CACHE.get("nstop", 99)
        for u in range(3):
            if u < 2:
                ltiles = [2 * u + i for i in range(6)]
                own0 = 2 * u
                own_tiles = [4 * u + i for i in range(4)]
            else:
                ltiles = [8, 9, 10, 11]
                own0 = 0
                own_tiles = ltiles
            nl = len(ltiles)
            self.norm_mod_T(st, ltiles, "na")

            def row0(i, ltiles=ltiles, own_tiles=own_tiles):
                t = ltiles[i]
                return t * 128 if t in own_tiles else None
            if NS <= 0:
                continue
            pend = self.Pending()
            self.run_jobs([
                self.proj_tok_job(win_d[j], D, D, nl, qk_consumer(kgn, Bkgn, kT, BkT, row0), pending=pend),
                self.proj_tok_job(win_d[j], 2 * D, D, nl, v_consumer(row0), pending=pend),
                self.proj_tok_job(win_d[j], 0, D, 4, qk_consumer(qgn, Bqgn, qT, BqT, lambda i: None), t0=own0, pending=pend),
                self.proj_feat_job(win_d[j], 3 * D, D, own0 * 128, 512, g_consume, pending=pend)], pending=pend)
            osrcs, oview = self.out_proj_srcs(wout_d[j], 8)
            opre = self.preload(osrcs, oview)
            if NS <= 1:
                continue
            if u < 2:
                P.dma("sp", lambda e: e.dma_start(out=mask[:, 0:6, :], in_=mask_d[:, 6 * u:6 * u + 6, :]), writes=[Bmask], sem_buf=Bmask)
                P.dma("sp", lambda e: e.dma_start(out=mask[:, 6, :], in_=mask_d[:, 16, :]), writes=[Bmask], sem_buf=Bmask)
                keys = []
                for li in range(6):
                    t = ltiles[li]
                    m0 = 11 - 2 * t + 8 * u
                    keys.append((kT, BkT, li * 128, vt[li], Bvt[li], m0, li))
                for c2 in range(2):
                    keys.append((kcT, BkcT, c2 * 128, vct[c2], Bvct[c2], None, 6))
                attend(512, 0, keys, True)
            else:
                for bb in range(2):
                    keys = [(kT, BkT, (2 * bb + q) * 128, vt[2 * bb + q], Bvt[2 * bb + q], None, None) for q in range(2)]
                    attend(256, 256 * bb, keys, False)
            if NS <= 2:
                continue
            self.out_proj(st, wout_d[j], 8, ogT, self.BhTs[0:4], own_tiles, "na", pre=opre)

    def ret_layer(self, st, j, win_d, wout_d, dl_d, gn_d, ropec_d, ropes_d, rtab_d, stin_d, st_d):
        P = self.P
        KS = 1.0 / 16.0
        Wt, BWt = self.sb(st, "Wt", [128, 4, 896], BF16)
        lg, Blg = self.sb(st, "lg", [128, 8])
        wtab, Bwtab = self.sb(st, "wtab", [128, 2, 2, 4])
        dq, Bdq = self.sb(st, "dq", [128, 2, 4, 4])
        cf, Bcf = self.sb(st, "cf", [128, 2, 8])
        gnwT, BgnwT = self.sb(st, "gnwT", [128, 16])
        gst, Bgst = self.sb(st, "gst", [128, 12])
        Bgst2 = P.buf("gst2")
        ident = self.ident

        def kd(a, d):
            return KD[:, (a * 2 + d) * 1024:(a * 2 + d + 1) * 1024]

        def PT(h, jt):
            return KD[:, (h * 4 + jt) * 512:(h * 4 + jt + 1) * 512]

        with contextlib.ExitStack() as st2:
            rt, Brt = self.sb(st2, "rt", [128, 3600])
            e1, Be1 = self.sb(st2, "e1", [128, 896])
            e2, Be2 = self.sb(st2, "e2", [128, 896])
            dl, Bdl = self.sb(st2, "dl", [128, 8])
            gr, Bgr = self.sb(st2, "gr", [64, 128])
            P.dma("sp", lambda e: e.dma_start(out=rt[:], in_=rtab_d), writes=[Brt])
            P.dma("sp", lambda e: e.dma_start(out=dl[:], in_=dl_d[0:1, :].partition_broadcast(128)), writes=[Bdl])
            P.op("act", lambda e: e.activation(lg[:], dl[:], AF.Exp, scale=-1.0), reads=[Bdl], writes=[Blg])
            P.op("dve", lambda e: e.tensor_scalar(lg[:], lg[:], 1.0, None, ALU.add), reads=[Blg], writes=[Blg])
            P.op("act", lambda e: e.activation(lg[:], lg[:], AF.Ln), reads=[Blg], writes=[Blg])
            P.op("dve", lambda e: e.tensor_scalar(lg[:], lg[:], -1.0, None, ALU.mult), reads=[Blg], writes=[Blg])
            for h in range(4):
                P.op("act", lambda e, h=h: e.activation(e1[:], rt[:, 0:896], AF.Exp, scale=lg[:, h:h + 1]), reads=[Brt, Blg], writes=[Be1])
                P.op("dve", lambda e: e.tensor_tensor(e1[:], e1[:], rt[:, 1792:2688], ALU.mult), reads=[Be1, Brt], writes=[Be1])
                P.op("act", lambda e, h=h: e.activation(e2[:], rt[:, 896:1792], AF.Exp, scale=lg[:, 4 + h:5 + h]), reads=[Brt, Blg], writes=[Be2])
                P.op("dve", lambda e: e.tensor_tensor(e2[:], e2[:], rt[:, 2688:3584], ALU.mult), reads=[Be2, Brt], writes=[Be2])
                P.op("dve", lambda e: e.tensor_tensor(e1[:], e1[:], e2[:], ALU.add), reads=[Be1, Be2], writes=[Be1])
                P.op("dve", lambda e, h=h: e.tensor_scalar(Wt[:, h, :], e1[:], KS, None, ALU.mult), reads=[Be1], writes=[BWt])
            base = 3584
            for d in range(2):
                P.op("dve", lambda e, d=d: e.tensor_tensor(wtab[:, d, :, :], rt[:, base + 2 * d:base + 2 * d + 2].unsqueeze(2).to_broadcast([128, 2, 4]),
                                                         lg[:, 4 * d:4 * d + 4].unsqueeze(1).to_broadcast([128, 2, 4]), ALU.mult),
                     reads=[Brt, Blg], writes=[Bwtab])
                P.op("dve", lambda e, d=d: e.tensor_tensor(dq[:, d, :, :], rt[:, base + 4 + 4 * d:base + 8 + 4 * d].unsqueeze(2).to_broadcast([128, 4, 4]),
                                                         lg[:, 4 * d:4 * d + 4].unsqueeze(1).to_broadcast([128, 4, 4]), ALU.mult),
                     reads=[Brt, Blg], writes=[Bdq])
            wt2 = wtab[:].rearrange("p a b c -> p (a b c)")
            dq2 = dq[:].rearrange("p a b c -> p (a b c)")
            P.op("act", lambda e: e.activation(wt2, wt2, AF.Exp), reads=[Bwtab], writes=[Bwtab])
            P.op("dve", lambda e: e.tensor_scalar(wt2, wt2, KS, None, ALU.mult), reads=[Bwtab], writes=[Bwtab])
            P.op("act", lambda e: e.activation(dq2, dq2, AF.Exp), reads=[Bdq], writes=[Bdq])
            P.op("dve", lambda e: e.tensor_scalar(dq2, dq2, self.flag[:, 0:1], None, ALU.mult), reads=[Bdq, self.Bflag], writes=[Bdq])
            P.op("act", lambda e: e.activation(cf[:, 0, :], lg[:], AF.Exp, scale=512.0), reads=[Blg], writes=[Bcf])
            P.op("act", lambda e: e.activation(cf[:, 1, :], lg[:], AF.Exp, scale=256.0), reads=[Blg], writes=[Bcf])
            P.op("dve", lambda e: e.memset(gr[:], 0.0), writes=[Bgr])
            P.dma("sp", lambda e: e.dma_start(out=gr[0:16, :], in_=gn_d[j].rearrange("(k p) -> k p", p=128)), writes=[Bgr])
            bank, Bb = self.next_bank("g")
            P.op("pe", lambda e: e.transpose(bank[:, 0:64], gr[:], self.identf[0:64, 0:64]), reads=[Bgr, self.Bidentf], writes=[Bb])
            self.copy("dve", gnwT[:], bank[:, 0:16], [Bb], [BgnwT])
            P.barrier()
            P.emit_block()

        QK, BQK = self.sb(st, "QK", [128, 16, 512], BF16)
        KD, BKD = self.sb(st, "KD", [128, 8 * 1024], BF16)
        v, Bv = self.sbs(st, "rv", 4, [128, 2048], BF16)
        o2, Bo2 = self.sbs(st, "ro", 2, [128, 2048])
        SF0, BSF0 = self.sb(st, "SF0", [128, 8, 512], BF16)
        SF1, BSF1 = self.sb(st, "SF1", [128, 8, 512], BF16)
        SBb, BSBb = self.sb(st, "SBb", [128, 8, 512], BF16)
        RC, BRC = self.sb(st, "RC", [128, 4, 256])
        RS, BRS = self.sb(st, "RS", [128, 4, 256])
        yb, Byb = RC[:].rearrange("p a c -> p (a c)").bitcast(BF16), BRC
        qr, Bqr = self.sbs(st, "qr", 2, [128, 512])
        t2, Bt2 = self.sb(st, "t2", [128, 512])
        qrb, Bqrb = self.sbs(st, "qrb", 2, [128, 512], BF16)
        stg, Bstg = qr, Bqr
        sxs, Bsxs = t2, Bt2
        nstat, Bnstat = self.sbs(st, "nm_stat", 2, [128, 2])
        self._nm[id(st)] = ([t2[:].bitcast(BF16), qr[0][:].bitcast(BF16)], [Bt2, Bqr[0]], nstat, Bnstat)
        rs2 = RS[:].rearrange("p a c -> p (a c)")
        self._optmp[id(st)] = ([rs2[:, 0:512], rs2[:, 512:1024]], [BRS, BRS])
        P.dma("pool", lambda e: e.dma_start(out=SF0[:], in_=stin_d[0].rearrange("h (c p) v -> p (h c) v", p=128)), writes=[BSF0])

        qc = [0]
        sc = [0]

        def run_group(g, mode):
            tiles = [4 * g + a for a in range(4)]
            rope = g < 2
            self.norm_mod_T(st, tiles, "ret")
            if rope and mode == "full" or (rope and mode == "pre"):
                P.dma("sp", lambda e: e.dma_start(out=RC[:], in_=ropec_d[g * 512:(g + 1) * 512, :].rearrange("(a p) c -> p a c", p=128)), writes=[BRC])
                P.dma("sp", lambda e: e.dma_start(out=RS[:], in_=ropes_d[g * 512:(g + 1) * 512, :].rearrange("(a p) c -> p a c", p=128)), writes=[BRS])

            def rope_to(dst, Bdst, bank, Bb, a):
                if not rope:
                    self.copy("act", dst, bank[:, :], [Bb], [Bdst])
                    return
                d3 = dst.rearrange("p (h c) -> p h c", h=2)
                x3 = bank[:, :].rearrange("p (h c) -> p h c", h=2)
                P.op("dve", lambda e: e.tensor_tensor(d3, x3, RC[:, a, :].unsqueeze(1).to_broadcast([128, 2, 256]), ALU.mult),
                     reads=[Bb, BRC], writes=[Bdst])
                x5 = bank[:, :].rearrange("p (h b f d) -> p h b f d", h=2, b=2, f=2, d=64)
                t5 = t2[:].rearrange("p (h b f d) -> p h b f d", h=2, b=2, f=2, d=64)
                S4 = RS[:, a, :].rearrange("p (b f d) -> p b f d", b=2, f=2, d=64)
                for hf in range(2):
                    P.op("dve", lambda e, hf=hf: e.tensor_tensor(t5[:, :, :, hf, :], x5[:, :, :, 1 - hf, :],
                                                               S4[:, :, hf, :].unsqueeze(1).to_broadcast([128, 2, 2, 64]), ALU.mult),
                         reads=[Bb, BRS], writes=[Bt2])
                P.op("dve", lambda e: e.tensor_tensor(dst, dst, t2[:], ALU.add), reads=[Bt2, Bdst], writes=[Bdst])

            def qk_consumer(chunk0, is_k):
                def consume(a, n0, bank, Bb):
                    k = qc[0] % 2
                    qc[0] += 1
                    rope_to(qr[k][:], Bqr[k], bank, Bb, a)
                    P.op("act", lambda e: e.activation(qrb[k][:], qr[k][:], AF.Identity), reads=[Bqr[k]], writes=[Bqrb[k]])
                    if is_k:
                        for hh in range(2):
                            h = n0 // 256 + hh
                            for d in range(2):
                                P.op("dve", lambda e, hh=hh, h=h, d=d: e.tensor_scalar(kd(a, d)[:, n0 + hh * 256:n0 + (hh + 1) * 256], qr[k][:, hh * 256:(hh + 1) * 256],
                                                                                     wtab[:, d, a % 2, h:h + 1], None, ALU.mult),
                                     reads=[Bqr[k], Bwtab], writes=[BKD])
                    def later():
                        bank2, Bb2 = self.next_bank("g")
                        bv2 = bank2[:].bitcast(BF16)
                        for q in range(4):
                            P.op("pe", lambda e, q=q: e.transpose(bv2[:, q * 128:(q + 1) * 128], qrb[k][:, q * 128:(q + 1) * 128], ident[:]),
                                 reads=[Bqrb[k], self.Bident], writes=[Bb2])
                        c0 = chunk0 + n0 // 128
                        self.copy("act", QK[:, c0:c0 + 4, a * 128:(a + 1) * 128], bv2[:, 0:512].rearrange("p (q c) -> p q c", c=128), [Bb2], [BQK])
                    return later
                return consume

            def v_consume(a, n0, bank, Bb):
                self.copy("act", v[a][:, n0:n0 + 512], bank[:, :], [Bb], [Bv[a]])

            jobs = []
            pend = self.Pending()
            if mode == "full":
                jobs.append(self.proj_tok_job(win_d[j], 0, D, 4, qk_consumer(0, False), pending=pend))
            jobs.append(self.proj_tok_job(win_d[j], D, D, 4, qk_consumer(8, True), pending=pend))
            jobs.append(self.proj_tok_job(win_d[j], 2 * D, 2 * D, 4, v_consume, pending=pend))
            self.run_jobs(jobs, pending=pend)

            for d in ((0, 1) if mode == "full" else (1,)):
                for h in range(4):
                    for dc in range(2):
                        bk = []
                        for lb in range(2):
                            bank, Bb = self.next_bank("g")
                            for p2 in range(2):
                                a = 2 * lb + p2
                                P.op("pe", lambda e, a=a, p2=p2: e.matmul(bank[:, :], kd(a, d)[:, h * 256 + dc * 128:h * 256 + (dc + 1) * 128], v[a][:, h * 512:(h + 1) * 512],
                                                                        start=(p2 == 0), stop=(p2 == 1)),
                                     reads=[BKD, Bv[a]], writes=[Bb])
                            bk.append((bank, Bb))
                        if mode == "full":
                            for lb in range(2):
                                k = sc[0] % 2
                                sc[0] += 1
                                self.copy("act" if lb == 0 else "dve", stg[k][:], bk[lb][0][:, :], [bk[lb][1]], [Bstg[k]])
                                P.dma("sp", lambda e, lb=lb, k=k: e.dma_start(out=st_d[2 * g + lb, d, h, dc * 128:(dc + 1) * 128, :], in_=stg[k][:]),
                                      reads=[Bstg[k]], sem_buf=Bstg[k])
                        want = (mode == "full" and g == 0 and d == 0) or (mode == "pre" and d == 1)
                        if want:
                            SX, BSX = (SF1, BSF1) if d == 0 else (SBb, BSBb)
                            far, near = (bk[0], bk[1]) if d == 0 else (bk[1], bk[0])
                            P.dma("sp", lambda e: e.dma_start(out=sxs[:], in_=stin_d[d, h, dc * 128:(dc + 1) * 128, :]), writes=[Bsxs])
                            P.op("dve", lambda e: e.tensor_scalar(sxs[:], sxs[:], cf[:, 0, 4 * d + h:4 * d + h + 1], None, ALU.mult), reads=[Bsxs, Bcf], writes=[Bsxs])
                            P.op("dve", lambda e: e.scalar_tensor_tensor(sxs[:], far[0][:, :], cf[:, 1, 4 * d + h:4 * d + h + 1], sxs[:], ALU.mult, ALU.add),
                                 reads=[far[1], Bcf, Bsxs], writes=[Bsxs])
                            P.op("dve", lambda e: e.tensor_tensor(SX[:, 2 * h + dc, :], near[0][:, :], sxs[:], ALU.add), reads=[near[1], Bsxs], writes=[BSX])
            if mode == "pre":
                return

            def g_consume(fc, bank, Bb):
                k = qc[0] % 2
                qc[0] += 1
                P.op("act", lambda e: e.activation(qr[k][:], bank[:, :], AF.Silu), reads=[Bb], writes=[Bqr[k]])
                P.op("dve", lambda e: e.scalar_tensor_tensor(QK[:, fc, :], QK[:, fc, :], gnwT[:, fc:fc + 1], qr[k][:], ALU.mult, ALU.mult),
                     reads=[BQK, BgnwT, Bqr[k]], writes=[BQK])
            gjob = self.proj_feat_job(win_d[j], 4 * D, 2 * D, 0, 512, g_consume)
            gpre = self.preload(gjob[0], "k8")

            for h in range(4):
                for jt in range(4):
                    bank, Bb = self.next_bank("g")
                    for dc in range(2):
                        P.op("pe", lambda e, dc=dc: e.matmul(bank[:, :], QK[:, 8 + 2 * h + dc, jt * 128:(jt + 1) * 128], QK[:, 2 * h + dc, :], start=(dc == 0), stop=(dc == 1)),
                             reads=[BQK], writes=[Bb])
                    w0 = 384 - 128 * jt
                    for ib in range(2):
                        c0 = ib * 256
                        Wsl = Wt[:, h, w0 + c0:w0 + c0 + 256]
                        if jt // 2 == ib:
                            P.op("dve", lambda e: e.tensor_tensor(PT(h, jt)[:, c0:c0 + 256], bank[:, c0:c0 + 256], Wsl, ALU.mult), reads=[Bb, BWt], writes=[BKD])
                        elif g < 2:
                            P.op("dve", lambda e: e.scalar_tensor_tensor(PT(h, jt)[:, c0:c0 + 256], bank[:, c0:c0 + 256], self.flag[:, 0:1], Wsl, ALU.mult, ALU.mult),
                                 reads=[Bb, BWt, self.Bflag], writes=[BKD])

            SFs, BSFs = (SF0, BSF0) if g == 0 else (SF1, BSF1)
            y3 = yb[:].rearrange("p (h c) -> p h c", h=4)

            def pv_phase(it):
                o, Bo = o2[it % 2], Bo2[it % 2]
                for h in range(4):
                    hc = slice(h * 512, (h + 1) * 512)
                    ob, Bob = self.next_bank("a")
                    jts = list(range(4)) if g < 2 else [jt for jt in range(4) if jt // 2 == it // 2]
                    for n, jt in enumerate(jts):
                        P.op("pe", lambda e, n=n, jt=jt: e.matmul(ob[:, :], PT(h, jt)[:, it * 128:(it + 1) * 128], v[jt][:, hc], start=(n == 0), stop=(n == len(jts) - 1)),
                             reads=[BKD, Bv[jt]], writes=[Bob])
                    self.copy("act", o[:, hc], ob[:, :], [Bob], [Bo])
                    if g < 2:
                        for d, S_, BS_ in ((0, SFs, BSFs), (1, SBb, BSBb)):
                            cb, Bcb = self.next_bank("b")
                            for dc in range(2):
                                P.op("pe", lambda e, dc=dc: e.matmul(cb[:, :], QK[:, 2 * h + dc, it * 128:(it + 1) * 128], S_[:, 2 * h + dc, :], start=(dc == 0), stop=(dc == 1)),
                                     reads=[BQK, BS_], writes=[Bcb])
                            P.op("dve", lambda e, d=d: e.scalar_tensor_tensor(o[:, hc], cb[:, :], dq[:, d, it, h:h + 1], o[:, hc], ALU.mult, ALU.add),
                                 reads=[Bcb, Bdq, Bo], writes=[Bo])

            def gn_phase(it):
                o, Bo = o2[it % 2], Bo2[it % 2]
                o3 = o[:].rearrange("p (h c) -> p h c", h=4)
                P.op("dve", lambda e: e.tensor_reduce(gst[:, 0:4], o3, AX.X, ALU.add), reads=[Bo], writes=[Bgst])
                for h in range(4):
                    P.op("act", lambda e, h=h: e.activation(yb[:, h * 512:(h + 1) * 512], o[:, h * 512:(h + 1) * 512], AF.Square, accum_out=gst[:, 4 + h:5 + h]),
                         reads=[Bo], writes=[Byb, Bgst2])
                P.op("dve", lambda e: e.tensor_scalar(gst[:, 0:4], gst[:, 0:4], 1.0 / 512, None, ALU.mult), reads=[Bgst], writes=[Bgst])
                P.op("dve", lambda e: e.tensor_tensor(gst[:, 8:12], gst[:, 0:4], gst[:, 0:4], ALU.mult), reads=[Bgst], writes=[Bgst])
                P.op("dve", lambda e: e.scalar_tensor_tensor(gst[:, 4:8], gst[:, 4:8], 1.0 / 512, gst[:, 8:12], ALU.mult, ALU.subtract), reads=[Bgst, Bgst2], writes=[Bgst, Bgst2])
                P.op("dve", lambda e: e.tensor_scalar(gst[:, 4:8], gst[:, 4:8], EPS, None, ALU.add), reads=[Bgst, Bgst2], writes=[Bgst, Bgst2])
                P.op("act", lambda e: e.activation(gst[:, 8:12], gst[:, 4:8], AF.Sqrt), reads=[Bgst, Bgst2], writes=[Bgst, Bgst2])
                P.op("dve", lambda e: e.reciprocal(gst[:, 8:12], gst[:, 8:12]), reads=[Bgst, Bgst2], writes=[Bgst, Bgst2])
                P.op("dve", lambda e: e.scalar_tensor_tensor(gst[:, 4:8], gst[:, 0:4], -1.0, gst[:, 8:12], ALU.mult, ALU.mult), reads=[Bgst, Bgst2], writes=[Bgst, Bgst2])
                for h in range(4):
                    P.op("act", lambda e, h=h: e.activation(yb[:, h * 512:(h + 1) * 512], o[:, h * 512:(h + 1) * 512], AF.Identity,
                                                            bias=gst[:, 4 + h:5 + h], scale=gst[:, 8 + h:9 + h]),
                         reads=[Bo, Bgst, Bgst2], writes=[Byb])

            def tr_phase(it):
                for half in range(2):
                    bank, Bb = self.next_bank("g")
                    bv = bank[:].bitcast(BF16)
                    for q in range(8):
                        kc = half * 8 + q
                        P.op("pe", lambda e, q=q, kc=kc: e.transpose(bv[:, q * 128:(q + 1) * 128], yb[:, kc * 128:(kc + 1) * 128], ident[:]),
                             reads=[Byb, self.Bident], writes=[Bb])
                    self.copy("act" if half == 0 else "dve", QK[:, half * 8:(half + 1) * 8, it * 128:(it + 1) * 128],
                              bv[:, :].rearrange("p (q c) -> p q c", c=128), [Bb], [BQK])

            pv_phase(0)
            for it in range(4):
                if it + 1 < 4:
                    pv_phase(it + 1)
                gn_phase(it)
                tr_phase(it)

            self.run_jobs([gjob, self.out_proj_job(st, wout_d[j], 16, QK[:], [BQK], tiles, "ret")], pre=gpre)

        run_group(1, "pre")
        run_group(0, "full")
        P.dma("pool", lambda e: e.dma_start(out=SBb[:], in_=stin_d[1].rearrange("h (c p) v -> p (h c) v", p=128)), writes=[BSBb])
        run_group(1, "full")
        run_group(2, "full")


def core_layout(c):
    if c < 4:
        return "real", [("s", c)] * 4 + [("p", 2 * c), ("p", 2 * c + 1)]
    base = 8 + 6 * (c - 4)
    return "pseudo", [("p", base + i) for i in range(6)]


def make_inputs(c, inp, consts):
    kind, blocks = core_layout(c)
    x = np.empty((NT * 128, D), np.float32)
    for m, (gk, b) in enumerate(blocks):
        if gk == "s":
            x[m * 256:(m + 1) * 256] = inp["x_sample"][b, m * 256:(m + 1) * 256]
        else:
            x[m * 256:(m + 1) * 256] = inp["x_prompt"][b]
    real = kind == "real"
    cond = np.stack([inp["c"][c] if real else inp["c_ctx"], inp["c_ctx"]]).astype(np.float32)
    z = np.zeros
    d = {
        "x": x, "cond": cond,
        "ctxk": inp["cache_na_k"][c].reshape(2, 256, D) if real else z((2, 256, D), np.float32),
        "ctxv": inp["cache_na_v"][c].reshape(2, 256, D) if real else z((2, 256, D), np.float32),
        "stin": inp["state_ret"][c, 0] if real else z((2, 4, 256, 512), np.float32),
        "flag": np.full((128, 1), 1.0 if real else 0.0, np.float32),
        "norm_w": inp["norm_w"], "w_ada": inp["w_ada"], "b_ada": inp["b_ada"],
        "na_w_in": inp["na_w_in"], "na_w_out": inp["na_w_out"], "na_q_gain": inp["na_q_gain"], "na_k_gain": inp["na_k_gain"],
        "nabias": consts["nabias_real"] if real else consts["nabias_zero"],
        "namask": consts["namask_real"] if real else consts["namask_pseudo"],
        "ret_w_in": inp["ret_w_in"], "ret_w_out": inp["ret_w_out"],
        "ret_decay_logit": inp["ret_decay_logit"].reshape(1, 8), "ret_gn_w": inp["ret_gn_w"],
        "rope_c": consts["rope_c"] if real else consts["rope_c1"],
        "rope_s": consts["rope_s"] if real else consts["rope_s0"],
        "rtab": consts["rtab"],
        "mlp_w_in": inp["mlp_w_in"], "mlp_ln_w": inp["mlp_ln_w"], "mlp_ln_b": inp["mlp_ln_b"],
        "mlp_w_s": inp["mlp_w_s"], "mlp_b_s": inp["mlp_b_s"], "mlp_w_out": inp["mlp_w_out"],
    }
    return {k: np.ascontiguousarray(v) for k, v in d.items()}


def make_consts(inp):
    consts = {}
    rpb = inp["na_rpb"]
    kc = np.arange(64)[:, None]
    qc = np.arange(64)[None, :]
    cidx = np.clip(kc - qc, -15, 15) + 15
    M0 = 11
    tab = np.zeros((2, 16, 128, 23, 64), np.float32)
    for m in range(23):
        for half in range(2):
            dr = M0 - m + half
            if -7 <= dr <= 7:
                tab[:, :, half * 64:(half + 1) * 64, m, :] = rpb[:, :, dr + 7][:, :, cidx]
    consts["nabias_real"] = tab.reshape(2, 16, 128, 23 * 64)
    consts["nabias_zero"] = np.zeros_like(consts["nabias_real"])
    R = 16
    r = np.arange(R)
    rstart = np.clip(r - 4, 0, R - 8)
    cq = np.arange(64)
    cstart = np.clip(cq - 8, 0, 48)
    ck = np.arange(64)
    col_in = (ck[None, :] >= cstart[:, None]) & (ck[None, :] < cstart[:, None] + 16)

    def build_mask(real):
        m = np.zeros((128, 17, 512), np.float32)
        for gi in range(2):
            for ti in range(6):
                t = ti + 2 * gi
                for e in range(2):
                    krow = 2 * t + e
                    for i in range(8):
                        qrow = 8 * gi + i
                        if real:
                            ok = rstart[qrow] <= krow < rstart[qrow] + 8
                            blk = col_in.T.astype(np.float32) if ok else 0.0
                        else:
                            blk = 1.0 if (krow // 4 == qrow // 4) else 0.0
                        m[e * 64:(e + 1) * 64, gi * 6 + ti, i * 64:(i + 1) * 64] = blk
        m[:, 16, :] = 1.0 if real else 0.0
        return m.astype(ml_dtypes.bfloat16)
    consts["namask_real"] = build_mask(True)
    consts["namask_pseudo"] = build_mask(False)
    t = np.arange(1024)
    row = (t // 64).astype(np.float32)
    col = (t % 64).astype(np.float32)
    inv = (10000.0 ** (-np.arange(0, 128, 2, dtype=np.float32) / 128)).astype(np.float32)
    ar = row[:, None] * inv[None, :]
    ac = col[:, None] * inv[None, :]
    consts["rope_c"] = np.concatenate([np.cos(ar), np.cos(ar), np.cos(ac), np.cos(ac)], 1).astype(np.float32)
    consts["rope_s"] = np.concatenate([-np.sin(ar), np.sin(ar), -np.sin(ac), np.sin(ac)], 1).astype(np.float32)
    consts["rope_c1"] = np.ones((1024, 256), np.float32)
    consts["rope_s0"] = np.zeros((1024, 256), np.float32)
    jj = np.arange(128)[:, None]
    cc = np.arange(896)[None, :]
    diff = (cc - 384) - jj
    rt = np.zeros((128, 4 * 896 + 16), np.float32)
    rt[:, 0:896] = np.maximum(diff, 0)
    rt[:, 896:1792] = np.maximum(-diff, 0)
    rt[:, 1792:2688] = (diff >= 0)
    rt[:, 2688:3584] = (diff <= 0)
    base = 3584
    for p2 in range(2):
        rt[:, base + p2] = 255 - (128 * p2 + np.arange(128))
        rt[:, base + 2 + p2] = 128 * p2 + np.arange(128)
    for it in range(4):
        rt[:, base + 4 + it] = 128 * it + np.arange(128) + 1
        rt[:, base + 8 + it] = 512 - 128 * it - np.arange(128)
    consts["rtab"] = rt
    return consts


_CACHE = {}


def kernel(**inp):
    inp = {k: np.asarray(v) for k, v in inp.items()}
    layers = _CACHE.get("layers", (0, 1, 2, 3))
    if "nc" not in _CACHE:
        _CACHE["nc"] = K(layers).build()
    nc = _CACHE["nc"]
    consts = make_consts(inp)
    in_maps = [make_inputs(c, inp, consts) for c in range(8)]
    sel = _CACHE.get("core_sel")
    if sel is None:
        res = run_bass_kernel_spmd(nc, in_maps, core_ids=list(range(8)))
        R = res.results
    else:
        res = run_bass_kernel_spmd(nc, [in_maps[c] for c in sel], core_ids=list(range(len(sel))))
        R = [res.results[sel.index(c)] if c in sel else res.results[0] for c in range(8)]
    y_p = np.empty((32, 256, D), np.float32)
    y_s = np.empty((4, 1024, D), np.float32)
    nk = np.empty((32, 2, 256, 16, 64), np.float32)
    nv = np.empty((32, 2, 256, 16, 64), np.float32)
    nr = np.empty((32, 1, 2, 4, 256, 512), np.float32)
    for c in range(8):
        kind, blocks = core_layout(c)
        r = R[c]
        for m, (gk, b) in enumerate(blocks):
            sl = slice(m * 256, (m + 1) * 256)
            if gk == "s":
                y_s[b, sl] = r["y"][sl]
            else:
                y_p[b] = r["y"][sl]
                nk[b] = r["nk"][:, sl].reshape(2, 256, 16, 64)
                nv[b] = r["nv"][:, sl].reshape(2, 256, 16, 64)
                nr[b, 0] = r["st"][m]
    return (y_p, y_s, nk, nv, nr)
```
